# Optimizing a Trainium2 kernel written in Bass

```python
import jax
import jax.numpy as jnp
from jax import lax
import numpy as np

D_MODEL = 1024
BATCH = 8
SEQ = 8192
DEPTH = 2

GRID_W = 64
CTX_LEN = 256
N_MIXERS = 2
D_FF = 2816
CONV_WIDTH = 3
HG_HEADS = 8
HG_DK = D_MODEL // HG_HEADS
HG_CHUNK = 64
N_CONV_LAYERS = (DEPTH + 1) // 2
N_HGRN_LAYERS = DEPTH // 2
N_MOD = 9
EPS = 1e-6

kernel_name = 'hybrid_shortconv_hgrn2_macaron_flow'


def rms_norm(x, g):
    xf = x.astype(jnp.float32)
    y = xf * lax.rsqrt(jnp.mean(xf * xf, axis=-1, keepdims=True) + EPS)
    return (y * g.astype(jnp.float32)).astype(x.dtype)


def ada_mods(cond, w, b):
    m = jax.nn.silu(cond) @ w + b
    return jnp.split(m[..., None, :], N_MOD, axis=-1)


def modulate(x, shift, scale):
    return x * (1.0 + scale) + shift


def swiglu(x, w_gu, w_down):
    gate, up = jnp.split(x @ w_gu, 2, axis=-1)
    return (jax.nn.silu(gate) * up) @ w_down


def ffn_half(h, shift, scale, gate, g, w_gu, w_down):
    return h + 0.5 * gate * swiglu(modulate(rms_norm(h, g), shift, scale), w_gu, w_down)


def conv3(z, w, axis):
    n = z.shape[axis]
    pad = [(0, 0)] * z.ndim
    pad[axis] = (1, 1)
    zp = jnp.pad(z, pad)
    return sum(lax.slice_in_dim(zp, j, j + n, axis=axis) * w[j] for j in range(CONV_WIDTH))


def grid_conv(z, w, rows):
    b, s, d = z.shape
    half = d // 2
    zg = z.reshape(b, rows, GRID_W, d)
    yh = conv3(zg[..., :half], w[:, :half], axis=2)
    yv = conv3(zg[..., half:], w[:, half:], axis=1)
    return jnp.concatenate([yh, yv], axis=-1).reshape(b, s, d)


def shortconv_mixer(a, w_in, conv_w, w_out, rows):
    bg, cg, v = jnp.split(a @ w_in, 3, axis=-1)
    z = cg * v
    zc = conv3(z, conv_w, axis=1) if rows is None else grid_conv(z, conv_w, rows)
    return (bg * zc) @ w_out


def hgrn_lower_bounds(lb_logits):
    p = jax.nn.softmax(lb_logits.astype(jnp.float32), axis=0)
    return jnp.cumsum(p, axis=0) - p[0]


def to_heads(t):
    return t.astype(jnp.float32).reshape(t.shape[0], t.shape[1], HG_HEADS, HG_DK)


def hgrn_gates(f_logit, lb):
    f = lb + (1.0 - lb) * jax.nn.sigmoid(f_logit.astype(jnp.float32))
    return to_heads(1.0 - f), to_heads(jnp.log(f))


def chunk_gla_scan(q, k, v, logf):
    b, l, h, dk = q.shape
    dv = v.shape[-1]
    n = l // HG_CHUNK

    def to_chunks(t):
        return jnp.moveaxis(t.reshape(b, n, HG_CHUNK, h, t.shape[-1]), 1, 0)

    mask = jnp.tril(jnp.ones((HG_CHUNK, HG_CHUNK), dtype=bool))

    def step(s, inp):
        qc, kc, vc, gc = inp
        g_cum = jnp.cumsum(gc, axis=1)
        q_t = qc * jnp.exp(g_cum)
        k_t = kc * jnp.exp(-g_cum)
        att = jnp.where(mask, jnp.einsum('bthd,bshd->bhts', q_t, k_t), 0.0)
        o = jnp.einsum('bhts,bshe->bthe', att, vc) + jnp.einsum('bthd,bhde->bthe', q_t, s)
        g_last = g_cum[:, -1]
        k_d = kc * jnp.exp(g_last[:, None] - g_cum)
        s = jnp.exp(g_last)[..., None] * s + jnp.einsum('bshd,bshe->bhde', k_d, vc)
        return s, o

    s0 = jnp.zeros((b, h, dk, dv), jnp.float32)
    _, o = lax.scan(step, s0, (to_chunks(q), to_chunks(k), to_chunks(v), to_chunks(logf)))
    return jnp.moveaxis(o, 0, 1).reshape(b, l, h, dv)


def prefix_reverse(t, lc):
    return jnp.concatenate([jnp.flip(t[:, :lc], axis=1), jnp.flip(t[:, lc:], axis=1)], axis=1)


def hgrn2_mixer(a_ctx, a_lat, w_in, lb_fwd, lb_bwd, gnorm_g, w_out, need_ctx):
    lc = a_ctx.shape[1]
    a = jnp.concatenate([a_ctx, a_lat], axis=1)
    q, i, f_fw, f_bw, og = jnp.split(a @ w_in, 5, axis=-1)
    q = to_heads(jax.nn.silu(q))
    i = to_heads(i)
    k_fw, lf_fw = hgrn_gates(f_fw, lb_fwd)
    k_bw, lf_bw = hgrn_gates(f_bw, lb_bwd)
    o_fw = chunk_gla_scan(q, k_fw, i, lf_fw)
    o_bw = prefix_reverse(chunk_gla_scan(prefix_reverse(q, lc), prefix_reverse(k_bw, lc),
                                         prefix_reverse(i, lc), prefix_reverse(lf_bw, lc)), lc)
    o = o_fw + o_bw
    if not need_ctx:
        o, og = o[:, lc:], og[:, lc:]
    o = o * lax.rsqrt(jnp.mean(o * o, axis=-1, keepdims=True) + EPS)
    o = o * gnorm_g.astype(jnp.float32).reshape(HG_HEADS, HG_DK)
    o = o.reshape(o.shape[0], o.shape[1], D_MODEL).astype(a.dtype)
    y = (o * jax.nn.silu(og)) @ w_out
    if need_ctx:
        return y[:, :lc], y[:, lc:]
    return None, y


def setup_inputs(seed: int = 0) -> dict:
    key = jax.random.key(seed)
    ks = jax.random.split(key, 17)
    d, f = D_MODEL, D_FF
    na, nb = N_CONV_LAYERS, N_HGRN_LAYERS

    def nrm(k, shape, scale=1.0):
        return scale * jax.random.normal(k, shape, jnp.float32)

    return {
        'x': nrm(ks[0], (BATCH, SEQ, d)),
        'c': nrm(ks[1], (BATCH, d)),
        'ctx': nrm(ks[2], (BATCH, CTX_LEN, d)),
        'c_ctx': nrm(ks[3], (d,)),
        'ada_w': nrm(ks[4], (DEPTH, d, N_MOD * d), 0.5 * d ** -0.5),
        'ada_b': nrm(ks[5], (DEPTH, N_MOD * d), 0.02),
        'norm_g': 1.0 + nrm(ks[6], (DEPTH, 3, d), 0.1),
        'ffn_w_gu': nrm(ks[7], (DEPTH, 2, d, 2 * f), d ** -0.5),
        'ffn_w_down': nrm(ks[8], (DEPTH, 2, f, d), f ** -0.5),
        'conv_w_in': nrm(ks[9], (na, d, 3 * d), d ** -0.5),
        'conv_w': nrm(ks[10], (na, CONV_WIDTH, d), CONV_WIDTH ** -0.5),
        'conv_w_out': nrm(ks[11], (na, d, d), d ** -0.5),
        'hg_w_in': nrm(ks[12], (nb, d, 5 * d), d ** -0.5),
        'hg_lb_logits': nrm(ks[13], (DEPTH, 2, d), 0.1),
        'hg_gnorm_g': 1.0 + nrm(ks[14], (nb, d), 0.1),
        'hg_w_out': nrm(ks[15], (nb, d, d), d ** -0.5),
        'final_norm_g': 1.0 + nrm(ks[16], (d,), 0.1),
    }


def reference(x, c, ctx, c_ctx, ada_w, ada_b, norm_g, ffn_w_gu, ffn_w_down, conv_w_in, conv_w,
              conv_w_out, hg_w_in, hg_lb_logits, hg_gnorm_g, hg_w_out, final_norm_g):
    rows = x.shape[1] // GRID_W
    lbs = hgrn_lower_bounds(hg_lb_logits)
    h, hc = x, ctx
    for layer in range(DEPTH):
        kind = layer % N_MIXERS
        j = layer // N_MIXERS
        last = layer == DEPTH - 1
        ctx_to_mixer = (kind == 1) or (not last)
        m = ada_mods(c, ada_w[layer], ada_b[layer])
        mc = ada_mods(c_ctx, ada_w[layer], ada_b[layer])
        g1, g2, g3 = norm_g[layer, 0], norm_g[layer, 1], norm_g[layer, 2]
        w_gu, w_dn = ffn_w_gu[layer], ffn_w_down[layer]
        h = ffn_half(h, m[0], m[1], m[2], g1, w_gu[0], w_dn[0])
        if ctx_to_mixer:
            hc = ffn_half(hc, mc[0], mc[1], mc[2], g1, w_gu[0], w_dn[0])
        a = modulate(rms_norm(h, g2), m[3], m[4])
        y_ctx = None
        if kind == 0:
            y = shortconv_mixer(a, conv_w_in[j], conv_w[j], conv_w_out[j], rows)
            if not last:
                ac = modulate(rms_norm(hc, g2), mc[3], mc[4])
                y_ctx = shortconv_mixer(ac, conv_w_in[j], conv_w[j], conv_w_out[j], None)
        else:
            ac = modulate(rms_norm(hc, g2), mc[3], mc[4])
            y_ctx, y = hgrn2_mixer(ac, a, hg_w_in[j], lbs[layer, 0], lbs[layer, 1],
                                   hg_gnorm_g[j], hg_w_out[j], not last)
        h = h + m[5] * y
        h = ffn_half(h, m[6], m[7], m[8], g3, w_gu[1], w_dn[1])
        if not last:
            hc = hc + mc[5] * y_ctx
            hc = ffn_half(hc, mc[6], mc[7], mc[8], g3, w_gu[1], w_dn[1])
    return rms_norm(h, final_norm_g)
```

```python
import types
import numpy as np
import concourse.bass as bass
import concourse.mybir as mybir
from concourse.bass_utils import run_bass_kernel_spmd

F32 = mybir.dt.float32
BF16 = mybir.dt.bfloat16
AF = mybir.ActivationFunctionType
ALU = mybir.AluOpType

D = 1024
FC = 8
FF = 2816
HC = 22
S = 8192
LC = 256
UU = S + LC
EPS = 1e-6
NSP = 16 + 48 + 24 + 32 + 8 + 8 + 144
ENGS = ("pe", "act", "dve", "pool", "sp")
DBG = {}
N_ACT_CONV = 38


def _freeze(fn):
    if fn.__closure__ is None:
        return fn
    cells = []
    for c in fn.__closure__:
        try:
            cells.append(types.CellType(c.cell_contents))
        except ValueError:
            cells.append(c)
    return types.FunctionType(fn.__code__, fn.__globals__, fn.__name__, fn.__defaults__, tuple(cells))


class Buf:
    __slots__ = ("name", "last_w", "readers")

    def __init__(self, name=""):
        self.name = name
        self.last_w = None
        self.readers = []


class Op:
    __slots__ = ("eng", "fn", "idx", "deps", "signal", "dma_key", "dma_cnt")


class Prog:
    def __init__(self, nc, n_auto=24):
        self.nc = nc
        self.ops = {e: [] for e in ENGS}
        self.order = []
        self.dma_keys = {}
        self.dma_last = {}
        self.pending = {e: [] for e in ENGS}
        self.n_auto = n_auto
        self.auto_i = {e: 0 for e in ENGS}

    def op(self, eng, fn, reads=(), writes=(), dma=False, dma_key=None):
        o = Op()
        o.eng = eng
        o.fn = _freeze(fn)
        o.idx = len(self.ops[eng])
        o.signal = False
        if dma and dma_key is None:
            dma_key = "auto_%s_%d" % (eng, self.auto_i[eng] % self.n_auto)
            self.auto_i[eng] += 1
        o.dma_key = dma_key
        o.dma_cnt = 0
        deps = list(self.pending[eng])
        self.pending[eng] = []
        for b in reads:
            if b.last_w is not None:
                deps.append(b.last_w)
        for b in writes:
            if b.last_w is not None:
                deps.append(b.last_w)
            deps.extend(b.readers)
        if dma_key is not None:
            prev = self.dma_last.get(dma_key)
            if prev is not None:
                deps.append(prev)
            self.dma_last[dma_key] = o
            c = self.dma_keys.get(dma_key, 0) + 1
            self.dma_keys[dma_key] = c
            o.dma_cnt = c
        red = {}
        for d in deps:
            if d is o:
                continue
            if eng == "pe" and d.eng == "pe" and d.dma_key is None:
                continue
            if d.dma_key is not None:
                k = ("dma", d.dma_key)
                v = d.dma_cnt
            else:
                k = ("eng", d.eng)
                v = d.idx
            if k not in red or red[k][0] < v:
                red[k] = (v, d)
        o.deps = red
        for b in writes:
            b.last_w = o
            b.readers = []
        for b in reads:
            if b not in writes:
                b.readers.append(o)
                if len(b.readers) > 48:
                    keep = {}
                    for r in b.readers:
                        k = ("dma", r.dma_key) if r.dma_key is not None else ("eng", r.eng)
                        keep[k] = r
                    b.readers = list(keep.values())
        self.ops[eng].append(o)
        self.order.append(o)
        return o

    def barrier(self, skip_pool=False, extra=()):
        lasts = list(extra)
        for e in ENGS:
            if skip_pool and e == "pool":
                continue
            for o in reversed(self.ops[e]):
                if o.dma_key is None:
                    lasts.append(o)
                    break
        for k, o in self.dma_last.items():
            if skip_pool and k.startswith("cv"):
                continue
            lasts.append(o)
        for e in ENGS:
            if skip_pool and e == "pool":
                continue
            self.pending[e].extend(lasts)

    def emit(self, final_wait_keys=()):
        nc = self.nc
        for o in self.order:
            for (k, (v, d)) in o.deps.items():
                if k[0] == "eng":
                    d.signal = True
        esem = {e: nc.alloc_semaphore(name="sem_%s" % e) for e in ENGS}
        dsem = {k: nc.alloc_semaphore(name="dsem_%d" % i) for i, k in enumerate(self.dma_keys)}
        cnts = {}
        for e in ENGS:
            c = 0
            arr = []
            for o in self.ops[e]:
                if o.signal and o.dma_key is None:
                    c += 1
                arr.append(c)
            cnts[e] = arr

        def run_engine(e, engobj):
            waited = {}
            for o in self.ops[e]:
                for (k, (v, d)) in o.deps.items():
                    if k[0] == "eng":
                        sem = esem[k[1]]
                        val = cnts[k[1]][v]
                    else:
                        sem = dsem[k[1]]
                        val = 16 * v
                    if waited.get(k, 0) >= val:
                        continue
                    engobj.wait_ge(sem, val)
                    waited[k] = val
                ins = o.fn(engobj)
                if o.dma_key is not None:
                    ins.then_inc(dsem[o.dma_key], 16)
                elif o.signal:
                    ins.then_inc(esem[e], 1)
            if e == "sp":
                for k in final_wait_keys:
                    engobj.wait_ge(dsem[k], 16 * self.dma_keys[k])

        with nc.Block() as block:
            @block.tensor
            def _(eng):
                run_engine("pe", eng)

            @block.scalar
            def _(eng):
                run_engine("act", eng)

            @block.vector
            def _(eng):
                run_engine("dve", eng)

            @block.gpsimd
            def _(eng):
                run_engine("pool", eng)

            @block.sync
            def _(eng):
                run_engine("sp", eng)


class Ring:
    def __init__(self, items):
        self.items = items
        self.i = 0

    def next(self):
        it = self.items[self.i % len(self.items)]
        self.i += 1
        return it


def build_program(debug=(), stages=(0, 1, 2, 3, 4)):
    nc = bass.Bass("TRN2", target_bir_lowering=False)
    P = Prog(nc)

    def din(name, shape):
        return nc.dram_tensor(name, shape, F32, kind="ExternalInput").ap()

    x_d = din("x", [S, D])
    ctx_d = din("ctx", [LC, D])
    sp_d = din("smallp", [128, NSP])
    adaw_d = din("ada_w", [2, D, 9 * D])
    wgu_d = din("ffn_w_gu", [2, 2, D, 2 * FF])
    wdn_d = din("ffn_w_down", [2, 2, FF, D])
    cwin_d = din("conv_w_in", [D, 3 * D])
    cwout_d = din("conv_w_out", [D, D])
    hwin_d = din("hg_w_in", [D, 5 * D])
    hwout_d = din("hg_w_out", [D, D])
    out_d = nc.dram_tensor("out", [S, D], F32, kind="ExternalOutput").ap()

    def scr(name, shape, dt=F32):
        kind = "ExternalOutput" if name in debug else "Internal"
        return nc.dram_tensor(name, shape, dt, kind=kind).ap()

    WGU = [[scr("WGU%d%d" % (l, h), [HC, 128, 8, 256], BF16) for h in range(2)] for l in range(2)]
    WDN = [[scr("WDN%d%d" % (l, h), [8, 128, HC, 128], BF16) for h in range(2)] for l in range(2)]
    CWIN = scr("CWIN", [8, 128, 8, 384], BF16)
    CWOUT = scr("CWOUT", [8, 128, 8, 128], BF16)
    HWIN = scr("HWIN", [10, 128, 8, 512], BF16)
    HWOUT = scr("HWOUT", [8, 128, 8, 128], BF16)
    hA = scr("hA", [8, 128, UU])
    zA = scr("zA", [8, 128, UU])
    bgA = scr("bgA", [8, 128, UU])
    hB = scr("hB", [8, 128, UU])
    QK = scr("QK", [8, 2, 128, 2, UU], BF16)
    KVT = scr("KVT", [8, 2, UU, 2, 128], BF16)
    DEC = scr("DEC", [8, 2, 128, 132])
    SOG = scr("SOG", [8, 128, S])
    OFW = scr("OFW", [8, 128, S])
    OBW = scr("OBW", [8, 128, S])
    dbufs = {}

    def dbuf(key):
        if key not in dbufs:
            dbufs[key] = Buf(str(key))
        return dbufs[key]

    AR_BYTES = 204 * 1024
    arena = nc.alloc_sbuf_tensor("arena", [128, AR_BYTES // 2], BF16).ap()

    def view(off, nelem, dt):
        assert off % 32 == 0
        if dt == F32:
            assert off + 4 * nelem <= AR_BYTES, (off, nelem)
            return arena[:, off // 2: off // 2 + 2 * nelem].bitcast(F32)
        assert off + 2 * nelem <= AR_BYTES, (off, nelem)
        return arena[:, off // 2: off // 2 + nelem]

    KB = 1024
    o_const = 0
    o_h = 8 * KB
    o_xn = o_h + 32 * KB
    o_hid = o_xn + 16 * KB
    o_wr = o_hid + 44 * KB
    NSLOT = 3
    o_sq = o_wr + NSLOT * 8 * KB
    o_tmp = o_sq + 8 * KB
    o_rs = o_tmp + 8 * KB
    o_stg = o_rs + 4 * KB
    o_cv = o_stg + 24 * KB

    c_off = [o_const]

    def calloc(nelem, dt):
        sz = nelem * (4 if dt == F32 else 2)
        sz = (sz + 31) // 32 * 32
        v = view(c_off[0], nelem, dt)
        c_off[0] += sz
        assert c_off[0] <= o_h
        return v

    ones_bf = calloc(128, BF16)
    ident_f = calloc(128, F32)
    ident_bf = calloc(128, BF16)
    tri_f = calloc(128, F32)
    tri_b = calloc(128, F32)
    mask01 = calloc(512, F32)
    maskF = calloc(64, F32)
    maskB = calloc(64, F32)
    smallp = calloc(NSP, F32)
    modT = calloc(2 * 72 * 2, F32).rearrange("p (l c w) -> p l c w", l=2, w=2)
    gsT = calloc(2 * 3 * 8 * 2, F32).rearrange("p (l s f w) -> p l s f w", l=2, s=3, w=2)
    gtT = calloc(2 * 3 * 8 * 2, F32).rearrange("p (l s f w) -> p l s f w", l=2, s=3, w=2)
    cs_t = calloc(16, F32).rearrange("p (k w) -> p k w", w=2)
    lbT = calloc(16, F32).rearrange("p (r f) -> p r f", f=8)
    omlT = calloc(16, F32).rearrange("p (r f) -> p r f", f=8)
    lbtmp = calloc(16, F32).rearrange("p (r f) -> p r f", f=8)
    bC = Buf("const")

    sp_cc = smallp[:, 0:16].rearrange("p (k w) -> p k w", w=2)
    sp_ng = smallp[:, 16:64].rearrange("p (l s f) -> p l s f", l=2, s=3)
    sp_cw = smallp[:, 64:88].rearrange("p (j f) -> p j f", j=3)
    sp_lb = smallp[:, 88:120].rearrange("p (l r f) -> p l r f", l=2, r=2)
    sp_gn = smallp[:, 120:128]
    sp_fg = smallp[:, 128:136]
    sp_ab = smallp[:, 136:280].rearrange("p (l c) -> p l c", l=2)

    Hh = view(o_h, 8 * 1024, F32).rearrange("p (f t) -> p f t", f=8)
    XN = view(o_xn, 8 * 1024, BF16).rearrange("p (f t) -> p f t", f=8)
    HID = view(o_hid, HC * 1024, BF16).rearrange("p (j t) -> p j t", j=HC)
    bH = [[Buf() for _ in range(2)] for _ in range(8)]
    bXN = [[Buf() for _ in range(2)] for _ in range(8)]
    bHID = [[Buf() for _ in range(2)] for _ in range(HC)]
    SQ = view(o_sq, 8 * 512, BF16).rearrange("p (f t) -> p f t", f=8)
    bSQ = [Buf() for _ in range(8)]
    tmpR = Ring([(view(o_tmp + i * 2 * KB, 512, F32), Buf()) for i in range(4)])
    RT = view(o_rs, 512, F32)
    RSTD = view(o_rs + 2 * KB, 512, F32)
    bRT = Buf()
    bRSTD = Buf()
    sq_i = [0]

    psb = [nc.alloc_psum_tensor("ps%d" % i, [128, 512], F32).ap() for i in range(7)]
    psR = Ring([(psb[i], Buf()) for i in range(6)])
    ps_ss = psb[6]
    b_ss = Buf()
    ps_bf = nc.alloc_psum_tensor("psbf", [128, 1024], BF16).ap()
    b_psbf = Buf()

    def sl(ap, a, n):
        return ap[:, a:a + n]

    wslots = [(view(o_wr + i * 8 * KB, 4096, BF16), Buf()) for i in range(NSLOT)]

    class WStream:
        def __init__(self):
            self.plan = []
            self.issued = 0
            self.cur = 0

        def add(self, dram_ap, dkey, kc, w):
            self.plan.append((dram_ap, dkey, kc, w))

        def _issue(self, i):
            dram_ap, dkey, kc, w = self.plan[i]
            slot, sb = wslots[i % NSLOT]
            dst = slot[:, 0:kc * w].rearrange("p (k c) -> p k c", c=w)
            P.op("sp", lambda e, d=dst, s=dram_ap: e.dma_start(out=d, in_=s),
                 reads=[dbuf(dkey)], writes=[sb], dma_key="wr%d" % (i % NSLOT))

        def get(self):
            i = self.cur
            while self.issued < min(len(self.plan), i + NSLOT):
                self._issue(self.issued)
                self.issued += 1
            dram_ap, dkey, kc, w = self.plan[i]
            slot, sb = wslots[i % NSLOT]
            self.cur += 1
            return slot[:, 0:kc * w].rearrange("p (k c) -> p k c", c=w), sb

    P.op("sp", lambda e: e.dma_start(out=smallp, in_=sp_d), writes=[bC], dma=True)
    P.op("pool", lambda e: e.memset(ones_bf, 1.0), writes=[bC])
    P.op("pool", lambda e: e.memset(ident_f, 1.0), writes=[bC])
    P.op("pool", lambda e: e.affine_select(out=ident_f, in_=ident_f, pattern=[[-1, 128]],
                                           compare_op=ALU.is_equal, fill=0.0, base=0, channel_multiplier=1),
         reads=[bC], writes=[bC])
    P.op("pool", lambda e: e.tensor_copy(out=ident_bf, in_=ident_f), reads=[bC], writes=[bC])
    P.op("pool", lambda e: e.memset(tri_f, 1.0), writes=[bC])
    P.op("pool", lambda e: e.affine_select(out=tri_f, in_=tri_f, pattern=[[1, 128]],
                                           compare_op=ALU.is_ge, fill=0.0, base=0, channel_multiplier=-1),
         reads=[bC], writes=[bC])
    P.op("pool", lambda e: e.memset(tri_b, 1.0), writes=[bC])
    P.op("pool", lambda e: e.affine_select(out=tri_b, in_=tri_b, pattern=[[-1, 128]],
                                           compare_op=ALU.is_ge, fill=0.0, base=0, channel_multiplier=1),
         reads=[bC], writes=[bC])
    for hh in range(2):
        P.op("pool", lambda e, hh=hh: e.tensor_copy(out=maskF[hh * 64:(hh + 1) * 64, :],
                                                    in_=tri_f[hh * 64:(hh + 1) * 64, hh * 64:(hh + 1) * 64]),
             reads=[bC], writes=[bC])
        P.op("pool", lambda e, hh=hh: e.tensor_copy(out=maskB[hh * 64:(hh + 1) * 64, :],
                                                    in_=tri_b[hh * 64:(hh + 1) * 64, hh * 64:(hh + 1) * 64]),
             reads=[bC], writes=[bC])
    P.op("pool", lambda e: e.memset(mask01, 1.0), writes=[bC])
    last_ci = P.op("pool", lambda e: e.memset(mask01.rearrange("p (c j) -> p c j", j=64)[:, :, 0:1], 0.0),
                   reads=[bC], writes=[bC])

    cvf = Ring([(view(o_cv + i * 12 * KB, 3072, F32), Buf()) for i in range(2)])
    cvb = Ring([(view(o_cv + 24 * KB + i * 6 * KB, 3072, BF16), Buf()) for i in range(2)])
    cv_n = [0]

    def conv_piece(src2d, kc, segs, dst_ap, dkey):
        i = cv_n[0]
        cv_n[0] += 1
        fv, fb = cvf.next()
        bv, bb = cvb.next()
        wtot = sum(w for _, w in segs)
        f3 = fv[:, 0:kc * wtot].rearrange("p (k c) -> p k c", c=wtot)
        b3 = bv[:, 0:kc * wtot].rearrange("p (k c) -> p k c", c=wtot)
        src3 = src2d.rearrange("(k p) n -> p k n", p=128)
        c = 0
        for si, (c0, w) in enumerate(segs):
            P.op("act" if i < N_ACT_CONV else "pool", lambda e, d=f3[:, :, c:c + w], s=src3[:, :, c0:c0 + w]: e.dma_start(out=d, in_=s),
                 writes=[fb], dma_key="cvl%d_%d" % (i % 2, si))
            c += w
        P.op("pool", lambda e: e.tensor_copy(out=b3, in_=f3), reads=[fb], writes=[bb])
        P.op("pool", lambda e: e.dma_start(out=dst_ap, in_=b3), reads=[bb], writes=[dbuf(dkey)],
             dma_key="cvs%d" % (i % 2))

    def conv_ffn(l, h):
        for j in range(HC):
            conv_piece(wgu_d[l, h], 8, [(j * 128, 128), (FF + j * 128, 128)], WGU[l][h][j], ("WGU", l, h, j))
        for m in range(8):
            conv_piece(wdn_d[l, h], HC, [(m * 128, 128)], WDN[l][h][m], ("WDN", l, h, m))

    def conv_all():
        conv_ffn(0, 0)
        for m in range(8):
            conv_piece(cwin_d, 8, [(m * 128, 128), (D + m * 128, 128), (2 * D + m * 128, 128)], CWIN[m], ("CWIN", m))
        for m in range(8):
            conv_piece(cwout_d, 8, [(m * 128, 128)], CWOUT[m], ("CWOUT", m))
        conv_ffn(0, 1)
        conv_ffn(1, 0)
        for hd in range(8):
            conv_piece(hwin_d, 8, [(hd * 128, 128), (2 * D + hd * 128, 128)], HWIN[hd][:, :, 0:256], ("HWIN", hd))
            conv_piece(hwin_d, 8, [(3 * D + hd * 128, 128), (4 * D + hd * 128, 128)], HWIN[hd][:, :, 256:512], ("HWIN", hd))
        for hf in range(2):
            conv_piece(hwin_d, 8, [(D + hf * 512, 256)], HWIN[8 + hf][:, :, 0:256], ("HWIN", 8 + hf))
            conv_piece(hwin_d, 8, [(D + hf * 512 + 256, 256)], HWIN[8 + hf][:, :, 256:512], ("HWIN", 8 + hf))
        for m in range(8):
            conv_piece(hwout_d, 8, [(m * 128, 128)], HWOUT[m], ("HWOUT", m))
        conv_ffn(1, 1)

    def stage0_pre():
        P.op("act", lambda e: e.activation(out=cs_t, in_=sp_cc, func=AF.Silu), reads=[bC], writes=[bC])
        P.op("dve", lambda e: e.tensor_tensor(out=lbtmp, in0=sp_lb[:, 1], in1=sp_lb[:, 0], op=ALU.subtract),
             reads=[bC], writes=[bC])
        P.op("act", lambda e: e.activation(out=lbT, in_=lbtmp, func=AF.Sigmoid), reads=[bC], writes=[bC])
        P.op("dve", lambda e: e.tensor_scalar(out=omlT, in0=lbT, scalar1=-1.0, scalar2=1.0, op0=ALU.mult, op1=ALU.add),
             reads=[bC], writes=[bC])

    def stage0():
        mT = view(o_h, 9 * D, F32)
        bmT = Buf()
        adaR = Ring([(view(o_hid + i * 16 * KB, 4096, F32).rearrange("p (k c) -> p k c", c=512), Buf())
                     for i in range(2)])
        for l in range(2):
            src3 = adaw_d[l].rearrange("(k p) n -> p k n", p=128)
            for cb in range(18):
                av, ab = adaR.next()
                P.op("sp", lambda e, d=av, s=src3[:, :, cb * 512:(cb + 1) * 512]: e.dma_start(out=d, in_=s),
                     writes=[ab], dma=True)
                pv, pb = psR.next()
                for kc in range(8):
                    P.op("pe", lambda e, o=pv[0:2, :], a=cs_t[:, kc, :], r=av[:, kc, :], kc=kc:
                         e.matmul(o, lhsT=a, rhs=r, start=(kc == 0), stop=(kc == 7)),
                         reads=[ab, bC], writes=[pb])
                P.op("dve", lambda e, o=mT[0:2, cb * 512:(cb + 1) * 512], i=pv[0:2, :]: e.tensor_copy(out=o, in_=i),
                     reads=[pb], writes=[bmT])
            pv, pb = psR.next()
            for ch in range(72):
                P.op("pe", lambda e, o=pv[:, ch * 2:ch * 2 + 2], i=mT[0:2, ch * 128:(ch + 1) * 128]:
                     e.transpose(o, i, ident_f[0:2, 0:2]), reads=[bmT, bC], writes=[pb])
            bias_b = bass.AP(sp_ab.tensor, sp_ab[:, l].offset, [list(sp_ab.ap[0]), [1, 72], [0, 2]])
            P.op("dve", lambda e, o=modT[:, l], i=pv[:, 0:144].rearrange("p (c w) -> p c w", w=2), b=bias_b:
                 e.tensor_tensor(out=o, in0=i, in1=b, op=ALU.add), reads=[pb, bC], writes=[bC])
            for s in range(3):
                g_b = bass.AP(sp_ng.tensor, sp_ng[:, l, s].offset, [list(sp_ng.ap[0]), [1, 8], [0, 2]])
                P.op("dve", lambda e, o=gsT[:, l, s], i=modT[:, l, (3 * s + 1) * 8:(3 * s + 2) * 8], g=g_b:
                     e.scalar_tensor_tensor(out=o, in0=i, scalar=1.0, in1=g, op0=ALU.add, op1=ALU.mult),
                     reads=[bC], writes=[bC])
                P.op("dve", lambda e, o=gtT[:, l, s], i=modT[:, l, (3 * s + 2) * 8:(3 * s + 3) * 8], s=s:
                     e.tensor_scalar(out=o, in0=i, scalar1=(1.0 if s == 1 else 0.5), scalar2=None, op0=ALU.mult),
                     reads=[bC], writes=[bC])

    def m_gs(l, s, fc, w):
        return gsT[:, l, s, fc, w:w + 1]

    def m_sh(l, s, fc, w):
        return modT[:, l, 3 * s * 8 + fc, w:w + 1]

    def m_gt(l, s, fc, w):
        return gtT[:, l, s, fc, w:w + 1]

    rsR = Ring([(view(o_rs + i * 2 * KB, 512, F32), Buf()) for i in range(2)])
    ssacc = [(ps_ss, b_ss), (ps_bf.bitcast(F32), b_psbf)]
    sqacc_i = [0]

    def preacc_square(m, nti, off, n):
        k = sqacc_i[0] % 8
        sqacc_i[0] += 1
        P.op("act", lambda e, o=SQ[:, k, 0:n], s=Hh[:, m, off:off + n]: e.activation(out=o, in_=s, func=AF.Square),
             reads=[bH[m][nti]], writes=[bSQ[k]])
        return (k, m, nti, n)

    def preacc_mm(pend):
        for (k, m, nti, n) in pend:
            av, ab = ssacc[nti]
            P.op("pe", lambda e, o=av[:, 0:n], r=SQ[:, k, 0:n]: e.matmul(o, lhsT=ones_bf, rhs=r, start=(m == 0), stop=(m == 7)),
                 reads=[bSQ[k], bC], writes=[ab])


    def sumsq_rstd(nts, nti, srcs, src_bufs, nfeat, pre=False):
        off, n = nts[nti]
        if pre:
            pv, pb = ssacc[nti]
            rv, rb = rsR.next()
            P.op("act", lambda e: e.activation(out=rv[:, 0:n], in_=pv[:, 0:n], func=AF.Ln, scale=1.0 / nfeat, bias=EPS),
                 reads=[pb], writes=[rb])
            P.op("act", lambda e: e.activation(out=rv[:, 0:n], in_=rv[:, 0:n], func=AF.Exp, scale=-0.5), reads=[rb], writes=[rb])
            return rv, rb
        nsrc = len(srcs)
        sqi = []
        for i in range(nsrc):
            k = sq_i[0] % 8 if nsrc == 1 else i
            sq_i[0] += 1
            sqi.append(k)
            P.op("act", lambda e, o=SQ[:, k, 0:n], s=srcs[i]: e.activation(out=o, in_=s, func=AF.Square),
                 reads=[src_bufs[i]], writes=[bSQ[k]])
        pv, pb = psR.next()
        for i in range(nsrc):
            k = sqi[i]
            P.op("pe", lambda e, o=pv[:, 0:n], r=SQ[:, k, 0:n], i=i:
                 e.matmul(o, lhsT=ones_bf, rhs=r, start=(i == 0), stop=(i == nsrc - 1)),
                 reads=[bSQ[k], bC], writes=[pb])
        rv, rb = rsR.next()
        P.op("act", lambda e: e.activation(out=rv[:, 0:n], in_=pv[:, 0:n], func=AF.Ln, scale=1.0 / nfeat, bias=EPS),
             reads=[pb], writes=[rb])
        P.op("act", lambda e: e.activation(out=rv[:, 0:n], in_=rv[:, 0:n], func=AF.Exp, scale=-0.5), reads=[rb], writes=[rb])
        return rv, rb

    def norm_mod(nts, l, s, w, pre=True):
        for nti, (off, n) in enumerate(nts):
            RSTD, bRSTD = sumsq_rstd(nts, nti, [Hh[:, fc, off:off + n] for fc in range(8)], [bH[fc][nti] for fc in range(8)], D, pre=pre)
            for fc in range(8):
                tv, tb = tmpR.next()
                P.op("dve", lambda e, o=tv[:, 0:n], a=Hh[:, fc, off:off + n]:
                     e.tensor_tensor(out=o, in0=a, in1=RSTD[:, 0:n], op=ALU.mult),
                     reads=[bH[fc][nti], bRSTD], writes=[tb])
                P.op("act", lambda e, o=XN[:, fc, off:off + n], i=tv[:, 0:n], fc=fc:
                     e.activation(out=o, in_=i, func=AF.Identity, scale=m_gs(l, s, fc, w), bias=m_sh(l, s, fc, w)),
                     reads=[tb, bC], writes=[bXN[fc][nti]])

    BG = [None]

    def tick(nmax=1):
        for _ in range(nmax):
            g = BG[0]
            if g is None:
                return
            try:
                next(g)
            except StopIteration:
                BG[0] = None

    def ffn(ws, nts, l, hf, w, pre=True):
        s = 0 if hf == 0 else 2
        norm_mod(nts, l, s, w, pre=pre)
        for j in range(HC):
            wv, wb = ws.get()
            for nti, (off, n) in enumerate(nts):
                pg, bg_ = psR.next()
                pu, bu_ = psR.next()
                for kc in range(8):
                    P.op("pe", lambda e, o=pg[:, 0:n], a=wv[:, kc, 0:128], r=XN[:, kc, off:off + n], kc=kc:
                         e.matmul(o, lhsT=a, rhs=r, start=(kc == 0), stop=(kc == 7)),
                         reads=[wb, bXN[kc][nti]], writes=[bg_])
                for kc in range(8):
                    P.op("pe", lambda e, o=pu[:, 0:n], a=wv[:, kc, 128:256], r=XN[:, kc, off:off + n], kc=kc:
                         e.matmul(o, lhsT=a, rhs=r, start=(kc == 0), stop=(kc == 7)),
                         reads=[wb, bXN[kc][nti]], writes=[bu_])
                tv, tb = tmpR.next()
                P.op("act", lambda e, o=tv[:, 0:n], i=pg[:, 0:n]: e.activation(out=o, in_=i, func=AF.Silu),
                     reads=[bg_], writes=[tb])
                P.op("dve", lambda e, o=HID[:, j, off:off + n], a=tv[:, 0:n], b=pu[:, 0:n]:
                     e.tensor_tensor(out=o, in0=a, in1=b, op=ALU.mult),
                     reads=[tb, bu_], writes=[bHID[j][nti]])
            tick()
        pend = []
        for m in range(8):
            wv, wb = ws.get()
            pend_new = []
            for nti, (off, n) in enumerate(nts):
                py, by_ = psR.next()
                for j in range(HC):
                    P.op("pe", lambda e, o=py[:, 0:n], a=wv[:, j, :], r=HID[:, j, off:off + n], j=j:
                         e.matmul(o, lhsT=a, rhs=r, start=(j == 0), stop=(j == HC - 1)),
                         reads=[wb, bHID[j][nti]], writes=[by_])
                P.op("dve", lambda e, o=Hh[:, m, off:off + n], i=py[:, 0:n], m=m:
                     e.scalar_tensor_tensor(out=o, in0=i, scalar=m_gt(l, s, m, w), in1=o, op0=ALU.mult, op1=ALU.add),
                     reads=[by_, bC, bH[m][nti]], writes=[bH[m][nti]])
                pend_new.append(preacc_square(m, nti, off, n))
            preacc_mm(pend)
            pend = pend_new
            tick()
        preacc_mm(pend)

    def ffn_plan(ws, l, hf):
        for j in range(HC):
            ws.add(WGU[l][hf][j], ("WGU", l, hf, j), 8, 256)
        for m in range(8):
            ws.add(WDN[l][hf][m], ("WDN", l, hf, m), HC, 128)

    def proj_out(ws, nts, l, w, SRC=None, bSRC=None):
        SRC = XN if SRC is None else SRC
        bSRC = bXN if bSRC is None else bSRC
        pend = []
        for m in range(8):
            wv, wb = ws.get()
            pend_new = []
            for nti, (off, n) in enumerate(nts):
                py, by_ = psR.next()
                for kc in range(8):
                    P.op("pe", lambda e, o=py[:, 0:n], a=wv[:, kc, :], r=SRC[:, kc, off:off + n], kc=kc:
                         e.matmul(o, lhsT=a, rhs=r, start=(kc == 0), stop=(kc == 7)),
                         reads=[wb, bSRC[kc][nti]], writes=[by_])
                P.op("dve", lambda e, o=Hh[:, m, off:off + n], i=py[:, 0:n], m=m:
                     e.scalar_tensor_tensor(out=o, in0=i, scalar=m_gt(l, 1, m, w), in1=o, op0=ALU.mult, op1=ALU.add),
                     reads=[by_, bC, bH[m][nti]], writes=[bH[m][nti]])
                pend_new.append(preacc_square(m, nti, off, n))
            preacc_mm(pend)
            pend = pend_new
        preacc_mm(pend)

    def supertiles(with_ctx):
        sts = []
        if with_ctx:
            sts.append((0, 256, [(0, 256)], 1))
        for k in range(8):
            sts.append((LC + k * 1024, 1024, [(0, 512), (512, 512)], 0))
        return sts

    def load_h(src, u0, T, nts):
        P.op("sp", lambda e: e.dma_start(out=Hh[:, :, 0:T], in_=src.rearrange("f p u -> p f u")[:, :, u0:u0 + T]),
             reads=[dbuf((id(src), u0))], writes=[bH[fc][nti] for fc in range(8) for nti in range(len(nts))], dma=True)

    def store_h(dst, u0, T, nts):
        P.op("sp", lambda e: e.dma_start(out=dst.rearrange("f p u -> p f u")[:, :, u0:u0 + T], in_=Hh[:, :, 0:T]),
             reads=[bH[fc][nti] for fc in range(8) for nti in range(len(nts))], writes=[dbuf((id(dst), u0))], dma=True)

    def stage1():
        ws = WStream()
        sts = supertiles(True)
        for st in sts:
            ffn_plan(ws, 0, 0)
            for m in range(8):
                ws.add(CWIN[m], ("CWIN", m), 8, 384)
        XT = view(o_hid, 8 * 1024, F32).rearrange("p (b d) -> p b d", d=1024)
        bXT = Buf()
        stgR = Ring([(view(o_stg + i * 2 * KB, 512, F32), Buf()) for i in range(6)])
        for (u0, T, nts, w) in sts:
            nb = T // 128
            src = ctx_d if w else x_d[u0 - LC:u0 - LC + T]
            P.op("sp", lambda e, s=src, nb=nb: e.dma_start(out=XT[:, 0:nb, :], in_=s.rearrange("(b p) d -> p b d", p=128)),
                 writes=[bXT] + [bHID[j][nti] for j in range(HC) for nti in range(2)], dma=True)
            k = 0
            for nti, (off, n) in enumerate(nts):
                for fc in range(8):
                    pv, pb = psR.next()
                    for bq in range(n // 128):
                        blk = off // 128 + bq
                        P.op("pe", lambda e, o=pv[:, bq * 128:(bq + 1) * 128], i=XT[:, blk, fc * 128:(fc + 1) * 128]:
                             e.transpose(o, i, ident_f), reads=[bXT, bC], writes=[pb])
                    eng = "act" if k % 2 == 0 else "dve"
                    k += 1
                    if eng == "act":
                        P.op("act", lambda e, o=Hh[:, fc, off:off + n], i=pv[:, 0:n]: e.activation(out=o, in_=i, func=AF.Copy),
                             reads=[pb], writes=[bH[fc][nti]])
                    else:
                        P.op("dve", lambda e, o=Hh[:, fc, off:off + n], i=pv[:, 0:n]: e.tensor_copy(out=o, in_=i),
                             reads=[pb], writes=[bH[fc][nti]])
            for j in range(HC):
                for nti in range(2):
                    bHID[j][nti].readers.extend(bXT.readers)
            ffn(ws, nts, 0, 0, w, pre=False)
            store_h(hA, u0, T, nts)
            norm_mod(nts, 0, 1, w)
            for m in range(8):
                wv, wb = ws.get()
                for nti, (off, n) in enumerate(nts):
                    pB, bB = psR.next()
                    pC, bCc = psR.next()
                    pV, bV = psR.next()
                    for (pp, bb, c0) in ((pB, bB, 0), (pC, bCc, 128), (pV, bV, 256)):
                        for kc in range(8):
                            P.op("pe", lambda e, o=pp[:, 0:n], a=wv[:, kc, c0:c0 + 128], r=XN[:, kc, off:off + n], kc=kc:
                                 e.matmul(o, lhsT=a, rhs=r, start=(kc == 0), stop=(kc == 7)),
                                 reads=[wb, bXN[kc][nti]], writes=[bb])
                    tv, tb = tmpR.next()
                    P.op("act", lambda e, o=tv[:, 0:n], i=pC[:, 0:n]: e.activation(out=o, in_=i, func=AF.Copy),
                         reads=[bCc], writes=[tb])
                    zv, zb = stgR.next()
                    P.op("dve", lambda e, o=zv[:, 0:n], a=tv[:, 0:n], b=pV[:, 0:n]: e.tensor_tensor(out=o, in0=a, in1=b, op=ALU.mult),
                         reads=[tb, bV], writes=[zb])
                    P.op("sp", lambda e, d=zA[m, :, u0 + off:u0 + off + n], s=zv[:, 0:n]: e.dma_start(out=d, in_=s),
                         reads=[zb], writes=[dbuf(("zA", m))], dma=True)
                    gv, gb = stgR.next()
                    P.op("act", lambda e, o=gv[:, 0:n], i=pB[:, 0:n]: e.activation(out=o, in_=i, func=AF.Copy),
                         reads=[bB], writes=[gb])
                    P.op("sp", lambda e, d=bgA[m, :, u0 + off:u0 + off + n], s=gv[:, 0:n]: e.dma_start(out=d, in_=s),
                         reads=[gb], writes=[dbuf(("bgA", m))], dma=True)

    def stage2():
        ws = WStream()
        sts = supertiles(True)
        for st in sts:
            for m in range(8):
                ws.add(CWOUT[m], ("CWOUT", m), 8, 128)
            ffn_plan(ws, 0, 1)
            ffn_plan(ws, 1, 0)
            for c in range(10):
                ws.add(HWIN[c], ("HWIN", c), 8, 512)
        zinR = Ring([(view(o_stg + i * 4608, 1152, F32), Buf()) for i in range(2)])
        bginR = Ring([(view(o_stg + 9216 + i * 4096, 1024, F32), Buf()) for i in range(2)])
        ZC = view(o_stg + 9216 + 8192, 1024, F32)
        bZC = Buf()
        hoff = [o_hid, o_wr]

        def halloc(nelem, dt):
            v = view(hoff[0], nelem, dt)
            hoff[0] += (nelem * (4 if dt == F32 else 2) + 31) // 32 * 32
            assert hoff[0] <= hoff[1], (hoff, nelem)
            return v

        def mkset(bw):
            d = dict(e=(halloc(512, F32), Buf()), la=(halloc(512, F32), Buf()), lb=(halloc(512, F32), Buf()),
                     g=(halloc(512, F32), Buf()), sk=(halloc(512, BF16), Buf()))
            if bw:
                d["g2"] = (halloc(512, F32), Buf())
            return d
        fsets = [mkset(False), mkset(False)]
        bsets = [mkset(True), mkset(True)]
        s_og = Ring([(halloc(512, F32), Buf()) for _ in range(2)])
        hg_bufs = [ts[k][1] for ts in fsets + bsets for k in ts] + [b for _, b in s_og.items]
        hoff[0], hoff[1] = o_cv, AR_BYTES
        fsets.append(mkset(False))
        bsets.append(mkset(True))
        tqR = Ring([(halloc(512, F32), Buf()) for _ in range(3)])
        sqR = Ring([(halloc(512, BF16), Buf()) for _ in range(3)])
        sktR = Ring([(halloc(512, BF16), Buf()) for _ in range(3)])
        decR = Ring([(halloc(8, F32), Buf()) for _ in range(4)])
        s_v = Ring([(halloc(512, BF16), Buf()) for _ in range(2)])

        for (u0, T, nts, w) in sts:
            load_h(hA, u0, T, nts)
            R = 256 if w else 64
            for fc in range(8):
                zv, zb = zinR.next()
                gv, gb = bginR.next()
                lo = u0 - 64
                hi = u0 + T + 64
                if w:
                    lo, hi = u0, u0 + T
                else:
                    if lo < LC:
                        P.op("pool", lambda e, o=zv[:, 0:64]: e.memset(o, 0.0), writes=[zb])
                        lo = u0
                    if hi > UU:
                        P.op("pool", lambda e, o=zv[:, 64 + T:128 + T]: e.memset(o, 0.0), writes=[zb])
                        hi = u0 + T
                P.op("sp", lambda e, d=zv[:, 64 + lo - u0:64 + hi - u0], s=zA[fc, :, lo:hi]: e.dma_start(out=d, in_=s),
                     reads=[dbuf(("zA", fc))], writes=[zb], dma=True)
                P.op("sp", lambda e, d=gv[:, 0:T], s=bgA[fc, :, u0:u0 + T]: e.dma_start(out=d, in_=s),
                     reads=[dbuf(("bgA", fc))], writes=[gb], dma=True)
                cw0, cw1, cw2 = sp_cw[:, 0, fc:fc + 1], sp_cw[:, 1, fc:fc + 1], sp_cw[:, 2, fc:fc + 1]
                P.op("act", lambda e, i=zv[:, 64:64 + T], c=cw1: e.activation(out=ZC[:, 0:T], in_=i, func=AF.Copy, scale=c),
                     reads=[zb, bC], writes=[bZC])
                if w or fc < 4:
                    zc3 = ZC[:, 0:T].rearrange("p (r c) -> p r c", c=R)
                    zi3 = zv[:, 64:64 + T].rearrange("p (r c) -> p r c", c=R)
                    P.op("dve", lambda e, o=zc3[:, :, 1:R], i=zi3[:, :, 0:R - 1], c=cw0:
                         e.scalar_tensor_tensor(out=o, in0=i, scalar=c, in1=o, op0=ALU.mult, op1=ALU.add),
                         reads=[zb, bC, bZC], writes=[bZC])
                    P.op("dve", lambda e, o=zc3[:, :, 0:R - 1], i=zi3[:, :, 1:R], c=cw2:
                         e.scalar_tensor_tensor(out=o, in0=i, scalar=c, in1=o, op0=ALU.mult, op1=ALU.add),
                         reads=[zb, bC, bZC], writes=[bZC])
                else:
                    P.op("dve", lambda e, i=zv[:, 0:T], c=cw0:
                         e.scalar_tensor_tensor(out=ZC[:, 0:T], in0=i, scalar=c, in1=ZC[:, 0:T], op0=ALU.mult, op1=ALU.add),
                         reads=[zb, bC, bZC], writes=[bZC])
                    P.op("dve", lambda e, i=zv[:, 128:128 + T], c=cw2:
                         e.scalar_tensor_tensor(out=ZC[:, 0:T], in0=i, scalar=c, in1=ZC[:, 0:T], op0=ALU.mult, op1=ALU.add),
                         reads=[zb, bC, bZC], writes=[bZC])
                P.op("pool", lambda e, o=XN[:, fc, 0:T], a=gv[:, 0:T]: e.tensor_tensor(out=o, in0=a, in1=ZC[:, 0:T], op=ALU.mult),
                     reads=[gb, bZC], writes=[bXN[fc][nti] for nti in range(len(nts))])
            proj_out(ws, nts, 0, w)
            ffn(ws, nts, 0, 1, w)
            ffn(ws, nts, 1, 0, w)
            if not w:
                store_h(hB, u0, T, nts)
            norm_mod(nts, 1, 1, w)
            hid_all = [bHID[j][nti] for j in range(HC) for nti in range(2)]
            P.op("dve", lambda e: e.engine_nop(), reads=[], writes=hid_all + hg_bufs)
            units = []
            for hd in range(8):
                for nti, (off, n) in enumerate(nts):
                    for dr in range(2):
                        ui = len(units)
                        units.append(dict(hd=hd, nti=nti, off=off, n=n, uo=u0 + off, dr=dr,
                                          ts=(fsets if dr == 0 else bsets)[(ui // 2) % 3]))
            wcur = [None]

            def ustep(k, U):
                hd, nti, off, n, uo, dr, ts = U["hd"], U["nti"], U["off"], U["n"], U["uo"], U["dr"], U["ts"]
                ncn = n // 64
                E, bE = ts["e"]
                LA, bLA = ts["la"]
                LB, bLB = ts["lb"]
                G, bG = ts["g"]
                sk_, bsk = ts["sk"]
                GG, bGG = (G, bG) if dr == 0 else ts["g2"]
                if k == 0:
                    if nti == 0 and dr == 0:
                        wcur[0] = ws.get()
                    wv, wb = wcur[0]

                    def mm8(c0):
                        pv, pb = psR.next()
                        for kc in range(8):
                            P.op("pe", lambda e, o=pv[:, 0:n], a=wv[:, kc, c0:c0 + 128], r=XN[:, kc, off:off + n], kc=kc:
                                 e.matmul(o, lhsT=a, rhs=r, start=(kc == 0), stop=(kc == 7)),
                                 reads=[wb, bXN[kc][nti]], writes=[pb])
                        return pv, pb
                    if dr == 0:
                        pq, bq_ = mm8(0)
                        tq, btq = tqR.next()
                        U["tq"] = (tq, btq)
                        P.op("act", lambda e: e.activation(out=tq[:, 0:n], in_=pq[:, 0:n], func=AF.Silu), reads=[bq_], writes=[btq])
                        if not w:
                            po, bo = mm8(384)
                            ov, bov = s_og.next()
                            P.op("act", lambda e: e.activation(out=ov[:, 0:n], in_=po[:, 0:n], func=AF.Silu), reads=[bo], writes=[bov])
                            P.op("sp", lambda e, d=SOG[hd, :, uo - LC:uo - LC + n]: e.dma_start(out=d, in_=ov[:, 0:n]),
                                 reads=[bov], writes=[dbuf(("SOG", hd))], dma=True)
                    else:
                        U["tq"] = units[U["ui"] - 1]["tq"]
                    px, bx = mm8(128 * (1 + dr))
                    P.op("act", lambda e: e.activation(out=E[:, 0:n], in_=px[:, 0:n], func=AF.Exp, scale=-1.0), reads=[bx], writes=[bE])
                elif k == 1:
                    P.op("act", lambda e: e.activation(out=LA[:, 0:n], in_=E[:, 0:n], func=AF.Ln, scale=lbT[:, dr, hd:hd + 1], bias=1.0),
                         reads=[bE, bC], writes=[bLA])
                    P.op("act", lambda e: e.activation(out=LB[:, 0:n], in_=E[:, 0:n], func=AF.Ln, bias=1.0), reads=[bE], writes=[bLB])
                elif k == 2:
                    P.op("pool", lambda e: e.tensor_tensor(out=LA[:, 0:n], in0=LA[:, 0:n], in1=LB[:, 0:n], op=ALU.subtract),
                         reads=[bLA, bLB], writes=[bLA])
                    P.op("dve", lambda e: e.tensor_tensor_scan(out=G[:, 0:n], data0=mask01[:, 0:n], data1=LA[:, 0:n],
                                                               initial=0.0, op0=ALU.mult, op1=ALU.add),
                         reads=[bLA, bC], writes=[bG])
                elif k == 3:
                    if dr == 1:
                        P.op("pool", lambda e: e.tensor_tensor(out=LA[:, 0:n], in0=LA[:, 0:n], in1=G[:, 0:n], op=ALU.subtract),
                             reads=[bLA, bG], writes=[bLA])
                        last = bass.AP(G.tensor, G.offset + 63, [list(G.ap[0]), [64, ncn], [0, 64]])
                        P.op("dve", lambda e: e.tensor_tensor(out=GG[:, 0:n].rearrange("p (c j) -> p c j", j=64),
                                                               in0=LA[:, 0:n].rearrange("p (c j) -> p c j", j=64), in1=last, op=ALU.add),
                             reads=[bLA, bG], writes=[bGG])
                    P.op("pool", lambda e: e.tensor_tensor(out=LB[:, 0:n], in0=LB[:, 0:n], in1=GG[:, 0:n], op=ALU.add),
                         reads=[bLB, bGG], writes=[bLB])
                elif k == 4:
                    P.op("act", lambda e: e.activation(out=LA[:, 0:n], in_=GG[:, 0:n], func=AF.Exp), reads=[bGG], writes=[bLA])
                    P.op("act", lambda e: e.activation(out=LB[:, 0:n], in_=LB[:, 0:n], func=AF.Exp, scale=-1.0), reads=[bLB], writes=[bLB])
                elif k == 5:
                    tq, btq = U["tq"]
                    sq_, bsq = sqR.next()
                    P.op("dve", lambda e: e.tensor_tensor(out=sq_[:, 0:n], in0=tq[:, 0:n], in1=LA[:, 0:n], op=ALU.mult),
                         reads=[btq, bLA], writes=[bsq])
                    P.op("sp", lambda e, d=QK[hd, dr, :, 0, uo:uo + n]: e.dma_start(out=d, in_=sq_[:, 0:n]),
                         reads=[bsq], writes=[dbuf(("QK", hd, dr))], dma=True)
                    P.op("dve", lambda e: e.scalar_tensor_tensor(out=sk_[:, 0:n], in0=E[:, 0:n], scalar=omlT[:, dr, hd:hd + 1],
                                                                 in1=LB[:, 0:n], op0=ALU.mult, op1=ALU.mult),
                         reads=[bE, bLB, bC], writes=[bsk])
                    P.op("sp", lambda e, d=QK[hd, dr, :, 1, uo:uo + n]: e.dma_start(out=d, in_=sk_[:, 0:n]),
                         reads=[bsk], writes=[dbuf(("QK", hd, dr))], dma=True)
                    dv, bdv = decR.next()
                    pos = 63 if dr == 0 else 0
                    dsrc = bass.AP(LA.tensor, LA.offset + pos, [list(LA.ap[0]), [64, ncn]])
                    P.op("pool", lambda e: e.tensor_copy(out=dv[:, 0:ncn], in_=dsrc), reads=[bLA], writes=[bdv])
                    P.op("sp", lambda e, d=DEC[hd, dr, :, uo // 64:uo // 64 + ncn]: e.dma_start(out=d, in_=dv[:, 0:ncn]),
                         reads=[bdv], writes=[dbuf(("DEC", hd, dr))], dma=True)
                elif k == 6:
                    pv, pb = psR.next()
                    pvb = pv[:, 0:256].bitcast(BF16)
                    for bq in range(n // 128):
                        P.op("pe", lambda e, o=pvb[:, bq * 128:(bq + 1) * 128], i=sk_[:, bq * 128:(bq + 1) * 128]:
                             e.transpose(o, i, ident_bf), reads=[bsk, bC], writes=[pb])
                    skt, bskt = sktR.next()
                    P.op("dve", lambda e: e.tensor_copy(out=skt[:, 0:n], in_=pvb[:, 0:n]), reads=[pb], writes=[bskt])
                    P.op("sp", lambda e, d=KVT[hd, dr, uo:uo + n, 0, :].rearrange("(b t) d -> t b d", t=128):
                         e.dma_start(out=d, in_=skt[:, 0:n].rearrange("p (b d) -> p b d", d=128)),
                         reads=[bskt], writes=[dbuf(("KVT", hd, dr))], dma=True)
            for ui, U in enumerate(units):
                U["ui"] = ui
            NU = len(units)
            for tick in range(NU + 6):
                for k in range(6, -1, -1):
                    ui = tick - k
                    if 0 <= ui < NU:
                        ustep(k, units[ui])
            for hf in range(2):
                wv, wb = ws.get()
                for blk in range(T // 128):
                    nti = (blk * 128) // 512
                    pv, pb = psR.next()
                    for kc in range(8):
                        P.op("pe", lambda e, o=pv, a=XN[:, kc, blk * 128:(blk + 1) * 128], r=wv[:, kc, :], kc=kc:
                             e.matmul(o, lhsT=a, rhs=r, start=(kc == 0), stop=(kc == 7)),
                             reads=[wb, bXN[kc][nti]], writes=[pb])
                    sv, bsv = s_v.next()
                    P.op("act", lambda e, sv=sv, pv=pv: e.activation(out=sv, in_=pv, func=AF.Copy), reads=[pb], writes=[bsv])
                    ub = u0 + blk * 128
                    for dr in range(2):
                        P.op("sp", lambda e, d=KVT[hf * 4:hf * 4 + 4, dr, ub:ub + 128, 1, :].rearrange("h t d -> t h d"),
                             s=sv.rearrange("p (h d) -> p h d", d=128): e.dma_start(out=d, in_=s),
                             reads=[bsv], writes=[dbuf(("KVT", hf * 4 + q, dr)) for q in range(4)], dma=True)
            P.op("dve", lambda e: e.engine_nop(), reads=[], writes=hid_all + hg_bufs)

    def stage3():
        NG = DBG.get('NG', UU // 256)
        per = 16 * KB
        for hg in range(DBG.get('HG', 2)):
            chains = []
            for q in range(4):
                for dr in range(2):
                    ci = q * 2 + dr
                    base = o_h + ci * per
                    off = [base]

                    def al(nelem, dt):
                        v = view(off[0], nelem, dt)
                        off[0] += (nelem * (4 if dt == F32 else 2) + 31) // 32 * 32
                        assert off[0] <= base + per
                        return v
                    ch = dict(hd=hg * 4 + q, dr=dr,
                              qk=[(al(512, BF16), Buf()) for _ in range(2)],
                              kv=[(al(512, BF16), Buf()) for _ in range(2)],
                              attm=[(al(128, BF16), Buf()) for _ in range(2)],
                              Pm=[(al(128, F32), Buf()) for _ in range(2)],
                              Sb=[(al(128, BF16), Buf()) for _ in range(2)],
                              ost=[(al(256, F32), [Buf() for _ in range(4)]) for _ in range(2)],
                              dec=(al(132, F32), Buf()), step=0)
                    chains.append(ch)
            for ch in chains:
                hd, dr = ch["hd"], ch["dr"]
                P.op("sp", lambda e, d=ch["dec"][0], s=DEC[hd, dr]: e.dma_start(out=d, in_=s),
                     reads=[dbuf(("DEC", hd, dr))], writes=[ch["dec"][1]], dma=True)
                P.op("pool", lambda e, o=ch["Pm"][1][0]: e.memset(o, 0.0), writes=[ch["Pm"][1][1]])
                P.op("pool", lambda e, o=ch["Sb"][1][0]: e.memset(o, 0.0), writes=[ch["Sb"][1][1]])

            def gidx(ch, gi):
                if ch["dr"] == 0 or gi == 0:
                    return gi
                return NG - gi

            def issue_loads(ch, gi):
                g = gidx(ch, gi)
                hd, dr = ch["hd"], ch["dr"]
                qv, qb = ch["qk"][gi % 2]
                kv, kb = ch["kv"][gi % 2]
                P.op("sp", lambda e, d=qv.rearrange("p (a t) -> p a t", a=2), s=QK[hd, dr, :, :, g * 256:(g + 1) * 256]:
                     e.dma_start(out=d, in_=s), reads=[dbuf(("QK", hd, dr))], writes=[qb], dma=True)
                P.op("sp", lambda e, d=kv.rearrange("p (b a d) -> p b a d", b=2, a=2),
                     s=KVT[hd, dr, g * 256:(g + 1) * 256].rearrange("(b t) a d -> t b a d", t=128):
                     e.dma_start(out=d, in_=s), reads=[dbuf(("KVT", hd, dr))], writes=[kb], dma=True)

            for ch in chains:
                issue_loads(ch, 0)
            pbk = [Buf() for _ in range(7)]
            psA = Ring([(psb[0][:, 0:128], pbk[0])])
            psO = Ring([(psb[1 + i][:, 0:64], pbk[1 + i]) for i in range(3)])
            psU = Ring([(psb[4 + i][:, 0:128], pbk[4 + i]) for i in range(3)])
            kk = 0
            for gi in range(NG):
                for ch in chains:
                    if gi + 1 < NG:
                        issue_loads(ch, gi + 1)
                    g = gidx(ch, gi)
                    hd, dr = ch["hd"], ch["dr"]
                    qv, qb = ch["qk"][gi % 2]
                    kv, kb = ch["kv"][gi % 2]
                    q2 = qv.rearrange("p (a t) -> p a t", a=2)
                    kv4 = kv.rearrange("p (b a d) -> p b a d", b=2, a=2)
                    latent = g >= 1
                    ch["cur_attm"] = None
                    if latent:
                        av, ab = psA.next()
                        for i in range(4):
                            b_, h_ = i // 2, i % 2
                            P.op("pe", lambda e, o=av[h_ * 64:(h_ + 1) * 64, b_ * 64:(b_ + 1) * 64],
                                 a=q2[:, 1, i * 64:(i + 1) * 64], r=q2[:, 0, i * 64:(i + 1) * 64]:
                                 e.matmul(o, lhsT=a, rhs=r, start=True, stop=True), reads=[qb], writes=[ab])
                        mv, mb = ch["attm"][gi % 2]
                        msk = maskF if dr == 0 else maskB
                        mb3 = bass.AP(msk.tensor, msk.offset, [list(msk.ap[0]), [0, 2], [1, 64]])
                        P.op("dve", lambda e, o=mv.rearrange("p (b j) -> p b j", j=64), i=av.rearrange("p (b j) -> p b j", j=64), m=mb3:
                             e.tensor_tensor(out=o, in0=i, in1=m, op=ALU.mult), reads=[ab, bC], writes=[mb])
                        ch["cur_attm"] = (mv, mb)
                        ch["cur_ost"] = ch["ost"][gi % 2]
                for ii in range(4):
                    for ch in chains:
                        g = gidx(ch, gi)
                        hd, dr = ch["hd"], ch["dr"]
                        i = ii if dr == 0 else 3 - ii
                        b_, h_ = i // 2, i % 2
                        c_glob = g * 4 + i
                        qv, qb = ch["qk"][gi % 2]
                        kv, kb = ch["kv"][gi % 2]
                        q2 = qv.rearrange("p (a t) -> p a t", a=2)
                        kv4 = kv.rearrange("p (b a d) -> p b a d", b=2, a=2)
                        st_ = ch["step"]
                        Pold, bPold = ch["Pm"][(st_ + 1) % 2]
                        Pnew, bPnew = ch["Pm"][st_ % 2]
                        Sold, bSold = ch["Sb"][(st_ + 1) % 2]
                        Snew, bSnew = ch["Sb"][st_ % 2]
                        decv, decb = ch["dec"]
                        ktm = kv4[h_ * 64:(h_ + 1) * 64, b_, 0, :]
                        vtm = kv4[h_ * 64:(h_ + 1) * 64, b_, 1, :]
                        if g >= 1:
                            mv, mb = ch["cur_attm"]
                            ov, ob = psO.next()
                            P.op("pe", lambda e, o=ov, a=vtm, r=mv[h_ * 64:(h_ + 1) * 64, b_ * 64:(b_ + 1) * 64]:
                                 e.matmul(o, lhsT=a, rhs=r, start=True, stop=False), reads=[kb, mb], writes=[ob])
                            P.op("pe", lambda e, o=ov, a=Sold, r=q2[:, 0, i * 64:(i + 1) * 64]:
                                 e.matmul(o, lhsT=a, rhs=r, start=False, stop=True), reads=[bSold, qb], writes=[ob])
                            osv, osb = ch["cur_ost"]
                            kk += 1
                            if kk % 2 == 0:
                                P.op("act", lambda e, o=osv[:, i * 64:(i + 1) * 64], s=ov: e.activation(out=o, in_=s, func=AF.Copy),
                                     reads=[ob], writes=[osb[i]])
                            else:
                                P.op("dve", lambda e, o=osv[:, i * 64:(i + 1) * 64], s=ov: e.tensor_copy(out=o, in_=s),
                                     reads=[ob], writes=[osb[i]])
                        uv, ub = psU.next()
                        P.op("pe", lambda e, o=uv, a=ktm, r=vtm: e.matmul(o, lhsT=a, rhs=r, start=True, stop=True),
                             reads=[kb], writes=[ub])
                        if st_ == 0:
                            P.op("dve", lambda e, o=Pnew, s=uv: e.tensor_copy(out=o, in_=s), reads=[ub], writes=[bPnew])
                        else:
                            pc = ch["prev_c"]
                            P.op("dve", lambda e, o=Pnew, a=Pold, s=uv, d=decv[:, pc:pc + 1]:
                                 e.scalar_tensor_tensor(out=o, in0=a, scalar=d, in1=s, op0=ALU.mult, op1=ALU.add),
                                 reads=[bPold, ub, decb], writes=[bPnew])
                        P.op("act", lambda e, o=Snew, a=Pnew, d=decv[:, c_glob:c_glob + 1]:
                             e.activation(out=o, in_=a, func=AF.Copy, scale=d), reads=[bPnew, decb], writes=[bSnew])
                        ch["prev_c"] = c_glob
                        ch["step"] = st_ + 1
                for ch in chains:
                    g = gidx(ch, gi)
                    if g >= 1:
                        hd, dr = ch["hd"], ch["dr"]
                        osv, osb = ch["cur_ost"]
                        dst = (OFW if dr == 0 else OBW)[hd, :, (g - 1) * 256:g * 256]
                        P.op("sp", lambda e, d=dst, s=osv: e.dma_start(out=d, in_=s), reads=osb,
                             writes=[dbuf(("O", dr, hd))], dma=True)

    def stage4():
        ws = WStream()
        sts = supertiles(False)
        for st in sts:
            for m in range(8):
                ws.add(HWOUT[m], ("HWOUT", m), 8, 128)
            ffn_plan(ws, 1, 1)
        inR = Ring([tuple((view(o_stg + (i * 3 + k) * 4 * KB, 1024, F32), Buf()) for k in range(3)) for i in range(2)])
        OST = view(o_hid, 4 * 1024, F32).rearrange("p (b d) -> p b d", d=1024)
        bOST = [Buf() for _ in range(8)]
        ONs = [view(o_cv + i * 16 * KB, 8 * 1024, BF16).rearrange("p (f t) -> p f t", f=8) for i in range(2)]
        bONs = [[[Buf() for _ in range(2)] for _ in range(8)] for _ in range(2)]

        rs4 = Ring([(view(o_rs + i * 2 * KB, 512, F32), Buf()) for i in range(2)] +
                   [(view(o_cv + 32 * KB + i * 2 * KB, 512, F32), Buf()) for i in range(2)])

        def pro(ki):
            (u0, T, nts, w) = sts[ki]
            t0 = u0 - LC
            ON, bON = ONs[ki % 2], bONs[ki % 2]
            for fp in range(4):
                items = []
                for fc in (2 * fp, 2 * fp + 1):
                    (fv, fb), (bv, bb), (gv, gb) = inR.next()
                    P.op("sp", lambda e, d=fv, s=OFW[fc, :, t0:t0 + T]: e.dma_start(out=d, in_=s),
                         reads=[dbuf(("O", 0, fc))], writes=[fb], dma=True)
                    P.op("sp", lambda e, d=bv, s=OBW[fc, :, t0:t0 + T]: e.dma_start(out=d, in_=s),
                         reads=[dbuf(("O", 1, fc))], writes=[bb], dma=True)
                    P.op("sp", lambda e, d=gv, s=SOG[fc, :, t0:t0 + T]: e.dma_start(out=d, in_=s),
                         reads=[dbuf(("SOG", fc))], writes=[gb], dma=True)
                    P.op("pool", lambda e, fv=fv, bv=bv: e.tensor_tensor(out=fv, in0=fv, in1=bv, op=ALU.add),
                         reads=[fb, bb], writes=[fb])
                    for nti, (off, n) in enumerate(nts):
                        items.append(dict(fc=fc, nti=nti, off=off, n=n, fv=fv, fb=fb, gv=gv, gb=gb))
                yield
                yield
                for it in items:
                    k = sq_i[0] % 8
                    sq_i[0] += 1
                    it["k"] = k
                    P.op("act", lambda e, o=SQ[:, k, 0:it["n"]], s=it["fv"][:, it["off"]:it["off"] + it["n"]]:
                         e.activation(out=o, in_=s, func=AF.Square), reads=[it["fb"]], writes=[bSQ[k]])
                yield
                yield
                for it in items:
                    it["ps"] = psR.next()
                    P.op("pe", lambda e, o=it["ps"][0][:, 0:it["n"]], r=SQ[:, it["k"], 0:it["n"]]:
                         e.matmul(o, lhsT=ones_bf, rhs=r, start=True, stop=True), reads=[bSQ[it["k"]], bC], writes=[it["ps"][1]])
                for it in items:
                    it["rs"] = rs4.next()
                    P.op("act", lambda e, o=it["rs"][0][:, 0:it["n"]], i=it["ps"][0][:, 0:it["n"]]:
                         e.activation(out=o, in_=i, func=AF.Ln, scale=1.0 / 128, bias=EPS), reads=[it["ps"][1]], writes=[it["rs"][1]])
                for it in items:
                    P.op("act", lambda e, o=it["rs"][0][:, 0:it["n"]]: e.activation(out=o, in_=o, func=AF.Exp, scale=-0.5),
                         reads=[it["rs"][1]], writes=[it["rs"][1]])
                yield
                for it in items:
                    fc, nti, off, n = it["fc"], it["nti"], it["off"], it["n"]
                    tv, tb = tmpR.next()
                    P.op("dve", lambda e, o=tv[:, 0:n], a=it["fv"][:, off:off + n], r=it["rs"][0][:, 0:n]:
                         e.tensor_tensor(out=o, in0=a, in1=r, op=ALU.mult), reads=[it["fb"], it["rs"][1]], writes=[tb])
                    P.op("dve", lambda e, o=ON[:, fc, off:off + n], a=tv[:, 0:n], g=it["gv"][:, off:off + n], fc=fc:
                         e.scalar_tensor_tensor(out=o, in0=a, scalar=sp_gn[:, fc:fc + 1], in1=g, op0=ALU.mult, op1=ALU.mult),
                         reads=[tb, it["gb"], bC], writes=[bON[fc][nti]])
                yield

        for _ in pro(0):
            pass
        for ki, (u0, T, nts, w) in enumerate(sts):
            t0 = u0 - LC
            load_h(hB, u0, T, nts)
            BG[0] = pro(ki + 1) if ki + 1 < len(sts) else None
            proj_out(ws, nts, 1, 0, ONs[ki % 2], bONs[ki % 2])
            ffn(ws, nts, 1, 1, 0)
            tick(100)
            for nti, (off, n) in enumerate(nts):
                RSTD, bRSTD = sumsq_rstd(nts, nti, [Hh[:, fc, off:off + n] for fc in range(8)], [bH[fc][nti] for fc in range(8)], D, pre=True)
                for fc in range(8):
                    P.op("dve", lambda e, o=Hh[:, fc, off:off + n], fc=fc:
                         e.scalar_tensor_tensor(out=o, in0=o, scalar=sp_fg[:, fc:fc + 1], in1=RSTD[:, 0:n], op0=ALU.mult, op1=ALU.mult),
                         reads=[bH[fc][nti], bRSTD, bC], writes=[bH[fc][nti]])
                hid_all = [bHID[j][q] for j in range(HC) for q in range(2)]
                k = 0
                for blk in range(4):
                    for half in range(2):
                        pv, pb = psR.next()
                        for q in range(4):
                            fc = half * 4 + q
                            P.op("pe", lambda e, o=pv[:, q * 128:(q + 1) * 128], i=Hh[:, fc, off + blk * 128:off + (blk + 1) * 128]:
                                 e.transpose(o, i, ident_f), reads=[bH[fc][nti], bC], writes=[pb])
                        k += 1
                        wr = [bOST[blk * 2 + half]] + (hid_all if (blk == 0 and half == 0) else [])
                        if k % 2 == 0:
                            P.op("act", lambda e, o=OST[:, blk, half * 512:(half + 1) * 512], i=pv: e.activation(out=o, in_=i, func=AF.Copy),
                                 reads=[pb], writes=wr)
                        else:
                            P.op("dve", lambda e, o=OST[:, blk, half * 512:(half + 1) * 512], i=pv: e.tensor_copy(out=o, in_=i),
                                 reads=[pb], writes=wr)
                a0 = t0 + off
                P.op("sp", lambda e, d=out_d[a0:a0 + 512, :].rearrange("(b p) d -> p b d", p=128): e.dma_start(out=d, in_=OST),
                     reads=bOST, writes=[dbuf("out")] + hid_all, dma_key="out")

    if not DBG.get("NOCONV"):
        stage0_pre()
        conv_all()
        stage0()
    P.barrier(skip_pool=True, extra=[last_ci])
    if 1 in stages:
        stage1()
        P.barrier()
    if 2 in stages:
        stage2()
        P.barrier()
    if 3 in stages:
        stage3()
        P.barrier()
    if 4 in stages:
        stage4()
    fk = [k for k in P.dma_keys if k.startswith("auto_sp") or k == "out"]
    P.emit(final_wait_keys=fk)
    return nc


def _fm(v):
    return np.ascontiguousarray(np.asarray(v, np.float32).reshape(8, 128).T)


def make_inputs(b, x, c, ctx, c_ctx, ada_w, ada_b, norm_g, ffn_w_gu, ffn_w_down, conv_w_in, conv_w,
                conv_w_out, hg_w_in, hg_lb_logits, hg_gnorm_g, hg_w_out, final_norm_g):
    sp = np.zeros((128, NSP), np.float32)
    cc = np.stack([_fm(c[b]), _fm(c_ctx)], axis=-1)
    sp[:, 0:16] = cc.reshape(128, 16)
    ng = np.stack([np.stack([_fm(norm_g[l, s]) for s in range(3)], 1) for l in range(2)], 1)
    sp[:, 16:64] = ng.reshape(128, 48)
    cw = np.stack([_fm(conv_w[0, j]) for j in range(3)], 1)
    sp[:, 64:88] = cw.reshape(128, 24)
    lb = np.stack([np.stack([_fm(hg_lb_logits[l, r]) for r in range(2)], 1) for l in range(2)], 1)
    sp[:, 88:120] = lb.reshape(128, 32)
    sp[:, 120:128] = _fm(hg_gnorm_g[0])
    sp[:, 128:136] = _fm(final_norm_g)
    ab = np.stack([np.ascontiguousarray(np.asarray(ada_b[l], np.float32).reshape(72, 128).T) for l in range(2)], 1)
    sp[:, 136:280] = ab.reshape(128, 144)
    return {
        "x": np.ascontiguousarray(x[b]), "ctx": np.ascontiguousarray(ctx[b]), "smallp": sp,
        "ada_w": ada_w, "ffn_w_gu": ffn_w_gu, "ffn_w_down": ffn_w_down,
        "conv_w_in": np.ascontiguousarray(conv_w_in[0]), "conv_w_out": np.ascontiguousarray(conv_w_out[0]),
        "hg_w_in": np.ascontiguousarray(hg_w_in[0]), "hg_w_out": np.ascontiguousarray(hg_w_out[0]),
    }


def kernel(**inputs):
    inputs = {k: np.asarray(v) for k, v in inputs.items()}
    nc = build_program()
    in_maps = [make_inputs(b, **inputs) for b in range(8)]
    res = run_bass_kernel_spmd(nc, in_maps, core_ids=list(range(8)))
    return np.stack([np.asarray(r["out"], np.float32) for r in res.results], axis=0)
```

```python
import types
import numpy as np
import concourse.bass as bass
import concourse.mybir as mybir
from concourse.bass_utils import run_bass_kernel_spmd

F32 = mybir.dt.float32
BF16 = mybir.dt.bfloat16
AF = mybir.ActivationFunctionType
ALU = mybir.AluOpType

D = 1024
FC = 8
FF = 2816
HC = 22
S = 8192
LC = 256
UU = S + LC
EPS = 1e-6
NSP = 16 + 48 + 24 + 32 + 8 + 8 + 144
ENGS = ("pe", "act", "dve", "pool", "sp")
DBG = {}
N_ACT_CONV = 38


def _freeze(fn):
    if fn.__closure__ is None:
        return fn
    cells = []
    for c in fn.__closure__:
        try:
            cells.append(types.CellType(c.cell_contents))
        except ValueError:
            cells.append(c)
    return types.FunctionType(fn.__code__, fn.__globals__, fn.__name__, fn.__defaults__, tuple(cells))


class Buf:
    __slots__ = ("name", "last_w", "readers")

    def __init__(self, name=""):
        self.name = name
        self.last_w = None
        self.readers = []


class Op:
    __slots__ = ("eng", "fn", "idx", "deps", "signal", "dma_key", "dma_cnt")


class Prog:
    def __init__(self, nc, n_auto=24):
        self.nc = nc
        self.ops = {e: [] for e in ENGS}
        self.order = []
        self.dma_keys = {}
        self.dma_last = {}
        self.pending = {e: [] for e in ENGS}
        self.n_auto = n_auto
        self.auto_i = {e: 0 for e in ENGS}

    def op(self, eng, fn, reads=(), writes=(), dma=False, dma_key=None):
        o = Op()
        o.eng = eng
        o.fn = _freeze(fn)
        o.idx = len(self.ops[eng])
        o.signal = False
        if dma and dma_key is None:
            dma_key = "auto_%s_%d" % (eng, self.auto_i[eng] % self.n_auto)
            self.auto_i[eng] += 1
        o.dma_key = dma_key
        o.dma_cnt = 0
        deps = list(self.pending[eng])
        self.pending[eng] = []
        for b in reads:
            if b.last_w is not None:
                deps.append(b.last_w)
        for b in writes:
            if b.last_w is not None:
                deps.append(b.last_w)
            deps.extend(b.readers)
        if dma_key is not None:
            prev = self.dma_last.get(dma_key)
            if prev is not None:
                deps.append(prev)
            self.dma_last[dma_key] = o
            c = self.dma_keys.get(dma_key, 0) + 1
            self.dma_keys[dma_key] = c
            o.dma_cnt = c
        red = {}
        for d in deps:
            if d is o:
                continue
            if eng == "pe" and d.eng == "pe" and d.dma_key is None:
                continue
            if d.dma_key is not None:
                k = ("dma", d.dma_key)
                v = d.dma_cnt
            else:
                k = ("eng", d.eng)
                v = d.idx
            if k not in red or red[k][0] < v:
                red[k] = (v, d)
        o.deps = red
        for b in writes:
            b.last_w = o
            b.readers = []
        for b in reads:
            if b not in writes:
                b.readers.append(o)
                if len(b.readers) > 48:
                    keep = {}
                    for r in b.readers:
                        k = ("dma", r.dma_key) if r.dma_key is not None else ("eng", r.eng)
                        keep[k] = r
                    b.readers = list(keep.values())
        self.ops[eng].append(o)
        self.order.append(o)
        return o

    def barrier(self, skip_pool=False, extra=()):
        lasts = list(extra)
        for e in ENGS:
            if skip_pool and e == "pool":
                continue
            for o in reversed(self.ops[e]):
                if o.dma_key is None:
                    lasts.append(o)
                    break
        for k, o in self.dma_last.items():
            if skip_pool and k.startswith("cv"):
                continue
            lasts.append(o)
        for e in ENGS:
            if skip_pool and e == "pool":
                continue
            self.pending[e].extend(lasts)

    def emit(self, final_wait_keys=()):
        nc = self.nc
        for o in self.order:
            for (k, (v, d)) in o.deps.items():
                if k[0] == "eng":
                    d.signal = True
        esem = {e: nc.alloc_semaphore(name="sem_%s" % e) for e in ENGS}
        dsem = {k: nc.alloc_semaphore(name="dsem_%d" % i) for i, k in enumerate(self.dma_keys)}
        cnts = {}
        for e in ENGS:
            c = 0
            arr = []
            for o in self.ops[e]:
                if o.signal and o.dma_key is None:
                    c += 1
                arr.append(c)
            cnts[e] = arr

        def run_engine(e, engobj):
            waited = {}
            for o in self.ops[e]:
                for (k, (v, d)) in o.deps.items():
                    if k[0] == "eng":
                        sem = esem[k[1]]
                        val = cnts[k[1]][v]
                    else:
                        sem = dsem[k[1]]
                        val = 16 * v
                    if waited.get(k, 0) >= val:
                        continue
                    engobj.wait_ge(sem, val)
                    waited[k] = val
                ins = o.fn(engobj)
                if o.dma_key is not None:
                    ins.then_inc(dsem[o.dma_key], 16)
                elif o.signal:
                    ins.then_inc(esem[e], 1)
            if e == "sp":
                for k in final_wait_keys:
                    engobj.wait_ge(dsem[k], 16 * self.dma_keys[k])

        with nc.Block() as block:
            @block.tensor
            def _(eng):
                run_engine("pe", eng)

            @block.scalar
            def _(eng):
                run_engine("act", eng)

            @block.vector
            def _(eng):
                run_engine("dve", eng)

            @block.gpsimd
            def _(eng):
                run_engine("pool", eng)

            @block.sync
            def _(eng):
                run_engine("sp", eng)


class Ring:
    def __init__(self, items):
        self.items = items
        self.i = 0

    def next(self):
        it = self.items[self.i % len(self.items)]
        self.i += 1
        return it


def build_program(debug=(), stages=(0, 1, 2, 3, 4)):
    nc = bass.Bass("TRN2", target_bir_lowering=False)
    P = Prog(nc)

    def din(name, shape):
        return nc.dram_tensor(name, shape, F32, kind="ExternalInput").ap()

    x_d = din("x", [S, D])
    ctx_d = din("ctx", [LC, D])
    sp_d = din("smallp", [128, NSP])
    adaw_d = din("ada_w", [2, D, 9 * D])
    wgu_d = din("ffn_w_gu", [2, 2, D, 2 * FF])
    wdn_d = din("ffn_w_down", [2, 2, FF, D])
    cwin_d = din("conv_w_in", [D, 3 * D])
    cwout_d = din("conv_w_out", [D, D])
    hwin_d = din("hg_w_in", [D, 5 * D])
    hwout_d = din("hg_w_out", [D, D])
    out_d = nc.dram_tensor("out", [S, D], F32, kind="ExternalOutput").ap()

    def scr(name, shape, dt=F32):
        kind = "ExternalOutput" if name in debug else "Internal"
        return nc.dram_tensor(name, shape, dt, kind=kind).ap()

    WGU = [[scr("WGU%d%d" % (l, h), [HC, 128, 8, 256], BF16) for h in range(2)] for l in range(2)]
    WDN = [[scr("WDN%d%d" % (l, h), [8, 128, HC, 128], BF16) for h in range(2)] for l in range(2)]
    CWIN = scr("CWIN", [8, 128, 8, 384], BF16)
    CWOUT = scr("CWOUT", [8, 128, 8, 128], BF16)
    HWIN = scr("HWIN", [10, 128, 8, 512], BF16)
    HWOUT = scr("HWOUT", [8, 128, 8, 128], BF16)
    hA = scr("hA", [8, 128, UU])
    zA = scr("zA", [8, 128, UU])
    bgA = scr("bgA", [8, 128, UU])
    hB = scr("hB", [8, 128, UU])
    QK = scr("QK", [8, 2, 128, 2, UU], BF16)
    KVT = scr("KVT", [8, 2, UU, 2, 128], BF16)
    DEC = scr("DEC", [8, 2, 128, 132])
    SOG = scr("SOG", [8, 128, S])
    OFW = scr("OFW", [8, 128, S])
    OBW = scr("OBW", [8, 128, S])
    dbufs = {}

    def dbuf(key):
        if key not in dbufs:
            dbufs[key] = Buf(str(key))
        return dbufs[key]

    AR_BYTES = 204 * 1024
    arena = nc.alloc_sbuf_tensor("arena", [128, AR_BYTES // 2], BF16).ap()

    def view(off, nelem, dt):
        assert off % 32 == 0
        if dt == F32:
            assert off + 4 * nelem <= AR_BYTES, (off, nelem)
            return arena[:, off // 2: off // 2 + 2 * nelem].bitcast(F32)
        assert off + 2 * nelem <= AR_BYTES, (off, nelem)
        return arena[:, off // 2: off // 2 + nelem]

    KB = 1024
    o_const = 0
    o_h = 8 * KB
    o_xn = o_h + 32 * KB
    o_hid = o_xn + 16 * KB
    o_wr = o_hid + 44 * KB
    NSLOT = 3
    o_sq = o_wr + NSLOT * 8 * KB
    o_tmp = o_sq + 8 * KB
    o_rs = o_tmp + 8 * KB
    o_stg = o_rs + 4 * KB
    o_cv = o_stg + 24 * KB

    c_off = [o_const]

    def calloc(nelem, dt):
        sz = nelem * (4 if dt == F32 else 2)
        sz = (sz + 31) // 32 * 32
        v = view(c_off[0], nelem, dt)
        c_off[0] += sz
        assert c_off[0] <= o_h
        return v

    ones_bf = calloc(128, BF16)
    ident_f = calloc(128, F32)
    ident_bf = calloc(128, BF16)
    tri_f = calloc(128, F32)
    tri_b = calloc(128, F32)
    mask01 = calloc(512, F32)
    maskF = calloc(64, F32)
    maskB = calloc(64, F32)
    smallp = calloc(NSP, F32)
    modT = calloc(2 * 72 * 2, F32).rearrange("p (l c w) -> p l c w", l=2, w=2)
    gsT = calloc(2 * 3 * 8 * 2, F32).rearrange("p (l s f w) -> p l s f w", l=2, s=3, w=2)
    gtT = calloc(2 * 3 * 8 * 2, F32).rearrange("p (l s f w) -> p l s f w", l=2, s=3, w=2)
    cs_t = calloc(16, F32).rearrange("p (k w) -> p k w", w=2)
    lbT = calloc(16, F32).rearrange("p (r f) -> p r f", f=8)
    omlT = calloc(16, F32).rearrange("p (r f) -> p r f", f=8)
    lbtmp = calloc(16, F32).rearrange("p (r f) -> p r f", f=8)
    bC = Buf("const")

    sp_cc = smallp[:, 0:16].rearrange("p (k w) -> p k w", w=2)
    sp_ng = smallp[:, 16:64].rearrange("p (l s f) -> p l s f", l=2, s=3)
    sp_cw = smallp[:, 64:88].rearrange("p (j f) -> p j f", j=3)
    sp_lb = smallp[:, 88:120].rearrange("p (l r f) -> p l r f", l=2, r=2)
    sp_gn = smallp[:, 120:128]
    sp_fg = smallp[:, 128:136]
    sp_ab = smallp[:, 136:280].rearrange("p (l c) -> p l c", l=2)

    Hh = view(o_h, 8 * 1024, F32).rearrange("p (f t) -> p f t", f=8)
    XN = view(o_xn, 8 * 1024, BF16).rearrange("p (f t) -> p f t", f=8)
    HID = view(o_hid, HC * 1024, BF16).rearrange("p (j t) -> p j t", j=HC)
    bH = [[Buf() for _ in range(2)] for _ in range(8)]
    bXN = [[Buf() for _ in range(2)] for _ in range(8)]
    bHID = [[Buf() for _ in range(2)] for _ in range(HC)]
    SQ = view(o_sq, 8 * 512, BF16).rearrange("p (f t) -> p f t", f=8)
    bSQ = [Buf() for _ in range(8)]
    tmpR = Ring([(view(o_tmp + i * 2 * KB, 512, F32), Buf()) for i in range(4)])
    RT = view(o_rs, 512, F32)
    RSTD = view(o_rs + 2 * KB, 512, F32)
    bRT = Buf()
    bRSTD = Buf()
    sq_i = [0]

    psb = [nc.alloc_psum_tensor("ps%d" % i, [128, 512], F32).ap() for i in range(7)]
    psR = Ring([(psb[i], Buf()) for i in range(6)])
    ps_ss = psb[6]
    b_ss = Buf()
    ps_bf = nc.alloc_psum_tensor("psbf", [128, 1024], BF16).ap()
    b_psbf = Buf()

    def sl(ap, a, n):
        return ap[:, a:a + n]

    wslots = [(view(o_wr + i * 8 * KB, 4096, BF16), Buf()) for i in range(NSLOT)]

    class WStream:
        def __init__(self):
            self.plan = []
            self.issued = 0
            self.cur = 0

        def add(self, dram_ap, dkey, kc, w):
            self.plan.append((dram_ap, dkey, kc, w))

        def _issue(self, i):
            dram_ap, dkey, kc, w = self.plan[i]
            slot, sb = wslots[i % NSLOT]
            dst = slot[:, 0:kc * w].rearrange("p (k c) -> p k c", c=w)
            P.op("sp", lambda e, d=dst, s=dram_ap: e.dma_start(out=d, in_=s),
                 reads=[dbuf(dkey)], writes=[sb], dma_key="wr%d" % (i % NSLOT))

        def get(self):
            i = self.cur
            while self.issued < min(len(self.plan), i + NSLOT):
                self._issue(self.issued)
                self.issued += 1
            dram_ap, dkey, kc, w = self.plan[i]
            slot, sb = wslots[i % NSLOT]
            self.cur += 1
            return slot[:, 0:kc * w].rearrange("p (k c) -> p k c", c=w), sb

    P.op("sp", lambda e: e.dma_start(out=smallp, in_=sp_d), writes=[bC], dma=True)
    P.op("pool", lambda e: e.memset(ones_bf, 1.0), writes=[bC])
    P.op("pool", lambda e: e.memset(ident_f, 1.0), writes=[bC])
    P.op("pool", lambda e: e.affine_select(out=ident_f, in_=ident_f, pattern=[[-1, 128]],
                                           compare_op=ALU.is_equal, fill=0.0, base=0, channel_multiplier=1),
         reads=[bC], writes=[bC])
    P.op("pool", lambda e: e.tensor_copy(out=ident_bf, in_=ident_f), reads=[bC], writes=[bC])
    P.op("pool", lambda e: e.memset(tri_f, 1.0), writes=[bC])
    P.op("pool", lambda e: e.affine_select(out=tri_f, in_=tri_f, pattern=[[1, 128]],
                                           compare_op=ALU.is_ge, fill=0.0, base=0, channel_multiplier=-1),
         reads=[bC], writes=[bC])
    P.op("pool", lambda e: e.memset(tri_b, 1.0), writes=[bC])
    P.op("pool", lambda e: e.affine_select(out=tri_b, in_=tri_b, pattern=[[-1, 128]],
                                           compare_op=ALU.is_ge, fill=0.0, base=0, channel_multiplier=1),
         reads=[bC], writes=[bC])
    for hh in range(2):
        P.op("pool", lambda e, hh=hh: e.tensor_copy(out=maskF[hh * 64:(hh + 1) * 64, :],
                                                    in_=tri_f[hh * 64:(hh + 1) * 64, hh * 64:(hh + 1) * 64]),
             reads=[bC], writes=[bC])
        P.op("pool", lambda e, hh=hh: e.tensor_copy(out=maskB[hh * 64:(hh + 1) * 64, :],
                                                    in_=tri_b[hh * 64:(hh + 1) * 64, hh * 64:(hh + 1) * 64]),
             reads=[bC], writes=[bC])
    P.op("pool", lambda e: e.memset(mask01, 1.0), writes=[bC])
    last_ci = P.op("pool", lambda e: e.memset(mask01.rearrange("p (c j) -> p c j", j=64)[:, :, 0:1], 0.0),
                   reads=[bC], writes=[bC])

    cvf = Ring([(view(o_cv + i * 12 * KB, 3072, F32), Buf()) for i in range(2)])
    cvb = Ring([(view(o_cv + 24 * KB + i * 6 * KB, 3072, BF16), Buf()) for i in range(2)])
    cv_n = [0]

    def conv_piece(src2d, kc, segs, dst_ap, dkey):
        i = cv_n[0]
        cv_n[0] += 1
        fv, fb = cvf.next()
        bv, bb = cvb.next()
        wtot = sum(w for _, w in segs)
        f3 = fv[:, 0:kc * wtot].rearrange("p (k c) -> p k c", c=wtot)
        b3 = bv[:, 0:kc * wtot].rearrange("p (k c) -> p k c", c=wtot)
        src3 = src2d.rearrange("(k p) n -> p k n", p=128)
        c = 0
        for si, (c0, w) in enumerate(segs):
            P.op("act" if i < N_ACT_CONV else "pool", lambda e, d=f3[:, :, c:c + w], s=src3[:, :, c0:c0 + w]: e.dma_start(out=d, in_=s),
                 writes=[fb], dma_key="cvl%d_%d" % (i % 2, si))
            c += w
        P.op("pool", lambda e: e.tensor_copy(out=b3, in_=f3), reads=[fb], writes=[bb])
        P.op("pool", lambda e: e.dma_start(out=dst_ap, in_=b3), reads=[bb], writes=[dbuf(dkey)],
             dma_key="cvs%d" % (i % 2))

    def conv_ffn(l, h):
        for j in range(HC):
            conv_piece(wgu_d[l, h], 8, [(j * 128, 128), (FF + j * 128, 128)], WGU[l][h][j], ("WGU", l, h, j))
        for m in range(8):
            conv_piece(wdn_d[l, h], HC, [(m * 128, 128)], WDN[l][h][m], ("WDN", l, h, m))

    def conv_all():
        conv_ffn(0, 0)
        for m in range(8):
            conv_piece(cwin_d, 8, [(m * 128, 128), (D + m * 128, 128), (2 * D + m * 128, 128)], CWIN[m], ("CWIN", m))
        for m in range(8):
            conv_piece(cwout_d, 8, [(m * 128, 128)], CWOUT[m], ("CWOUT", m))
        conv_ffn(0, 1)
        conv_ffn(1, 0)
        for hd in range(8):
            conv_piece(hwin_d, 8, [(hd * 128, 128), (2 * D + hd * 128, 128)], HWIN[hd][:, :, 0:256], ("HWIN", hd))
            conv_piece(hwin_d, 8, [(3 * D + hd * 128, 128), (4 * D + hd * 128, 128)], HWIN[hd][:, :, 256:512], ("HWIN", hd))
        for hf in range(2):
            conv_piece(hwin_d, 8, [(D + hf * 512, 256)], HWIN[8 + hf][:, :, 0:256], ("HWIN", 8 + hf))
            conv_piece(hwin_d, 8, [(D + hf * 512 + 256, 256)], HWIN[8 + hf][:, :, 256:512], ("HWIN", 8 + hf))
        for m in range(8):
            conv_piece(hwout_d, 8, [(m * 128, 128)], HWOUT[m], ("HWOUT", m))
        conv_ffn(1, 1)

    def stage0_pre():
        P.op("act", lambda e: e.activation(out=cs_t, in_=sp_cc, func=AF.Silu), reads=[bC], writes=[bC])
        P.op("dve", lambda e: e.tensor_tensor(out=lbtmp, in0=sp_lb[:, 1], in1=sp_lb[:, 0], op=ALU.subtract),
             reads=[bC], writes=[bC])
        P.op("act", lambda e: e.activation(out=lbT, in_=lbtmp, func=AF.Sigmoid), reads=[bC], writes=[bC])
        P.op("dve", lambda e: e.tensor_scalar(out=omlT, in0=lbT, scalar1=-1.0, scalar2=1.0, op0=ALU.mult, op1=ALU.add),
             reads=[bC], writes=[bC])

    def stage0():
        mT = view(o_h, 9 * D, F32)
        bmT = Buf()
        adaR = Ring([(view(o_hid + i * 16 * KB, 4096, F32).rearrange("p (k c) -> p k c", c=512), Buf())
                     for i in range(2)])
        for l in range(2):
            src3 = adaw_d[l].rearrange("(k p) n -> p k n", p=128)
            for cb in range(18):
                av, ab = adaR.next()
                P.op("sp", lambda e, d=av, s=src3[:, :, cb * 512:(cb + 1) * 512]: e.dma_start(out=d, in_=s),
                     writes=[ab], dma=True)
                pv, pb = psR.next()
                for kc in range(8):
                    P.op("pe", lambda e, o=pv[0:2, :], a=cs_t[:, kc, :], r=av[:, kc, :], kc=kc:
                         e.matmul(o, lhsT=a, rhs=r, start=(kc == 0), stop=(kc == 7)),
                         reads=[ab, bC], writes=[pb])
                P.op("dve", lambda e, o=mT[0:2, cb * 512:(cb + 1) * 512], i=pv[0:2, :]: e.tensor_copy(out=o, in_=i),
                     reads=[pb], writes=[bmT])
            pv, pb = psR.next()
            for ch in range(72):
                P.op("pe", lambda e, o=pv[:, ch * 2:ch * 2 + 2], i=mT[0:2, ch * 128:(ch + 1) * 128]:
                     e.transpose(o, i, ident_f[0:2, 0:2]), reads=[bmT, bC], writes=[pb])
            bias_b = bass.AP(sp_ab.tensor, sp_ab[:, l].offset, [list(sp_ab.ap[0]), [1, 72], [0, 2]])
            P.op("dve", lambda e, o=modT[:, l], i=pv[:, 0:144].rearrange("p (c w) -> p c w", w=2), b=bias_b:
                 e.tensor_tensor(out=o, in0=i, in1=b, op=ALU.add), reads=[pb, bC], writes=[bC])
            for s in range(3):
                g_b = bass.AP(sp_ng.tensor, sp_ng[:, l, s].offset, [list(sp_ng.ap[0]), [1, 8], [0, 2]])
                P.op("dve", lambda e, o=gsT[:, l, s], i=modT[:, l, (3 * s + 1) * 8:(3 * s + 2) * 8], g=g_b:
                     e.scalar_tensor_tensor(out=o, in0=i, scalar=1.0, in1=g, op0=ALU.add, op1=ALU.mult),
                     reads=[bC], writes=[bC])
                P.op("dve", lambda e, o=gtT[:, l, s], i=modT[:, l, (3 * s + 2) * 8:(3 * s + 3) * 8], s=s:
                     e.tensor_scalar(out=o, in0=i, scalar1=(1.0 if s == 1 else 0.5), scalar2=None, op0=ALU.mult),
                     reads=[bC], writes=[bC])

    def m_gs(l, s, fc, w):
        return gsT[:, l, s, fc, w:w + 1]

    def m_sh(l, s, fc, w):
        return modT[:, l, 3 * s * 8 + fc, w:w + 1]

    def m_gt(l, s, fc, w):
        return gtT[:, l, s, fc, w:w + 1]

    rsR = Ring([(view(o_rs + i * 2 * KB, 512, F32), Buf()) for i in range(2)])
    ssacc = [(ps_ss, b_ss), (ps_bf.bitcast(F32), b_psbf)]
    sqacc_i = [0]

    def preacc_square(m, nti, off, n):
        k = sqacc_i[0] % 8
        sqacc_i[0] += 1
        P.op("act", lambda e, o=SQ[:, k, 0:n], s=Hh[:, m, off:off + n]: e.activation(out=o, in_=s, func=AF.Square),
             reads=[bH[m][nti]], writes=[bSQ[k]])
        return (k, m, nti, n)

    def preacc_mm(pend):
        for (k, m, nti, n) in pend:
            av, ab = ssacc[nti]
            P.op("pe", lambda e, o=av[:, 0:n], r=SQ[:, k, 0:n]: e.matmul(o, lhsT=ones_bf, rhs=r, start=(m == 0), stop=(m == 7)),
                 reads=[bSQ[k], bC], writes=[ab])


    def sumsq_rstd(nts, nti, srcs, src_bufs, nfeat, pre=False):
        off, n = nts[nti]
        if pre:
            pv, pb = ssacc[nti]
            rv, rb = rsR.next()
            P.op("act", lambda e: e.activation(out=rv[:, 0:n], in_=pv[:, 0:n], func=AF.Ln, scale=1.0 / nfeat, bias=EPS),
                 reads=[pb], writes=[rb])
            P.op("act", lambda e: e.activation(out=rv[:, 0:n], in_=rv[:, 0:n], func=AF.Exp, scale=-0.5), reads=[rb], writes=[rb])
            return rv, rb
        nsrc = len(srcs)
        sqi = []
        for i in range(nsrc):
            k = sq_i[0] % 8 if nsrc == 1 else i
            sq_i[0] += 1
            sqi.append(k)
            P.op("act", lambda e, o=SQ[:, k, 0:n], s=srcs[i]: e.activation(out=o, in_=s, func=AF.Square),
                 reads=[src_bufs[i]], writes=[bSQ[k]])
        pv, pb = psR.next()
        for i in range(nsrc):
            k = sqi[i]
            P.op("pe", lambda e, o=pv[:, 0:n], r=SQ[:, k, 0:n], i=i:
                 e.matmul(o, lhsT=ones_bf, rhs=r, start=(i == 0), stop=(i == nsrc - 1)),
                 reads=[bSQ[k], bC], writes=[pb])
        rv, rb = rsR.next()
        P.op("act", lambda e: e.activation(out=rv[:, 0:n], in_=pv[:, 0:n], func=AF.Ln, scale=1.0 / nfeat, bias=EPS),
             reads=[pb], writes=[rb])
        P.op("act", lambda e: e.activation(out=rv[:, 0:n], in_=rv[:, 0:n], func=AF.Exp, scale=-0.5), reads=[rb], writes=[rb])
        return rv, rb

    def norm_mod(nts, l, s, w, pre=True):
        for nti, (off, n) in enumerate(nts):
            RSTD, bRSTD = sumsq_rstd(nts, nti, [Hh[:, fc, off:off + n] for fc in range(8)], [bH[fc][nti] for fc in range(8)], D, pre=pre)
            for fc in range(8):
                tv, tb = tmpR.next()
                P.op("dve", lambda e, o=tv[:, 0:n], a=Hh[:, fc, off:off + n]:
                     e.tensor_tensor(out=o, in0=a, in1=RSTD[:, 0:n], op=ALU.mult),
                     reads=[bH[fc][nti], bRSTD], writes=[tb])
                P.op("act", lambda e, o=XN[:, fc, off:off + n], i=tv[:, 0:n], fc=fc:
                     e.activation(out=o, in_=i, func=AF.Identity, scale=m_gs(l, s, fc, w), bias=m_sh(l, s, fc, w)),
                     reads=[tb, bC], writes=[bXN[fc][nti]])

    BG = [None]

    def tick(nmax=1):
        for _ in range(nmax):
            g = BG[0]
            if g is None:
                return
            try:
                next(g)
            except StopIteration:
                BG[0] = None

    def ffn(ws, nts, l, hf, w, pre=True):
        s = 0 if hf == 0 else 2
        norm_mod(nts, l, s, w, pre=pre)
        for j in range(HC):
            wv, wb = ws.get()
            for nti, (off, n) in enumerate(nts):
                pg, bg_ = psR.next()
                pu, bu_ = psR.next()
                for kc in range(8):
                    P.op("pe", lambda e, o=pg[:, 0:n], a=wv[:, kc, 0:128], r=XN[:, kc, off:off + n], kc=kc:
                         e.matmul(o, lhsT=a, rhs=r, start=(kc == 0), stop=(kc == 7)),
                         reads=[wb, bXN[kc][nti]], writes=[bg_])
                for kc in range(8):
                    P.op("pe", lambda e, o=pu[:, 0:n], a=wv[:, kc, 128:256], r=XN[:, kc, off:off + n], kc=kc:
                         e.matmul(o, lhsT=a, rhs=r, start=(kc == 0), stop=(kc == 7)),
                         reads=[wb, bXN[kc][nti]], writes=[bu_])
                tv, tb = tmpR.next()
                P.op("act", lambda e, o=tv[:, 0:n], i=pg[:, 0:n]: e.activation(out=o, in_=i, func=AF.Silu),
                     reads=[bg_], writes=[tb])
                P.op("dve", lambda e, o=HID[:, j, off:off + n], a=tv[:, 0:n], b=pu[:, 0:n]:
                     e.tensor_tensor(out=o, in0=a, in1=b, op=ALU.mult),
                     reads=[tb, bu_], writes=[bHID[j][nti]])
            tick()
        pend = []
        for m in range(8):
            wv, wb = ws.get()
            pend_new = []
            for nti, (off, n) in enumerate(nts):
                py, by_ = psR.next()
                for j in range(HC):
                    P.op("pe", lambda e, o=py[:, 0:n], a=wv[:, j, :], r=HID[:, j, off:off + n], j=j:
                         e.matmul(o, lhsT=a, rhs=r, start=(j == 0), stop=(j == HC - 1)),
                         reads=[wb, bHID[j][nti]], writes=[by_])
                P.op("dve", lambda e, o=Hh[:, m, off:off + n], i=py[:, 0:n], m=m:
                     e.scalar_tensor_tensor(out=o, in0=i, scalar=m_gt(l, s, m, w), in1=o, op0=ALU.mult, op1=ALU.add),
                     reads=[by_, bC, bH[m][nti]], writes=[bH[m][nti]])
                pend_new.append(preacc_square(m, nti, off, n))
            preacc_mm(pend)
            pend = pend_new
            tick()
        preacc_mm(pend)

    def ffn_plan(ws, l, hf):
        for j in range(HC):
            ws.add(WGU[l][hf][j], ("WGU", l, hf, j), 8, 256)
        for m in range(8):
            ws.add(WDN[l][hf][m], ("WDN", l, hf, m), HC, 128)

    def proj_out(ws, nts, l, w, SRC=None, bSRC=None):
        SRC = XN if SRC is None else SRC
        bSRC = bXN if bSRC is None else bSRC
        pend = []
        for m in range(8):
            wv, wb = ws.get()
            pend_new = []
            for nti, (off, n) in enumerate(nts):
                py, by_ = psR.next()
                for kc in range(8):
                    P.op("pe", lambda e, o=py[:, 0:n], a=wv[:, kc, :], r=SRC[:, kc, off:off + n], kc=kc:
                         e.matmul(o, lhsT=a, rhs=r, start=(kc == 0), stop=(kc == 7)),
                         reads=[wb, bSRC[kc][nti]], writes=[by_])
                P.op("dve", lambda e, o=Hh[:, m, off:off + n], i=py[:, 0:n], m=m:
                     e.scalar_tensor_tensor(out=o, in0=i, scalar=m_gt(l, 1, m, w), in1=o, op0=ALU.mult, op1=ALU.add),
                     reads=[by_, bC, bH[m][nti]], writes=[bH[m][nti]])
                pend_new.append(preacc_square(m, nti, off, n))
            preacc_mm(pend)
            pend = pend_new
        preacc_mm(pend)

    def supertiles(with_ctx):
        sts = []
        if with_ctx:
            sts.append((0, 256, [(0, 256)], 1))
        for k in range(8):
            sts.append((LC + k * 1024, 1024, [(0, 512), (512, 512)], 0))
        return sts

    def load_h(src, u0, T, nts):
        P.op("sp", lambda e: e.dma_start(out=Hh[:, :, 0:T], in_=src.rearrange("f p u -> p f u")[:, :, u0:u0 + T]),
             reads=[dbuf((id(src), u0))], writes=[bH[fc][nti] for fc in range(8) for nti in range(len(nts))], dma=True)

    def store_h(dst, u0, T, nts):
        P.op("sp", lambda e: e.dma_start(out=dst.rearrange("f p u -> p f u")[:, :, u0:u0 + T], in_=Hh[:, :, 0:T]),
             reads=[bH[fc][nti] for fc in range(8) for nti in range(len(nts))], writes=[dbuf((id(dst), u0))], dma=True)

    def stage1():
        ws = WStream()
        sts = supertiles(True)
        for st in sts:
            ffn_plan(ws, 0, 0)
            for m in range(8):
                ws.add(CWIN[m], ("CWIN", m), 8, 384)
        XT = view(o_hid, 8 * 1024, F32).rearrange("p (b d) -> p b d", d=1024)
        bXT = Buf()
        stgR = Ring([(view(o_stg + i * 2 * KB, 512, F32), Buf()) for i in range(6)])
        for (u0, T, nts, w) in sts:
            nb = T // 128
            src = ctx_d if w else x_d[u0 - LC:u0 - LC + T]
            P.op("sp", lambda e, s=src, nb=nb: e.dma_start(out=XT[:, 0:nb, :], in_=s.rearrange("(b p) d -> p b d", p=128)),
                 writes=[bXT] + [bHID[j][nti] for j in range(HC) for nti in range(2)], dma=True)
            k = 0
            for nti, (off, n) in enumerate(nts):
                for fc in range(8):
                    pv, pb = psR.next()
                    for bq in range(n // 128):
                        blk = off // 128 + bq
                        P.op("pe", lambda e, o=pv[:, bq * 128:(bq + 1) * 128], i=XT[:, blk, fc * 128:(fc + 1) * 128]:
                             e.transpose(o, i, ident_f), reads=[bXT, bC], writes=[pb])
                    eng = "act" if k % 2 == 0 else "dve"
                    k += 1
                    if eng == "act":
                        P.op("act", lambda e, o=Hh[:, fc, off:off + n], i=pv[:, 0:n]: e.activation(out=o, in_=i, func=AF.Copy),
                             reads=[pb], writes=[bH[fc][nti]])
                    else:
                        P.op("dve", lambda e, o=Hh[:, fc, off:off + n], i=pv[:, 0:n]: e.tensor_copy(out=o, in_=i),
                             reads=[pb], writes=[bH[fc][nti]])
            for j in range(HC):
                for nti in range(2):
                    bHID[j][nti].readers.extend(bXT.readers)
            ffn(ws, nts, 0, 0, w, pre=False)
            store_h(hA, u0, T, nts)
            norm_mod(nts, 0, 1, w)
            for m in range(8):
                wv, wb = ws.get()
                for nti, (off, n) in enumerate(nts):
                    pB, bB = psR.next()
                    pC, bCc = psR.next()
                    pV, bV = psR.next()
                    for (pp, bb, c0) in ((pB, bB, 0), (pC, bCc, 128), (pV, bV, 256)):
                        for kc in range(8):
                            P.op("pe", lambda e, o=pp[:, 0:n], a=wv[:, kc, c0:c0 + 128], r=XN[:, kc, off:off + n], kc=kc:
                                 e.matmul(o, lhsT=a, rhs=r, start=(kc == 0), stop=(kc == 7)),
                                 reads=[wb, bXN[kc][nti]], writes=[bb])
                    tv, tb = tmpR.next()
                    P.op("act", lambda e, o=tv[:, 0:n], i=pC[:, 0:n]: e.activation(out=o, in_=i, func=AF.Copy),
                         reads=[bCc], writes=[tb])
                    zv, zb = stgR.next()
                    P.op("dve", lambda e, o=zv[:, 0:n], a=tv[:, 0:n], b=pV[:, 0:n]: e.tensor_tensor(out=o, in0=a, in1=b, op=ALU.mult),
                         reads=[tb, bV], writes=[zb])
                    P.op("sp", lambda e, d=zA[m, :, u0 + off:u0 + off + n], s=zv[:, 0:n]: e.dma_start(out=d, in_=s),
                         reads=[zb], writes=[dbuf(("zA", m))], dma=True)
                    gv, gb = stgR.next()
                    P.op("act", lambda e, o=gv[:, 0:n], i=pB[:, 0:n]: e.activation(out=o, in_=i, func=AF.Copy),
                         reads=[bB], writes=[gb])
                    P.op("sp", lambda e, d=bgA[m, :, u0 + off:u0 + off + n], s=gv[:, 0:n]: e.dma_start(out=d, in_=s),
                         reads=[gb], writes=[dbuf(("bgA", m))], dma=True)

    def stage2():
        ws = WStream()
        sts = supertiles(True)
        for st in sts:
            for m in range(8):
                ws.add(CWOUT[m], ("CWOUT", m), 8, 128)
            ffn_plan(ws, 0, 1)
            ffn_plan(ws, 1, 0)
            for c in (8, 9, 0, 1, 2, 3, 4, 5, 6, 7):
                ws.add(HWIN[c], ("HWIN", c), 8, 512)
        zinR = Ring([(view(o_stg + i * 4608, 1152, F32), Buf()) for i in range(2)])
        bginR = Ring([(view(o_stg + 9216 + i * 4096, 1024, F32), Buf()) for i in range(2)])
        ZC = view(o_stg + 9216 + 8192, 1024, F32)
        bZC = Buf()
        bZCh = [Buf(), Buf()]
        hoff = [o_hid, o_wr]

        def halloc(nelem, dt):
            v = view(hoff[0], nelem, dt)
            hoff[0] += (nelem * (4 if dt == F32 else 2) + 31) // 32 * 32
            assert hoff[0] <= hoff[1], (hoff, nelem)
            return v

        def mkset(bw):
            d = dict(e=(halloc(512, F32), Buf()), la=(halloc(512, F32), Buf()), lb=(halloc(512, F32), Buf()),
                     g=(halloc(512, F32), Buf()), sk=(halloc(512, BF16), Buf()))
            if bw:
                d["g2"] = (halloc(512, F32), Buf())
            return d
        fsets = [mkset(False), mkset(False)]
        bsets = [mkset(True), mkset(True)]
        s_og = Ring([(halloc(512, F32), Buf()) for _ in range(2)])
        hg_bufs = [ts[k][1] for ts in fsets + bsets for k in ts] + [b for _, b in s_og.items]
        hoff[0], hoff[1] = o_cv, AR_BYTES
        fsets.append(mkset(False))
        bsets.append(mkset(True))
        tqR = Ring([(halloc(512, F32), Buf()) for _ in range(3)])
        sqR = Ring([(halloc(512, BF16), Buf()) for _ in range(3)])
        sktR = Ring([(halloc(512, BF16), Buf()) for _ in range(3)])
        decR = Ring([(halloc(8, F32), Buf()) for _ in range(4)])
        s_v = Ring([(halloc(512, BF16), Buf()) for _ in range(2)])

        for sti, (u0, T, nts, w) in enumerate(sts):
            if sti == 0:
                load_h(hA, u0, T, nts)
            R = 256 if w else 64
            for fc in range(8):
                zv, zb = zinR.next()
                gv, gb = bginR.next()
                lo = u0 - 64
                hi = u0 + T + 64
                if w:
                    lo, hi = u0, u0 + T
                else:
                    if lo < LC:
                        P.op("pool", lambda e, o=zv[:, 0:64]: e.memset(o, 0.0), writes=[zb])
                        lo = u0
                    if hi > UU:
                        P.op("pool", lambda e, o=zv[:, 64 + T:128 + T]: e.memset(o, 0.0), writes=[zb])
                        hi = u0 + T
                P.op("sp", lambda e, d=zv[:, 64 + lo - u0:64 + hi - u0], s=zA[fc, :, lo:hi]: e.dma_start(out=d, in_=s),
                     reads=[dbuf(("zA", fc))], writes=[zb], dma=True)
                P.op("sp", lambda e, d=gv[:, 0:T], s=bgA[fc, :, u0:u0 + T]: e.dma_start(out=d, in_=s),
                     reads=[dbuf(("bgA", fc))], writes=[gb], dma=True)
                cw0, cw1, cw2 = sp_cw[:, 0, fc:fc + 1], sp_cw[:, 1, fc:fc + 1], sp_cw[:, 2, fc:fc + 1]
                hs = [(nti, off, n, bZCh[nti]) for nti, (off, n) in enumerate(nts)]
                for (nti, off, n, bz) in hs:
                    P.op("act", lambda e, i=zv[:, 64 + off:64 + off + n], c=cw1, o=ZC[:, off:off + n]:
                         e.activation(out=o, in_=i, func=AF.Copy, scale=c), reads=[zb, bC], writes=[bz])
                for step in range(2):
                    for (nti, off, n, bz) in hs:
                        cw = cw0 if step == 0 else cw2
                        if w or fc < 4:
                            zc3 = ZC[:, off:off + n].rearrange("p (r c) -> p r c", c=R)
                            zi3 = zv[:, 64 + off:64 + off + n].rearrange("p (r c) -> p r c", c=R)
                            if step == 0:
                                o_, i_ = zc3[:, :, 1:R], zi3[:, :, 0:R - 1]
                            else:
                                o_, i_ = zc3[:, :, 0:R - 1], zi3[:, :, 1:R]
                        else:
                            o_ = ZC[:, off:off + n]
                            i_ = zv[:, off:off + n] if step == 0 else zv[:, 128 + off:128 + off + n]
                        P.op("dve", lambda e, o=o_, i=i_, c=cw:
                             e.scalar_tensor_tensor(out=o, in0=i, scalar=c, in1=o, op0=ALU.mult, op1=ALU.add),
                             reads=[zb, bC, bz], writes=[bz])
                for (nti, off, n, bz) in hs:
                    P.op("pool", lambda e, o=XN[:, fc, off:off + n], a=gv[:, off:off + n], z=ZC[:, off:off + n]:
                         e.tensor_tensor(out=o, in0=a, in1=z, op=ALU.mult), reads=[gb, bz], writes=[bXN[fc][nti]])
            proj_out(ws, nts, 0, w)
            ffn(ws, nts, 0, 1, w)
            ffn(ws, nts, 1, 0, w)
            if not w:
                store_h(hB, u0, T, nts)
            norm_mod(nts, 1, 1, w)
            if sti + 1 < len(sts):
                nu0, nT, nnts, _ = sts[sti + 1]
                load_h(hA, nu0, nT, nnts)
            for hf in range(2):
                wv, wb = ws.get()
                for blk in range(T // 128):
                    nti = (blk * 128) // 512
                    pv, pb = psR.next()
                    for kc in range(8):
                        P.op("pe", lambda e, o=pv, a=XN[:, kc, blk * 128:(blk + 1) * 128], r=wv[:, kc, :], kc=kc:
                             e.matmul(o, lhsT=a, rhs=r, start=(kc == 0), stop=(kc == 7)),
                             reads=[wb, bXN[kc][nti]], writes=[pb])
                    sv, bsv = s_v.next()
                    P.op("act", lambda e, sv=sv, pv=pv: e.activation(out=sv, in_=pv, func=AF.Copy), reads=[pb], writes=[bsv])
                    ub = u0 + blk * 128
                    for dr in range(2):
                        P.op("sp", lambda e, d=KVT[hf * 4:hf * 4 + 4, dr, ub:ub + 128, 1, :].rearrange("h t d -> t h d"),
                             s=sv.rearrange("p (h d) -> p h d", d=128): e.dma_start(out=d, in_=s),
                             reads=[bsv], writes=[dbuf(("KVT", hf * 4 + q, dr)) for q in range(4)], dma=True)
            hid_all = [bHID[j][nti] for j in range(HC) for nti in range(2)]
            P.op("dve", lambda e: e.engine_nop(), reads=[], writes=hid_all + hg_bufs)
            units = []
            for hd in range(8):
                for nti, (off, n) in enumerate(nts):
                    for dr in range(2):
                        ui = len(units)
                        units.append(dict(hd=hd, nti=nti, off=off, n=n, uo=u0 + off, dr=dr,
                                          ts=(fsets if dr == 0 else bsets)[(ui // 2) % 3]))
            wcur = [None]

            def ustep(k, U):
                hd, nti, off, n, uo, dr, ts = U["hd"], U["nti"], U["off"], U["n"], U["uo"], U["dr"], U["ts"]
                ncn = n // 64
                E, bE = ts["e"]
                LA, bLA = ts["la"]
                LB, bLB = ts["lb"]
                G, bG = ts["g"]
                sk_, bsk = ts["sk"]
                GG, bGG = (G, bG) if dr == 0 else ts["g2"]
                if k == 0:
                    if nti == 0 and dr == 0:
                        wcur[0] = ws.get()
                    wv, wb = wcur[0]

                    def mm8(c0):
                        pv, pb = psR.next()
                        for kc in range(8):
                            P.op("pe", lambda e, o=pv[:, 0:n], a=wv[:, kc, c0:c0 + 128], r=XN[:, kc, off:off + n], kc=kc:
                                 e.matmul(o, lhsT=a, rhs=r, start=(kc == 0), stop=(kc == 7)),
                                 reads=[wb, bXN[kc][nti]], writes=[pb])
                        return pv, pb
                    if dr == 0:
                        pq, bq_ = mm8(0)
                        tq, btq = tqR.next()
                        U["tq"] = (tq, btq)
                        P.op("act", lambda e: e.activation(out=tq[:, 0:n], in_=pq[:, 0:n], func=AF.Silu), reads=[bq_], writes=[btq])
                        if not w:
                            po, bo = mm8(384)
                            ov, bov = s_og.next()
                            P.op("act", lambda e: e.activation(out=ov[:, 0:n], in_=po[:, 0:n], func=AF.Silu), reads=[bo], writes=[bov])
                            P.op("sp", lambda e, d=SOG[hd, :, uo - LC:uo - LC + n]: e.dma_start(out=d, in_=ov[:, 0:n]),
                                 reads=[bov], writes=[dbuf(("SOG", hd))], dma=True)
                    else:
                        U["tq"] = units[U["ui"] - 1]["tq"]
                    px, bx = mm8(128 * (1 + dr))
                    P.op("act", lambda e: e.activation(out=E[:, 0:n], in_=px[:, 0:n], func=AF.Exp, scale=-1.0), reads=[bx], writes=[bE])
                elif k == 1:
                    P.op("act", lambda e: e.activation(out=LA[:, 0:n], in_=E[:, 0:n], func=AF.Ln, scale=lbT[:, dr, hd:hd + 1], bias=1.0),
                         reads=[bE, bC], writes=[bLA])
                    P.op("act", lambda e: e.activation(out=LB[:, 0:n], in_=E[:, 0:n], func=AF.Ln, bias=1.0), reads=[bE], writes=[bLB])
                elif k == 2:
                    P.op("pool", lambda e: e.tensor_tensor(out=LA[:, 0:n], in0=LA[:, 0:n], in1=LB[:, 0:n], op=ALU.subtract),
                         reads=[bLA, bLB], writes=[bLA])
                    P.op("dve", lambda e: e.tensor_tensor_scan(out=G[:, 0:n], data0=mask01[:, 0:n], data1=LA[:, 0:n],
                                                               initial=0.0, op0=ALU.mult, op1=ALU.add),
                         reads=[bLA, bC], writes=[bG])
                elif k == 3:
                    if dr == 1:
                        P.op("pool", lambda e: e.tensor_tensor(out=LA[:, 0:n], in0=LA[:, 0:n], in1=G[:, 0:n], op=ALU.subtract),
                             reads=[bLA, bG], writes=[bLA])
                        last = bass.AP(G.tensor, G.offset + 63, [list(G.ap[0]), [64, ncn], [0, 64]])
                        P.op("dve", lambda e: e.tensor_tensor(out=GG[:, 0:n].rearrange("p (c j) -> p c j", j=64),
                                                               in0=LA[:, 0:n].rearrange("p (c j) -> p c j", j=64), in1=last, op=ALU.add),
                             reads=[bLA, bG], writes=[bGG])
                    P.op("pool", lambda e: e.tensor_tensor(out=LB[:, 0:n], in0=LB[:, 0:n], in1=GG[:, 0:n], op=ALU.add),
                         reads=[bLB, bGG], writes=[bLB])
                elif k == 4:
                    P.op("act", lambda e: e.activation(out=LA[:, 0:n], in_=GG[:, 0:n], func=AF.Exp), reads=[bGG], writes=[bLA])
                    P.op("act", lambda e: e.activation(out=LB[:, 0:n], in_=LB[:, 0:n], func=AF.Exp, scale=-1.0), reads=[bLB], writes=[bLB])
                elif k == 5:
                    tq, btq = U["tq"]
                    sq_, bsq = sqR.next()
                    P.op("dve", lambda e: e.tensor_tensor(out=sq_[:, 0:n], in0=tq[:, 0:n], in1=LA[:, 0:n], op=ALU.mult),
                         reads=[btq, bLA], writes=[bsq])
                    P.op("sp", lambda e, d=QK[hd, dr, :, 0, uo:uo + n]: e.dma_start(out=d, in_=sq_[:, 0:n]),
                         reads=[bsq], writes=[dbuf(("QK", hd, dr))], dma=True)
                    P.op("dve", lambda e: e.scalar_tensor_tensor(out=sk_[:, 0:n], in0=E[:, 0:n], scalar=omlT[:, dr, hd:hd + 1],
                                                                 in1=LB[:, 0:n], op0=ALU.mult, op1=ALU.mult),
                         reads=[bE, bLB, bC], writes=[bsk])
                    P.op("sp", lambda e, d=QK[hd, dr, :, 1, uo:uo + n]: e.dma_start(out=d, in_=sk_[:, 0:n]),
                         reads=[bsk], writes=[dbuf(("QK", hd, dr))], dma=True)
                    dv, bdv = decR.next()
                    pos = 63 if dr == 0 else 0
                    dsrc = bass.AP(LA.tensor, LA.offset + pos, [list(LA.ap[0]), [64, ncn]])
                    P.op("pool", lambda e: e.tensor_copy(out=dv[:, 0:ncn], in_=dsrc), reads=[bLA], writes=[bdv])
                    P.op("sp", lambda e, d=DEC[hd, dr, :, uo // 64:uo // 64 + ncn]: e.dma_start(out=d, in_=dv[:, 0:ncn]),
                         reads=[bdv], writes=[dbuf(("DEC", hd, dr))], dma=True)
                elif k == 6:
                    pv, pb = psR.next()
                    pvb = pv[:, 0:256].bitcast(BF16)
                    for bq in range(n // 128):
                        P.op("pe", lambda e, o=pvb[:, bq * 128:(bq + 1) * 128], i=sk_[:, bq * 128:(bq + 1) * 128]:
                             e.transpose(o, i, ident_bf), reads=[bsk, bC], writes=[pb])
                    skt, bskt = sktR.next()
                    P.op("dve", lambda e: e.tensor_copy(out=skt[:, 0:n], in_=pvb[:, 0:n]), reads=[pb], writes=[bskt])
                    P.op("sp", lambda e, d=KVT[hd, dr, uo:uo + n, 0, :].rearrange("(b t) d -> t b d", t=128):
                         e.dma_start(out=d, in_=skt[:, 0:n].rearrange("p (b d) -> p b d", d=128)),
                         reads=[bskt], writes=[dbuf(("KVT", hd, dr))], dma=True)
            for ui, U in enumerate(units):
                U["ui"] = ui
            NU = len(units)
            for tick in range(NU + 6):
                for k in range(6, -1, -1):
                    ui = tick - k
                    if 0 <= ui < NU:
                        ustep(k, units[ui])
            P.op("dve", lambda e: e.engine_nop(), reads=[], writes=hid_all + hg_bufs)

    def stage3():
        NG = DBG.get('NG', UU // 256)
        per = 16 * KB
        for hg in range(DBG.get('HG', 2)):
            chains = []
            for q in range(4):
                for dr in range(2):
                    ci = q * 2 + dr
                    base = o_h + ci * per
                    off = [base]

                    def al(nelem, dt):
                        v = view(off[0], nelem, dt)
                        off[0] += (nelem * (4 if dt == F32 else 2) + 31) // 32 * 32
                        assert off[0] <= base + per
                        return v
                    ch = dict(hd=hg * 4 + q, dr=dr,
                              qk=[(al(512, BF16), Buf()) for _ in range(2)],
                              kv=[(al(512, BF16), Buf()) for _ in range(2)],
                              attm=[(al(128, BF16), Buf()) for _ in range(2)],
                              Pm=[(al(128, F32), Buf()) for _ in range(2)],
                              Sb=[(al(128, BF16), Buf()) for _ in range(2)],
                              ost=[(al(256, F32), [Buf() for _ in range(4)]) for _ in range(2)],
                              dec=(al(132, F32), Buf()), step=0)
                    chains.append(ch)
            for ch in chains:
                hd, dr = ch["hd"], ch["dr"]
                P.op("sp", lambda e, d=ch["dec"][0], s=DEC[hd, dr]: e.dma_start(out=d, in_=s),
                     reads=[dbuf(("DEC", hd, dr))], writes=[ch["dec"][1]], dma=True)
                P.op("pool", lambda e, o=ch["Pm"][1][0]: e.memset(o, 0.0), writes=[ch["Pm"][1][1]])
                P.op("pool", lambda e, o=ch["Sb"][1][0]: e.memset(o, 0.0), writes=[ch["Sb"][1][1]])

            def gidx(ch, gi):
                if ch["dr"] == 0 or gi == 0:
                    return gi
                return NG - gi

            def issue_loads(ch, gi):
                g = gidx(ch, gi)
                hd, dr = ch["hd"], ch["dr"]
                qv, qb = ch["qk"][gi % 2]
                kv, kb = ch["kv"][gi % 2]
                P.op("sp", lambda e, d=qv.rearrange("p (a t) -> p a t", a=2), s=QK[hd, dr, :, :, g * 256:(g + 1) * 256]:
                     e.dma_start(out=d, in_=s), reads=[dbuf(("QK", hd, dr))], writes=[qb], dma=True)
                P.op("sp", lambda e, d=kv.rearrange("p (b a d) -> p b a d", b=2, a=2),
                     s=KVT[hd, dr, g * 256:(g + 1) * 256].rearrange("(b t) a d -> t b a d", t=128):
                     e.dma_start(out=d, in_=s), reads=[dbuf(("KVT", hd, dr))], writes=[kb], dma=True)

            for ch in chains:
                issue_loads(ch, 0)
            pbk = [Buf() for _ in range(7)]
            psA = Ring([(psb[0][:, 0:128], pbk[0])])
            psO = Ring([(psb[1 + i][:, 0:64], pbk[1 + i]) for i in range(3)])
            psU = Ring([(psb[4 + i][:, 0:128], pbk[4 + i]) for i in range(3)])
            kk = 0
            for gi in range(NG):
                for ch in chains:
                    if gi + 1 < NG:
                        issue_loads(ch, gi + 1)
                    g = gidx(ch, gi)
                    hd, dr = ch["hd"], ch["dr"]
                    qv, qb = ch["qk"][gi % 2]
                    kv, kb = ch["kv"][gi % 2]
                    q2 = qv.rearrange("p (a t) -> p a t", a=2)
                    kv4 = kv.rearrange("p (b a d) -> p b a d", b=2, a=2)
                    latent = g >= 1
                    ch["cur_attm"] = None
                    if latent:
                        av, ab = psA.next()
                        for i in range(4):
                            b_, h_ = i // 2, i % 2
                            P.op("pe", lambda e, o=av[h_ * 64:(h_ + 1) * 64, b_ * 64:(b_ + 1) * 64],
                                 a=q2[:, 1, i * 64:(i + 1) * 64], r=q2[:, 0, i * 64:(i + 1) * 64]:
                                 e.matmul(o, lhsT=a, rhs=r, start=True, stop=True), reads=[qb], writes=[ab])
                        mv, mb = ch["attm"][gi % 2]
                        msk = maskF if dr == 0 else maskB
                        mb3 = bass.AP(msk.tensor, msk.offset, [list(msk.ap[0]), [0, 2], [1, 64]])
                        P.op("dve", lambda e, o=mv.rearrange("p (b j) -> p b j", j=64), i=av.rearrange("p (b j) -> p b j", j=64), m=mb3:
                             e.tensor_tensor(out=o, in0=i, in1=m, op=ALU.mult), reads=[ab, bC], writes=[mb])
                        ch["cur_attm"] = (mv, mb)
                        ch["cur_ost"] = ch["ost"][gi % 2]
                for ii in range(4):
                    for ch in chains:
                        g = gidx(ch, gi)
                        hd, dr = ch["hd"], ch["dr"]
                        i = ii if dr == 0 else 3 - ii
                        b_, h_ = i // 2, i % 2
                        c_glob = g * 4 + i
                        qv, qb = ch["qk"][gi % 2]
                        kv, kb = ch["kv"][gi % 2]
                        q2 = qv.rearrange("p (a t) -> p a t", a=2)
                        kv4 = kv.rearrange("p (b a d) -> p b a d", b=2, a=2)
                        st_ = ch["step"]
                        Pold, bPold = ch["Pm"][(st_ + 1) % 2]
                        Pnew, bPnew = ch["Pm"][st_ % 2]
                        Sold, bSold = ch["Sb"][(st_ + 1) % 2]
                        Snew, bSnew = ch["Sb"][st_ % 2]
                        decv, decb = ch["dec"]
                        ktm = kv4[h_ * 64:(h_ + 1) * 64, b_, 0, :]
                        vtm = kv4[h_ * 64:(h_ + 1) * 64, b_, 1, :]
                        if g >= 1:
                            mv, mb = ch["cur_attm"]
                            ov, ob = psO.next()
                            P.op("pe", lambda e, o=ov, a=vtm, r=mv[h_ * 64:(h_ + 1) * 64, b_ * 64:(b_ + 1) * 64]:
                                 e.matmul(o, lhsT=a, rhs=r, start=True, stop=False), reads=[kb, mb], writes=[ob])
                            P.op("pe", lambda e, o=ov, a=Sold, r=q2[:, 0, i * 64:(i + 1) * 64]:
                                 e.matmul(o, lhsT=a, rhs=r, start=False, stop=True), reads=[bSold, qb], writes=[ob])
                            osv, osb = ch["cur_ost"]
                            kk += 1
                            if kk % 2 == 0:
                                P.op("act", lambda e, o=osv[:, i * 64:(i + 1) * 64], s=ov: e.activation(out=o, in_=s, func=AF.Copy),
                                     reads=[ob], writes=[osb[i]])
                            else:
                                P.op("dve", lambda e, o=osv[:, i * 64:(i + 1) * 64], s=ov: e.tensor_copy(out=o, in_=s),
                                     reads=[ob], writes=[osb[i]])
                        uv, ub = psU.next()
                        P.op("pe", lambda e, o=uv, a=ktm, r=vtm: e.matmul(o, lhsT=a, rhs=r, start=True, stop=True),
                             reads=[kb], writes=[ub])
                        if st_ == 0:
                            P.op("dve", lambda e, o=Pnew, s=uv: e.tensor_copy(out=o, in_=s), reads=[ub], writes=[bPnew])
                        else:
                            pc = ch["prev_c"]
                            P.op("dve", lambda e, o=Pnew, a=Pold, s=uv, d=decv[:, pc:pc + 1]:
                                 e.scalar_tensor_tensor(out=o, in0=a, scalar=d, in1=s, op0=ALU.mult, op1=ALU.add),
                                 reads=[bPold, ub, decb], writes=[bPnew])
                        P.op("act", lambda e, o=Snew, a=Pnew, d=decv[:, c_glob:c_glob + 1]:
                             e.activation(out=o, in_=a, func=AF.Copy, scale=d), reads=[bPnew, decb], writes=[bSnew])
                        ch["prev_c"] = c_glob
                        ch["step"] = st_ + 1
                for ch in chains:
                    g = gidx(ch, gi)
                    if g >= 1:
                        hd, dr = ch["hd"], ch["dr"]
                        osv, osb = ch["cur_ost"]
                        dst = (OFW if dr == 0 else OBW)[hd, :, (g - 1) * 256:g * 256]
                        P.op("sp", lambda e, d=dst, s=osv: e.dma_start(out=d, in_=s), reads=osb,
                             writes=[dbuf(("O", dr, hd))], dma=True)

    def stage4():
        ws = WStream()
        sts = supertiles(False)
        for st in sts:
            for m in range(8):
                ws.add(HWOUT[m], ("HWOUT", m), 8, 128)
            ffn_plan(ws, 1, 1)
        inR = Ring([tuple((view(o_stg + (i * 3 + k) * 4 * KB, 1024, F32), Buf()) for k in range(3)) for i in range(2)])
        OST = view(o_hid, 4 * 1024, F32).rearrange("p (b d) -> p b d", d=1024)
        bOST = [Buf() for _ in range(8)]
        ONs = [view(o_cv + i * 16 * KB, 8 * 1024, BF16).rearrange("p (f t) -> p f t", f=8) for i in range(2)]
        bONs = [[[Buf() for _ in range(2)] for _ in range(8)] for _ in range(2)]

        rs4 = Ring([(view(o_rs + i * 2 * KB, 512, F32), Buf()) for i in range(2)] +
                   [(view(o_cv + 32 * KB + i * 2 * KB, 512, F32), Buf()) for i in range(2)])

        def pro(ki):
            (u0, T, nts, w) = sts[ki]
            t0 = u0 - LC
            ON, bON = ONs[ki % 2], bONs[ki % 2]
            for fp in range(4):
                items = []
                for fc in (2 * fp, 2 * fp + 1):
                    (fv, fb), (bv, bb), (gv, gb) = inR.next()
                    P.op("sp", lambda e, d=fv, s=OFW[fc, :, t0:t0 + T]: e.dma_start(out=d, in_=s),
                         reads=[dbuf(("O", 0, fc))], writes=[fb], dma=True)
                    P.op("sp", lambda e, d=bv, s=OBW[fc, :, t0:t0 + T]: e.dma_start(out=d, in_=s),
                         reads=[dbuf(("O", 1, fc))], writes=[bb], dma=True)
                    P.op("sp", lambda e, d=gv, s=SOG[fc, :, t0:t0 + T]: e.dma_start(out=d, in_=s),
                         reads=[dbuf(("SOG", fc))], writes=[gb], dma=True)
                    P.op("pool", lambda e, fv=fv, bv=bv: e.tensor_tensor(out=fv, in0=fv, in1=bv, op=ALU.add),
                         reads=[fb, bb], writes=[fb])
                    for nti, (off, n) in enumerate(nts):
                        items.append(dict(fc=fc, nti=nti, off=off, n=n, fv=fv, fb=fb, gv=gv, gb=gb))
                yield
                yield
                for it in items:
                    k = sq_i[0] % 8
                    sq_i[0] += 1
                    it["k"] = k
                    P.op("act", lambda e, o=SQ[:, k, 0:it["n"]], s=it["fv"][:, it["off"]:it["off"] + it["n"]]:
                         e.activation(out=o, in_=s, func=AF.Square), reads=[it["fb"]], writes=[bSQ[k]])
                yield
                yield
                for it in items:
                    it["ps"] = psR.next()
                    P.op("pe", lambda e, o=it["ps"][0][:, 0:it["n"]], r=SQ[:, it["k"], 0:it["n"]]:
                         e.matmul(o, lhsT=ones_bf, rhs=r, start=True, stop=True), reads=[bSQ[it["k"]], bC], writes=[it["ps"][1]])
                for it in items:
                    it["rs"] = rs4.next()
                    P.op("act", lambda e, o=it["rs"][0][:, 0:it["n"]], i=it["ps"][0][:, 0:it["n"]]:
                         e.activation(out=o, in_=i, func=AF.Ln, scale=1.0 / 128, bias=EPS), reads=[it["ps"][1]], writes=[it["rs"][1]])
                for it in items:
                    P.op("act", lambda e, o=it["rs"][0][:, 0:it["n"]]: e.activation(out=o, in_=o, func=AF.Exp, scale=-0.5),
                         reads=[it["rs"][1]], writes=[it["rs"][1]])
                yield
                for it in items:
                    fc, nti, off, n = it["fc"], it["nti"], it["off"], it["n"]
                    tv, tb = tmpR.next()
                    P.op("dve", lambda e, o=tv[:, 0:n], a=it["fv"][:, off:off + n], r=it["rs"][0][:, 0:n]:
                         e.tensor_tensor(out=o, in0=a, in1=r, op=ALU.mult), reads=[it["fb"], it["rs"][1]], writes=[tb])
                    P.op("dve", lambda e, o=ON[:, fc, off:off + n], a=tv[:, 0:n], g=it["gv"][:, off:off + n], fc=fc:
                         e.scalar_tensor_tensor(out=o, in0=a, scalar=sp_gn[:, fc:fc + 1], in1=g, op0=ALU.mult, op1=ALU.mult),
                         reads=[tb, it["gb"], bC], writes=[bON[fc][nti]])
                yield

        for _ in pro(0):
            pass
        for ki, (u0, T, nts, w) in enumerate(sts):
            t0 = u0 - LC
            load_h(hB, u0, T, nts)
            BG[0] = pro(ki + 1) if ki + 1 < len(sts) else None
            proj_out(ws, nts, 1, 0, ONs[ki % 2], bONs[ki % 2])
            ffn(ws, nts, 1, 1, 0)
            tick(100)
            for nti, (off, n) in enumerate(nts):
                RSTD, bRSTD = sumsq_rstd(nts, nti, [Hh[:, fc, off:off + n] for fc in range(8)], [bH[fc][nti] for fc in range(8)], D, pre=True)
                for fc in range(8):
                    P.op("dve", lambda e, o=Hh[:, fc, off:off + n], fc=fc:
                         e.scalar_tensor_tensor(out=o, in0=o, scalar=sp_fg[:, fc:fc + 1], in1=RSTD[:, 0:n], op0=ALU.mult, op1=ALU.mult),
                         reads=[bH[fc][nti], bRSTD, bC], writes=[bH[fc][nti]])
                hid_all = [bHID[j][q] for j in range(HC) for q in range(2)]
                k = 0
                for blk in range(4):
                    for half in range(2):
                        pv, pb = psR.next()
                        for q in range(4):
                            fc = half * 4 + q
                            P.op("pe", lambda e, o=pv[:, q * 128:(q + 1) * 128], i=Hh[:, fc, off + blk * 128:off + (blk + 1) * 128]:
                                 e.transpose(o, i, ident_f), reads=[bH[fc][nti], bC], writes=[pb])
                        k += 1
                        wr = [bOST[blk * 2 + half]] + (hid_all if (blk == 0 and half == 0) else [])
                        if k % 2 == 0:
                            P.op("act", lambda e, o=OST[:, blk, half * 512:(half + 1) * 512], i=pv: e.activation(out=o, in_=i, func=AF.Copy),
                                 reads=[pb], writes=wr)
                        else:
                            P.op("dve", lambda e, o=OST[:, blk, half * 512:(half + 1) * 512], i=pv: e.tensor_copy(out=o, in_=i),
                                 reads=[pb], writes=wr)
                a0 = t0 + off
                P.op("sp", lambda e, d=out_d[a0:a0 + 512, :].rearrange("(b p) d -> p b d", p=128): e.dma_start(out=d, in_=OST),
                     reads=bOST, writes=[dbuf("out")] + hid_all, dma_key="out")

    if not DBG.get("NOCONV"):
        stage0_pre()
        conv_all()
        stage0()
    P.barrier(skip_pool=True, extra=[last_ci])
    if 1 in stages:
        stage1()
        P.barrier()
    if 2 in stages:
        stage2()
        P.barrier()
    if 3 in stages:
        stage3()
        P.barrier()
    if 4 in stages:
        stage4()
    fk = [k for k in P.dma_keys if k.startswith("auto_sp") or k == "out"]
    P.emit(final_wait_keys=fk)
    return nc


def _fm(v):
    return np.ascontiguousarray(np.asarray(v, np.float32).reshape(8, 128).T)


def make_inputs(b, x, c, ctx, c_ctx, ada_w, ada_b, norm_g, ffn_w_gu, ffn_w_down, conv_w_in, conv_w,
                conv_w_out, hg_w_in, hg_lb_logits, hg_gnorm_g, hg_w_out, final_norm_g):
    sp = np.zeros((128, NSP), np.float32)
    cc = np.stack([_fm(c[b]), _fm(c_ctx)], axis=-1)
    sp[:, 0:16] = cc.reshape(128, 16)
    ng = np.stack([np.stack([_fm(norm_g[l, s]) for s in range(3)], 1) for l in range(2)], 1)
    sp[:, 16:64] = ng.reshape(128, 48)
    cw = np.stack([_fm(conv_w[0, j]) for j in range(3)], 1)
    sp[:, 64:88] = cw.reshape(128, 24)
    lb = np.stack([np.stack([_fm(hg_lb_logits[l, r]) for r in range(2)], 1) for l in range(2)], 1)
    sp[:, 88:120] = lb.reshape(128, 32)
    sp[:, 120:128] = _fm(hg_gnorm_g[0])
    sp[:, 128:136] = _fm(final_norm_g)
    ab = np.stack([np.ascontiguousarray(np.asarray(ada_b[l], np.float32).reshape(72, 128).T) for l in range(2)], 1)
    sp[:, 136:280] = ab.reshape(128, 144)
    return {
        "x": np.ascontiguousarray(x[b]), "ctx": np.ascontiguousarray(ctx[b]), "smallp": sp,
        "ada_w": ada_w, "ffn_w_gu": ffn_w_gu, "ffn_w_down": ffn_w_down,
        "conv_w_in": np.ascontiguousarray(conv_w_in[0]), "conv_w_out": np.ascontiguousarray(conv_w_out[0]),
        "hg_w_in": np.ascontiguousarray(hg_w_in[0]), "hg_w_out": np.ascontiguousarray(hg_w_out[0]),
    }


def kernel(**inputs):
    inputs = {k: np.asarray(v) for k, v in inputs.items()}
    nc = build_program()
    in_maps = [make_inputs(b, **inputs) for b in range(8)]
    res = run_bass_kernel_spmd(nc, in_maps, core_ids=list(range(8)))
    return np.stack([np.asarray(r["out"], np.float32) for r in res.results], axis=0)
```

```python
import types
import numpy as np
import concourse.bass as bass
import concourse.mybir as mybir
from concourse.bass_utils import run_bass_kernel_spmd

F32 = mybir.dt.float32
BF16 = mybir.dt.bfloat16
AF = mybir.ActivationFunctionType
ALU = mybir.AluOpType

D = 1024
FC = 8
FF = 2816
HC = 22
S = 8192
LC = 256
UU = S + LC
EPS = 1e-6
NSP = 16 + 48 + 24 + 32 + 8 + 8 + 144
ENGS = ("pe", "act", "dve", "pool", "sp")
DBG = {}
N_ACT_CONV = 38


def _freeze(fn):
    if fn.__closure__ is None:
        return fn
    cells = []
    for c in fn.__closure__:
        try:
            cells.append(types.CellType(c.cell_contents))
        except ValueError:
            cells.append(c)
    return types.FunctionType(fn.__code__, fn.__globals__, fn.__name__, fn.__defaults__, tuple(cells))


class Buf:
    __slots__ = ("name", "last_w", "readers")

    def __init__(self, name=""):
        self.name = name
        self.last_w = None
        self.readers = []


class Op:
    __slots__ = ("eng", "fn", "idx", "deps", "signal", "dma_key", "dma_cnt")


class Prog:
    def __init__(self, nc, n_auto=24):
        self.nc = nc
        self.ops = {e: [] for e in ENGS}
        self.order = []
        self.dma_keys = {}
        self.dma_last = {}
        self.pending = {e: [] for e in ENGS}
        self.n_auto = n_auto
        self.auto_i = {e: 0 for e in ENGS}

    def op(self, eng, fn, reads=(), writes=(), dma=False, dma_key=None):
        o = Op()
        o.eng = eng
        o.fn = _freeze(fn)
        o.idx = len(self.ops[eng])
        o.signal = False
        if dma and dma_key is None:
            dma_key = "auto_%s_%d" % (eng, self.auto_i[eng] % self.n_auto)
            self.auto_i[eng] += 1
        o.dma_key = dma_key
        o.dma_cnt = 0
        deps = list(self.pending[eng])
        self.pending[eng] = []
        for b in reads:
            if b.last_w is not None:
                deps.append(b.last_w)
        for b in writes:
            if b.last_w is not None:
                deps.append(b.last_w)
            deps.extend(b.readers)
        if dma_key is not None:
            prev = self.dma_last.get(dma_key)
            if prev is not None:
                deps.append(prev)
            self.dma_last[dma_key] = o
            c = self.dma_keys.get(dma_key, 0) + 1
            self.dma_keys[dma_key] = c
            o.dma_cnt = c
        red = {}
        for d in deps:
            if d is o:
                continue
            if eng == "pe" and d.eng == "pe" and d.dma_key is None:
                continue
            if d.dma_key is not None:
                k = ("dma", d.dma_key)
                v = d.dma_cnt
            else:
                k = ("eng", d.eng)
                v = d.idx
            if k not in red or red[k][0] < v:
                red[k] = (v, d)
        o.deps = red
        for b in writes:
            b.last_w = o
            b.readers = []
        for b in reads:
            if b not in writes:
                b.readers.append(o)
                if len(b.readers) > 48:
                    keep = {}
                    for r in b.readers:
                        k = ("dma", r.dma_key) if r.dma_key is not None else ("eng", r.eng)
                        keep[k] = r
                    b.readers = list(keep.values())
        self.ops[eng].append(o)
        self.order.append(o)
        return o

    def barrier(self, skip_pool=False, extra=()):
        lasts = list(extra)
        for e in ENGS:
            if skip_pool and e == "pool":
                continue
            for o in reversed(self.ops[e]):
                if o.dma_key is None:
                    lasts.append(o)
                    break
        for k, o in self.dma_last.items():
            if skip_pool and k.startswith("cv"):
                continue
            lasts.append(o)
        for e in ENGS:
            if skip_pool and e == "pool":
                continue
            self.pending[e].extend(lasts)

    def emit(self, final_wait_keys=()):
        nc = self.nc
        for o in self.order:
            for (k, (v, d)) in o.deps.items():
                if k[0] == "eng":
                    d.signal = True
        esem = {e: nc.alloc_semaphore(name="sem_%s" % e) for e in ENGS}
        dsem = {k: nc.alloc_semaphore(name="dsem_%d" % i) for i, k in enumerate(self.dma_keys)}
        cnts = {}
        for e in ENGS:
            c = 0
            arr = []
            for o in self.ops[e]:
                if o.signal and o.dma_key is None:
                    c += 1
                arr.append(c)
            cnts[e] = arr

        def run_engine(e, engobj):
            waited = {}
            for o in self.ops[e]:
                for (k, (v, d)) in o.deps.items():
                    if k[0] == "eng":
                        sem = esem[k[1]]
                        val = cnts[k[1]][v]
                    else:
                        sem = dsem[k[1]]
                        val = 16 * v
                    if waited.get(k, 0) >= val:
                        continue
                    engobj.wait_ge(sem, val)
                    waited[k] = val
                ins = o.fn(engobj)
                if o.dma_key is not None:
                    ins.then_inc(dsem[o.dma_key], 16)
                elif o.signal:
                    ins.then_inc(esem[e], 1)
            if e == "sp":
                for k in final_wait_keys:
                    engobj.wait_ge(dsem[k], 16 * self.dma_keys[k])

        with nc.Block() as block:
            @block.tensor
            def _(eng):
                run_engine("pe", eng)

            @block.scalar
            def _(eng):
                run_engine("act", eng)

            @block.vector
            def _(eng):
                run_engine("dve", eng)

            @block.gpsimd
            def _(eng):
                run_engine("pool", eng)

            @block.sync
            def _(eng):
                run_engine("sp", eng)


class Ring:
    def __init__(self, items):
        self.items = items
        self.i = 0

    def next(self):
        it = self.items[self.i % len(self.items)]
        self.i += 1
        return it


def build_program(debug=(), stages=(0, 1, 2, 3, 4)):
    nc = bass.Bass("TRN2", target_bir_lowering=False)
    P = Prog(nc)

    def din(name, shape):
        return nc.dram_tensor(name, shape, F32, kind="ExternalInput").ap()

    x_d = din("x", [S, D])
    ctx_d = din("ctx", [LC, D])
    sp_d = din("smallp", [128, NSP])
    adaw_d = din("ada_w", [2, D, 9 * D])
    wgu_d = din("ffn_w_gu", [2, 2, D, 2 * FF])
    wdn_d = din("ffn_w_down", [2, 2, FF, D])
    cwin_d = din("conv_w_in", [D, 3 * D])
    cwout_d = din("conv_w_out", [D, D])
    hwin_d = din("hg_w_in", [D, 5 * D])
    hwout_d = din("hg_w_out", [D, D])
    out_d = nc.dram_tensor("out", [S, D], F32, kind="ExternalOutput").ap()

    def scr(name, shape, dt=F32):
        kind = "ExternalOutput" if name in debug else "Internal"
        return nc.dram_tensor(name, shape, dt, kind=kind).ap()

    WGU = [[scr("WGU%d%d" % (l, h), [HC, 128, 8, 256], BF16) for h in range(2)] for l in range(2)]
    WDN = [[scr("WDN%d%d" % (l, h), [8, 128, HC, 128], BF16) for h in range(2)] for l in range(2)]
    CWIN = scr("CWIN", [8, 128, 8, 384], BF16)
    CWOUT = scr("CWOUT", [8, 128, 8, 128], BF16)
    HWIN = scr("HWIN", [10, 128, 8, 512], BF16)
    HWOUT = scr("HWOUT", [8, 128, 8, 128], BF16)
    hA = scr("hA", [8, 128, UU])
    zA = scr("zA", [8, 128, UU])
    bgA = scr("bgA", [8, 128, UU])
    hB = scr("hB", [8, 128, UU])
    QK = scr("QK", [8, 2, 128, 2, UU], BF16)
    KVT = scr("KVT", [8, 2, UU, 2, 128], BF16)
    DEC = scr("DEC", [8, 2, 128, 132])
    SOG = scr("SOG", [8, 128, S])
    OFW = scr("OFW", [8, 128, S])
    OBW = scr("OBW", [8, 128, S])
    dbufs = {}

    def dbuf(key):
        if key not in dbufs:
            dbufs[key] = Buf(str(key))
        return dbufs[key]

    AR_BYTES = 204 * 1024
    arena = nc.alloc_sbuf_tensor("arena", [128, AR_BYTES // 2], BF16).ap()

    def view(off, nelem, dt):
        assert off % 32 == 0
        if dt == F32:
            assert off + 4 * nelem <= AR_BYTES, (off, nelem)
            return arena[:, off // 2: off // 2 + 2 * nelem].bitcast(F32)
        assert off + 2 * nelem <= AR_BYTES, (off, nelem)
        return arena[:, off // 2: off // 2 + nelem]

    KB = 1024
    o_const = 0
    o_h = 8 * KB
    o_xn = o_h + 32 * KB
    o_hid = o_xn + 16 * KB
    o_wr = o_hid + 44 * KB
    NSLOT = 3
    o_sq = o_wr + NSLOT * 8 * KB
    o_tmp = o_sq + 8 * KB
    o_rs = o_tmp + 8 * KB
    o_stg = o_rs + 4 * KB
    o_cv = o_stg + 24 * KB

    c_off = [o_const]

    def calloc(nelem, dt):
        sz = nelem * (4 if dt == F32 else 2)
        sz = (sz + 31) // 32 * 32
        v = view(c_off[0], nelem, dt)
        c_off[0] += sz
        assert c_off[0] <= o_h
        return v

    ones_bf = calloc(128, BF16)
    ident_f = calloc(128, F32)
    ident_bf = calloc(128, BF16)
    tri_f = calloc(128, F32)
    tri_b = calloc(128, F32)
    mask01 = calloc(512, F32)
    maskF = calloc(64, F32)
    maskB = calloc(64, F32)
    smallp = calloc(NSP, F32)
    modT = calloc(2 * 72 * 2, F32).rearrange("p (l c w) -> p l c w", l=2, w=2)
    gsT = calloc(2 * 3 * 8 * 2, F32).rearrange("p (l s f w) -> p l s f w", l=2, s=3, w=2)
    gtT = calloc(2 * 3 * 8 * 2, F32).rearrange("p (l s f w) -> p l s f w", l=2, s=3, w=2)
    cs_t = calloc(16, F32).rearrange("p (k w) -> p k w", w=2)
    lbT = calloc(16, F32).rearrange("p (r f) -> p r f", f=8)
    omlT = calloc(16, F32).rearrange("p (r f) -> p r f", f=8)
    lbtmp = calloc(16, F32).rearrange("p (r f) -> p r f", f=8)
    bC = Buf("const")

    sp_cc = smallp[:, 0:16].rearrange("p (k w) -> p k w", w=2)
    sp_ng = smallp[:, 16:64].rearrange("p (l s f) -> p l s f", l=2, s=3)
    sp_cw = smallp[:, 64:88].rearrange("p (j f) -> p j f", j=3)
    sp_lb = smallp[:, 88:120].rearrange("p (l r f) -> p l r f", l=2, r=2)
    sp_gn = smallp[:, 120:128]
    sp_fg = smallp[:, 128:136]
    sp_ab = smallp[:, 136:280].rearrange("p (l c) -> p l c", l=2)

    Hh = view(o_h, 8 * 1024, F32).rearrange("p (f t) -> p f t", f=8)
    XN = view(o_xn, 8 * 1024, BF16).rearrange("p (f t) -> p f t", f=8)
    HID = view(o_hid, HC * 1024, BF16).rearrange("p (j t) -> p j t", j=HC)
    bH = [[Buf() for _ in range(2)] for _ in range(8)]
    bXN = [[Buf() for _ in range(2)] for _ in range(8)]
    bHID = [[Buf() for _ in range(2)] for _ in range(HC)]
    SQ = view(o_sq, 8 * 512, BF16).rearrange("p (f t) -> p f t", f=8)
    bSQ = [Buf() for _ in range(8)]
    tmpR = Ring([(view(o_tmp + i * 2 * KB, 512, F32), Buf()) for i in range(4)])
    RT = view(o_rs, 512, F32)
    RSTD = view(o_rs + 2 * KB, 512, F32)
    bRT = Buf()
    bRSTD = Buf()
    sq_i = [0]

    psb = [nc.alloc_psum_tensor("ps%d" % i, [128, 512], F32).ap() for i in range(7)]
    psR = Ring([(psb[i], Buf()) for i in range(6)])
    ps_ss = psb[6]
    b_ss = Buf()
    ps_bf = nc.alloc_psum_tensor("psbf", [128, 1024], BF16).ap()
    b_psbf = Buf()

    def sl(ap, a, n):
        return ap[:, a:a + n]

    wslots = [(view(o_wr + i * 8 * KB, 4096, BF16), Buf()) for i in range(NSLOT)]

    class WStream:
        def __init__(self):
            self.plan = []
            self.issued = 0
            self.cur = 0

        def add(self, dram_ap, dkey, kc, w):
            self.plan.append((dram_ap, dkey, kc, w))

        def _issue(self, i):
            dram_ap, dkey, kc, w = self.plan[i]
            slot, sb = wslots[i % NSLOT]
            dst = slot[:, 0:kc * w].rearrange("p (k c) -> p k c", c=w)
            P.op("sp", lambda e, d=dst, s=dram_ap: e.dma_start(out=d, in_=s),
                 reads=[dbuf(dkey)], writes=[sb], dma_key="wr%d" % (i % NSLOT))

        def get(self):
            i = self.cur
            while self.issued < min(len(self.plan), i + NSLOT):
                self._issue(self.issued)
                self.issued += 1
            dram_ap, dkey, kc, w = self.plan[i]
            slot, sb = wslots[i % NSLOT]
            self.cur += 1
            return slot[:, 0:kc * w].rearrange("p (k c) -> p k c", c=w), sb

    P.op("sp", lambda e: e.dma_start(out=smallp, in_=sp_d), writes=[bC], dma=True)
    P.op("pool", lambda e: e.memset(ones_bf, 1.0), writes=[bC])
    P.op("pool", lambda e: e.memset(ident_f, 1.0), writes=[bC])
    P.op("pool", lambda e: e.affine_select(out=ident_f, in_=ident_f, pattern=[[-1, 128]],
                                           compare_op=ALU.is_equal, fill=0.0, base=0, channel_multiplier=1),
         reads=[bC], writes=[bC])
    P.op("pool", lambda e: e.tensor_copy(out=ident_bf, in_=ident_f), reads=[bC], writes=[bC])
    P.op("pool", lambda e: e.memset(tri_f, 1.0), writes=[bC])
    P.op("pool", lambda e: e.affine_select(out=tri_f, in_=tri_f, pattern=[[1, 128]],
                                           compare_op=ALU.is_ge, fill=0.0, base=0, channel_multiplier=-1),
         reads=[bC], writes=[bC])
    P.op("pool", lambda e: e.memset(tri_b, 1.0), writes=[bC])
    P.op("pool", lambda e: e.affine_select(out=tri_b, in_=tri_b, pattern=[[-1, 128]],
                                           compare_op=ALU.is_ge, fill=0.0, base=0, channel_multiplier=1),
         reads=[bC], writes=[bC])
    for hh in range(2):
        P.op("pool", lambda e, hh=hh: e.tensor_copy(out=maskF[hh * 64:(hh + 1) * 64, :],
                                                    in_=tri_f[hh * 64:(hh + 1) * 64, hh * 64:(hh + 1) * 64]),
             reads=[bC], writes=[bC])
        P.op("pool", lambda e, hh=hh: e.tensor_copy(out=maskB[hh * 64:(hh + 1) * 64, :],
                                                    in_=tri_b[hh * 64:(hh + 1) * 64, hh * 64:(hh + 1) * 64]),
             reads=[bC], writes=[bC])
    P.op("pool", lambda e: e.memset(mask01, 1.0), writes=[bC])
    last_ci = P.op("pool", lambda e: e.memset(mask01.rearrange("p (c j) -> p c j", j=64)[:, :, 0:1], 0.0),
                   reads=[bC], writes=[bC])

    cvf = Ring([(view(o_cv + i * 12 * KB, 3072, F32), Buf()) for i in range(2)])
    cvb = Ring([(view(o_cv + 24 * KB + i * 6 * KB, 3072, BF16), Buf()) for i in range(2)])
    cv_n = [0]

    def conv_piece(src2d, kc, segs, dst_ap, dkey):
        i = cv_n[0]
        cv_n[0] += 1
        fv, fb = cvf.next()
        bv, bb = cvb.next()
        wtot = sum(w for _, w in segs)
        f3 = fv[:, 0:kc * wtot].rearrange("p (k c) -> p k c", c=wtot)
        b3 = bv[:, 0:kc * wtot].rearrange("p (k c) -> p k c", c=wtot)
        src3 = src2d.rearrange("(k p) n -> p k n", p=128)
        c = 0
        for si, (c0, w) in enumerate(segs):
            P.op("act" if i < N_ACT_CONV else "pool", lambda e, d=f3[:, :, c:c + w], s=src3[:, :, c0:c0 + w]: e.dma_start(out=d, in_=s),
                 writes=[fb], dma_key="cvl%d_%d" % (i % 2, si))
            c += w
        P.op("pool", lambda e: e.tensor_copy(out=b3, in_=f3), reads=[fb], writes=[bb])
        P.op("pool", lambda e: e.dma_start(out=dst_ap, in_=b3), reads=[bb], writes=[dbuf(dkey)],
             dma_key="cvs%d" % (i % 2))

    def conv_ffn(l, h):
        for j in range(HC):
            conv_piece(wgu_d[l, h], 8, [(j * 128, 128), (FF + j * 128, 128)], WGU[l][h][j], ("WGU", l, h, j))
        for m in range(8):
            conv_piece(wdn_d[l, h], HC, [(m * 128, 128)], WDN[l][h][m], ("WDN", l, h, m))

    def conv_all():
        conv_ffn(0, 0)
        for m in range(8):
            conv_piece(cwin_d, 8, [(m * 128, 128), (D + m * 128, 128), (2 * D + m * 128, 128)], CWIN[m], ("CWIN", m))
        for m in range(8):
            conv_piece(cwout_d, 8, [(m * 128, 128)], CWOUT[m], ("CWOUT", m))
        conv_ffn(0, 1)
        conv_ffn(1, 0)
        for hd in range(8):
            conv_piece(hwin_d, 8, [(hd * 128, 128), (2 * D + hd * 128, 128)], HWIN[hd][:, :, 0:256], ("HWIN", hd))
            conv_piece(hwin_d, 8, [(3 * D + hd * 128, 128), (4 * D + hd * 128, 128)], HWIN[hd][:, :, 256:512], ("HWIN", hd))
        for hf in range(2):
            conv_piece(hwin_d, 8, [(D + hf * 512, 256)], HWIN[8 + hf][:, :, 0:256], ("HWIN", 8 + hf))
            conv_piece(hwin_d, 8, [(D + hf * 512 + 256, 256)], HWIN[8 + hf][:, :, 256:512], ("HWIN", 8 + hf))
        for m in range(8):
            conv_piece(hwout_d, 8, [(m * 128, 128)], HWOUT[m], ("HWOUT", m))
        conv_ffn(1, 1)

    def stage0_pre():
        P.op("act", lambda e: e.activation(out=cs_t, in_=sp_cc, func=AF.Silu), reads=[bC], writes=[bC])
        P.op("dve", lambda e: e.tensor_tensor(out=lbtmp, in0=sp_lb[:, 1], in1=sp_lb[:, 0], op=ALU.subtract),
             reads=[bC], writes=[bC])
        P.op("act", lambda e: e.activation(out=lbT, in_=lbtmp, func=AF.Sigmoid), reads=[bC], writes=[bC])
        P.op("dve", lambda e: e.tensor_scalar(out=omlT, in0=lbT, scalar1=-1.0, scalar2=1.0, op0=ALU.mult, op1=ALU.add),
             reads=[bC], writes=[bC])

    def stage0():
        mT = view(o_h, 9 * D, F32)
        bmT = Buf()
        adaR = Ring([(view(o_hid + i * 16 * KB, 4096, F32).rearrange("p (k c) -> p k c", c=512), Buf())
                     for i in range(2)])
        for l in range(2):
            src3 = adaw_d[l].rearrange("(k p) n -> p k n", p=128)
            for cb in range(18):
                av, ab = adaR.next()
                P.op("sp", lambda e, d=av, s=src3[:, :, cb * 512:(cb + 1) * 512]: e.dma_start(out=d, in_=s),
                     writes=[ab], dma=True)
                pv, pb = psR.next()
                for kc in range(8):
                    P.op("pe", lambda e, o=pv[0:2, :], a=cs_t[:, kc, :], r=av[:, kc, :], kc=kc:
                         e.matmul(o, lhsT=a, rhs=r, start=(kc == 0), stop=(kc == 7)),
                         reads=[ab, bC], writes=[pb])
                P.op("dve", lambda e, o=mT[0:2, cb * 512:(cb + 1) * 512], i=pv[0:2, :]: e.tensor_copy(out=o, in_=i),
                     reads=[pb], writes=[bmT])
            pv, pb = psR.next()
            for ch in range(72):
                P.op("pe", lambda e, o=pv[:, ch * 2:ch * 2 + 2], i=mT[0:2, ch * 128:(ch + 1) * 128]:
                     e.transpose(o, i, ident_f[0:2, 0:2]), reads=[bmT, bC], writes=[pb])
            bias_b = bass.AP(sp_ab.tensor, sp_ab[:, l].offset, [list(sp_ab.ap[0]), [1, 72], [0, 2]])
            P.op("dve", lambda e, o=modT[:, l], i=pv[:, 0:144].rearrange("p (c w) -> p c w", w=2), b=bias_b:
                 e.tensor_tensor(out=o, in0=i, in1=b, op=ALU.add), reads=[pb, bC], writes=[bC])
            for s in range(3):
                g_b = bass.AP(sp_ng.tensor, sp_ng[:, l, s].offset, [list(sp_ng.ap[0]), [1, 8], [0, 2]])
                P.op("dve", lambda e, o=gsT[:, l, s], i=modT[:, l, (3 * s + 1) * 8:(3 * s + 2) * 8], g=g_b:
                     e.scalar_tensor_tensor(out=o, in0=i, scalar=1.0, in1=g, op0=ALU.add, op1=ALU.mult),
                     reads=[bC], writes=[bC])
                P.op("dve", lambda e, o=gtT[:, l, s], i=modT[:, l, (3 * s + 2) * 8:(3 * s + 3) * 8], s=s:
                     e.tensor_scalar(out=o, in0=i, scalar1=(1.0 if s == 1 else 0.5), scalar2=None, op0=ALU.mult),
                     reads=[bC], writes=[bC])

    def m_gs(l, s, fc, w):
        return gsT[:, l, s, fc, w:w + 1]

    def m_sh(l, s, fc, w):
        return modT[:, l, 3 * s * 8 + fc, w:w + 1]

    def m_gt(l, s, fc, w):
        return gtT[:, l, s, fc, w:w + 1]

    rsR = Ring([(view(o_rs + i * 2 * KB, 512, F32), Buf()) for i in range(2)])
    ssacc = [(ps_ss, b_ss), (ps_bf.bitcast(F32), b_psbf)]
    sqacc_i = [0]

    def preacc_square(m, nti, off, n):
        k = sqacc_i[0] % 8
        sqacc_i[0] += 1
        P.op("act", lambda e, o=SQ[:, k, 0:n], s=Hh[:, m, off:off + n]: e.activation(out=o, in_=s, func=AF.Square),
             reads=[bH[m][nti]], writes=[bSQ[k]])
        return (k, m, nti, n)

    def preacc_mm(pend):
        for (k, m, nti, n) in pend:
            av, ab = ssacc[nti]
            P.op("pe", lambda e, o=av[:, 0:n], r=SQ[:, k, 0:n]: e.matmul(o, lhsT=ones_bf, rhs=r, start=(m == 0), stop=(m == 7)),
                 reads=[bSQ[k], bC], writes=[ab])


    def sumsq_rstd(nts, nti, srcs, src_bufs, nfeat, pre=False):
        off, n = nts[nti]
        if pre:
            pv, pb = ssacc[nti]
            rv, rb = rsR.next()
            P.op("act", lambda e: e.activation(out=rv[:, 0:n], in_=pv[:, 0:n], func=AF.Ln, scale=1.0 / nfeat, bias=EPS),
                 reads=[pb], writes=[rb])
            P.op("act", lambda e: e.activation(out=rv[:, 0:n], in_=rv[:, 0:n], func=AF.Exp, scale=-0.5), reads=[rb], writes=[rb])
            return rv, rb
        nsrc = len(srcs)
        sqi = []
        for i in range(nsrc):
            k = sq_i[0] % 8 if nsrc == 1 else i
            sq_i[0] += 1
            sqi.append(k)
            P.op("act", lambda e, o=SQ[:, k, 0:n], s=srcs[i]: e.activation(out=o, in_=s, func=AF.Square),
                 reads=[src_bufs[i]], writes=[bSQ[k]])
        pv, pb = psR.next()
        for i in range(nsrc):
            k = sqi[i]
            P.op("pe", lambda e, o=pv[:, 0:n], r=SQ[:, k, 0:n], i=i:
                 e.matmul(o, lhsT=ones_bf, rhs=r, start=(i == 0), stop=(i == nsrc - 1)),
                 reads=[bSQ[k], bC], writes=[pb])
        rv, rb = rsR.next()
        P.op("act", lambda e: e.activation(out=rv[:, 0:n], in_=pv[:, 0:n], func=AF.Ln, scale=1.0 / nfeat, bias=EPS),
             reads=[pb], writes=[rb])
        P.op("act", lambda e: e.activation(out=rv[:, 0:n], in_=rv[:, 0:n], func=AF.Exp, scale=-0.5), reads=[rb], writes=[rb])
        return rv, rb

    def norm_mod(nts, l, s, w, pre=True):
        for nti, (off, n) in enumerate(nts):
            RSTD, bRSTD = sumsq_rstd(nts, nti, [Hh[:, fc, off:off + n] for fc in range(8)], [bH[fc][nti] for fc in range(8)], D, pre=pre)
            for fc in range(8):
                tv, tb = tmpR.next()
                P.op("dve", lambda e, o=tv[:, 0:n], a=Hh[:, fc, off:off + n]:
                     e.tensor_tensor(out=o, in0=a, in1=RSTD[:, 0:n], op=ALU.mult),
                     reads=[bH[fc][nti], bRSTD], writes=[tb])
                P.op("act", lambda e, o=XN[:, fc, off:off + n], i=tv[:, 0:n], fc=fc:
                     e.activation(out=o, in_=i, func=AF.Identity, scale=m_gs(l, s, fc, w), bias=m_sh(l, s, fc, w)),
                     reads=[tb, bC], writes=[bXN[fc][nti]])

    BG = [None]

    def tick(nmax=1):
        for _ in range(nmax):
            g = BG[0]
            if g is None:
                return
            try:
                next(g)
            except StopIteration:
                BG[0] = None

    def ffn(ws, nts, l, hf, w, pre=True):
        s = 0 if hf == 0 else 2
        norm_mod(nts, l, s, w, pre=pre)
        for j in range(HC):
            wv, wb = ws.get()
            for nti, (off, n) in enumerate(nts):
                pg, bg_ = psR.next()
                pu, bu_ = psR.next()
                for kc in range(8):
                    P.op("pe", lambda e, o=pg[:, 0:n], a=wv[:, kc, 0:128], r=XN[:, kc, off:off + n], kc=kc:
                         e.matmul(o, lhsT=a, rhs=r, start=(kc == 0), stop=(kc == 7)),
                         reads=[wb, bXN[kc][nti]], writes=[bg_])
                for kc in range(8):
                    P.op("pe", lambda e, o=pu[:, 0:n], a=wv[:, kc, 128:256], r=XN[:, kc, off:off + n], kc=kc:
                         e.matmul(o, lhsT=a, rhs=r, start=(kc == 0), stop=(kc == 7)),
                         reads=[wb, bXN[kc][nti]], writes=[bu_])
                tv, tb = tmpR.next()
                P.op("act", lambda e, o=tv[:, 0:n], i=pg[:, 0:n]: e.activation(out=o, in_=i, func=AF.Silu),
                     reads=[bg_], writes=[tb])
                P.op("dve", lambda e, o=HID[:, j, off:off + n], a=tv[:, 0:n], b=pu[:, 0:n]:
                     e.tensor_tensor(out=o, in0=a, in1=b, op=ALU.mult),
                     reads=[tb, bu_], writes=[bHID[j][nti]])
            tick()
        pend = []
        for m in range(8):
            wv, wb = ws.get()
            pend_new = []
            for nti, (off, n) in enumerate(nts):
                py, by_ = psR.next()
                for j in range(HC):
                    P.op("pe", lambda e, o=py[:, 0:n], a=wv[:, j, :], r=HID[:, j, off:off + n], j=j:
                         e.matmul(o, lhsT=a, rhs=r, start=(j == 0), stop=(j == HC - 1)),
                         reads=[wb, bHID[j][nti]], writes=[by_])
                P.op("dve", lambda e, o=Hh[:, m, off:off + n], i=py[:, 0:n], m=m:
                     e.scalar_tensor_tensor(out=o, in0=i, scalar=m_gt(l, s, m, w), in1=o, op0=ALU.mult, op1=ALU.add),
                     reads=[by_, bC, bH[m][nti]], writes=[bH[m][nti]])
                pend_new.append(preacc_square(m, nti, off, n))
            preacc_mm(pend)
            pend = pend_new
            tick()
        preacc_mm(pend)

    def ffn_plan(ws, l, hf):
        for j in range(HC):
            ws.add(WGU[l][hf][j], ("WGU", l, hf, j), 8, 256)
        for m in range(8):
            ws.add(WDN[l][hf][m], ("WDN", l, hf, m), HC, 128)

    def proj_out(ws, nts, l, w, SRC=None, bSRC=None):
        SRC = XN if SRC is None else SRC
        bSRC = bXN if bSRC is None else bSRC
        pend = []
        for m in range(8):
            wv, wb = ws.get()
            pend_new = []
            for nti, (off, n) in enumerate(nts):
                py, by_ = psR.next()
                for kc in range(8):
                    P.op("pe", lambda e, o=py[:, 0:n], a=wv[:, kc, :], r=SRC[:, kc, off:off + n], kc=kc:
                         e.matmul(o, lhsT=a, rhs=r, start=(kc == 0), stop=(kc == 7)),
                         reads=[wb, bSRC[kc][nti]], writes=[by_])
                P.op("dve", lambda e, o=Hh[:, m, off:off + n], i=py[:, 0:n], m=m:
                     e.scalar_tensor_tensor(out=o, in0=i, scalar=m_gt(l, 1, m, w), in1=o, op0=ALU.mult, op1=ALU.add),
                     reads=[by_, bC, bH[m][nti]], writes=[bH[m][nti]])
                pend_new.append(preacc_square(m, nti, off, n))
            preacc_mm(pend)
            pend = pend_new
        preacc_mm(pend)

    def supertiles(with_ctx):
        sts = []
        if with_ctx:
            sts.append((0, 256, [(0, 256)], 1))
        for k in range(8):
            sts.append((LC + k * 1024, 1024, [(0, 512), (512, 512)], 0))
        return sts

    def load_h(src, u0, T, nts):
        for nti, (off, n) in enumerate(nts):
            P.op("sp", lambda e: e.dma_start(out=Hh[:, :, off:off + n], in_=src.rearrange("f p u -> p f u")[:, :, u0 + off:u0 + off + n]),
                 reads=[dbuf((id(src), u0))], writes=[bH[fc][nti] for fc in range(8)], dma=True)

    def store_h(dst, u0, T, nts):
        P.op("sp", lambda e: e.dma_start(out=dst.rearrange("f p u -> p f u")[:, :, u0:u0 + T], in_=Hh[:, :, 0:T]),
             reads=[bH[fc][nti] for fc in range(8) for nti in range(len(nts))], writes=[dbuf((id(dst), u0))], dma=True)

    def stage1():
        ws = WStream()
        sts = supertiles(True)
        for st in sts:
            ffn_plan(ws, 0, 0)
            for m in range(8):
                ws.add(CWIN[m], ("CWIN", m), 8, 384)
        XT = view(o_hid, 8 * 1024, F32).rearrange("p (b d) -> p b d", d=1024)
        bXT = Buf()
        bXTs = [Buf(), Buf()]
        stgR = Ring([(view(o_stg + i * 2 * KB, 512, F32), Buf()) for i in range(6)])
        for (u0, T, nts, w) in sts:
            nb = T // 128
            src = ctx_d if w else x_d[u0 - LC:u0 - LC + T]
            for nti, (off, n) in enumerate(nts):
                P.op("sp", lambda e, s=src[off:off + n]: e.dma_start(out=XT[:, off // 128:(off + n) // 128, :], in_=s.rearrange("(b p) d -> p b d", p=128)),
                     writes=[bXTs[nti]] + ([bHID[j][q] for j in range(HC) for q in range(2)] if nti == 0 else []), dma=True)
            k = 0
            for nti, (off, n) in enumerate(nts):
                for fc in range(8):
                    pv, pb = psR.next()
                    for bq in range(n // 128):
                        blk = off // 128 + bq
                        P.op("pe", lambda e, o=pv[:, bq * 128:(bq + 1) * 128], i=XT[:, blk, fc * 128:(fc + 1) * 128]:
                             e.transpose(o, i, ident_f), reads=[bXTs[nti], bC], writes=[pb])
                    eng = "act" if k % 2 == 0 else "dve"
                    k += 1
                    if eng == "act":
                        P.op("act", lambda e, o=Hh[:, fc, off:off + n], i=pv[:, 0:n]: e.activation(out=o, in_=i, func=AF.Copy),
                             reads=[pb], writes=[bH[fc][nti]])
                    else:
                        P.op("dve", lambda e, o=Hh[:, fc, off:off + n], i=pv[:, 0:n]: e.tensor_copy(out=o, in_=i),
                             reads=[pb], writes=[bH[fc][nti]])
            for j in range(HC):
                for nti in range(2):
                    bHID[j][nti].readers.extend(bXTs[0].readers + bXTs[1].readers)
            ffn(ws, nts, 0, 0, w, pre=False)
            store_h(hA, u0, T, nts)
            norm_mod(nts, 0, 1, w)
            for m in range(8):
                wv, wb = ws.get()
                for nti, (off, n) in enumerate(nts):
                    pB, bB = psR.next()
                    pC, bCc = psR.next()
                    pV, bV = psR.next()
                    for (pp, bb, c0) in ((pB, bB, 0), (pC, bCc, 128), (pV, bV, 256)):
                        for kc in range(8):
                            P.op("pe", lambda e, o=pp[:, 0:n], a=wv[:, kc, c0:c0 + 128], r=XN[:, kc, off:off + n], kc=kc:
                                 e.matmul(o, lhsT=a, rhs=r, start=(kc == 0), stop=(kc == 7)),
                                 reads=[wb, bXN[kc][nti]], writes=[bb])
                    tv, tb = tmpR.next()
                    P.op("act", lambda e, o=tv[:, 0:n], i=pC[:, 0:n]: e.activation(out=o, in_=i, func=AF.Copy),
                         reads=[bCc], writes=[tb])
                    zv, zb = stgR.next()
                    P.op("dve", lambda e, o=zv[:, 0:n], a=tv[:, 0:n], b=pV[:, 0:n]: e.tensor_tensor(out=o, in0=a, in1=b, op=ALU.mult),
                         reads=[tb, bV], writes=[zb])
                    P.op("sp", lambda e, d=zA[m, :, u0 + off:u0 + off + n], s=zv[:, 0:n]: e.dma_start(out=d, in_=s),
                         reads=[zb], writes=[dbuf(("zA", m))], dma=True)
                    gv, gb = stgR.next()
                    P.op("act", lambda e, o=gv[:, 0:n], i=pB[:, 0:n]: e.activation(out=o, in_=i, func=AF.Copy),
                         reads=[bB], writes=[gb])
                    P.op("sp", lambda e, d=bgA[m, :, u0 + off:u0 + off + n], s=gv[:, 0:n]: e.dma_start(out=d, in_=s),
                         reads=[gb], writes=[dbuf(("bgA", m))], dma=True)

    def stage2():
        ws = WStream()
        sts = supertiles(True)
        for st in sts:
            for m in range(8):
                ws.add(CWOUT[m], ("CWOUT", m), 8, 128)
            ffn_plan(ws, 0, 1)
            ffn_plan(ws, 1, 0)
            for c in (8, 9, 0, 1, 2, 3, 4, 5, 6, 7):
                ws.add(HWIN[c], ("HWIN", c), 8, 512)
        zinR = Ring([(view(o_stg + i * 4608, 1152, F32), Buf()) for i in range(2)])
        bginR = Ring([(view(o_stg + 9216 + i * 4096, 1024, F32), Buf()) for i in range(2)])
        ZC = view(o_stg + 9216 + 8192, 1024, F32)
        bZC = Buf()
        bZCh = [Buf(), Buf()]
        hoff = [o_hid, o_wr]

        def halloc(nelem, dt):
            v = view(hoff[0], nelem, dt)
            hoff[0] += (nelem * (4 if dt == F32 else 2) + 31) // 32 * 32
            assert hoff[0] <= hoff[1], (hoff, nelem)
            return v

        def mkset(bw):
            d = dict(e=(halloc(512, F32), Buf()), la=(halloc(512, F32), Buf()), lb=(halloc(512, F32), Buf()),
                     g=(halloc(512, F32), Buf()), sk=(halloc(512, BF16), Buf()))
            if bw:
                d["g2"] = (halloc(512, F32), Buf())
            return d
        fsets = [mkset(False), mkset(False)]
        bsets = [mkset(True), mkset(True)]
        s_og = Ring([(halloc(512, F32), Buf()) for _ in range(2)])
        hg_bufs = [ts[k][1] for ts in fsets + bsets for k in ts] + [b for _, b in s_og.items]
        hoff[0], hoff[1] = o_cv, AR_BYTES
        fsets.append(mkset(False))
        bsets.append(mkset(True))
        tqR = Ring([(halloc(512, F32), Buf()) for _ in range(3)])
        sqR = Ring([(halloc(512, BF16), Buf()) for _ in range(3)])
        sktR = Ring([(halloc(512, BF16), Buf()) for _ in range(3)])
        decR = Ring([(halloc(8, F32), Buf()) for _ in range(4)])
        s_v = Ring([(halloc(512, BF16), Buf()) for _ in range(2)])

        for sti, (u0, T, nts, w) in enumerate(sts):
            if sti == 0:
                load_h(hA, u0, T, nts)
            R = 256 if w else 64
            for fc in range(8):
                zv, zb = zinR.next()
                gv, gb = bginR.next()
                lo = u0 - 64
                hi = u0 + T + 64
                if w:
                    lo, hi = u0, u0 + T
                else:
                    if lo < LC:
                        P.op("pool", lambda e, o=zv[:, 0:64]: e.memset(o, 0.0), writes=[zb])
                        lo = u0
                    if hi > UU:
                        P.op("pool", lambda e, o=zv[:, 64 + T:128 + T]: e.memset(o, 0.0), writes=[zb])
                        hi = u0 + T
                P.op("sp", lambda e, d=zv[:, 64 + lo - u0:64 + hi - u0], s=zA[fc, :, lo:hi]: e.dma_start(out=d, in_=s),
                     reads=[dbuf(("zA", fc))], writes=[zb], dma=True)
                P.op("sp", lambda e, d=gv[:, 0:T], s=bgA[fc, :, u0:u0 + T]: e.dma_start(out=d, in_=s),
                     reads=[dbuf(("bgA", fc))], writes=[gb], dma=True)
                cw0, cw1, cw2 = sp_cw[:, 0, fc:fc + 1], sp_cw[:, 1, fc:fc + 1], sp_cw[:, 2, fc:fc + 1]
                hs = [(nti, off, n, bZCh[nti]) for nti, (off, n) in enumerate(nts)]
                for (nti, off, n, bz) in hs:
                    P.op("act", lambda e, i=zv[:, 64 + off:64 + off + n], c=cw1, o=ZC[:, off:off + n]:
                         e.activation(out=o, in_=i, func=AF.Copy, scale=c), reads=[zb, bC], writes=[bz])
                for step in range(2):
                    for (nti, off, n, bz) in hs:
                        cw = cw0 if step == 0 else cw2
                        if w or fc < 4:
                            zc3 = ZC[:, off:off + n].rearrange("p (r c) -> p r c", c=R)
                            zi3 = zv[:, 64 + off:64 + off + n].rearrange("p (r c) -> p r c", c=R)
                            if step == 0:
                                o_, i_ = zc3[:, :, 1:R], zi3[:, :, 0:R - 1]
                            else:
                                o_, i_ = zc3[:, :, 0:R - 1], zi3[:, :, 1:R]
                        else:
                            o_ = ZC[:, off:off + n]
                            i_ = zv[:, off:off + n] if step == 0 else zv[:, 128 + off:128 + off + n]
                        P.op("dve", lambda e, o=o_, i=i_, c=cw:
                             e.scalar_tensor_tensor(out=o, in0=i, scalar=c, in1=o, op0=ALU.mult, op1=ALU.add),
                             reads=[zb, bC, bz], writes=[bz])
                for (nti, off, n, bz) in hs:
                    P.op("pool", lambda e, o=XN[:, fc, off:off + n], a=gv[:, off:off + n], z=ZC[:, off:off + n]:
                         e.tensor_tensor(out=o, in0=a, in1=z, op=ALU.mult), reads=[gb, bz], writes=[bXN[fc][nti]])
            proj_out(ws, nts, 0, w)
            ffn(ws, nts, 0, 1, w)
            ffn(ws, nts, 1, 0, w)
            if not w:
                store_h(hB, u0, T, nts)
            norm_mod(nts, 1, 1, w)
            if sti + 1 < len(sts):
                nu0, nT, nnts, _ = sts[sti + 1]
                load_h(hA, nu0, nT, nnts)
            for hf in range(2):
                wv, wb = ws.get()
                for blk in range(T // 128):
                    nti = (blk * 128) // 512
                    pv, pb = psR.next()
                    for kc in range(8):
                        P.op("pe", lambda e, o=pv, a=XN[:, kc, blk * 128:(blk + 1) * 128], r=wv[:, kc, :], kc=kc:
                             e.matmul(o, lhsT=a, rhs=r, start=(kc == 0), stop=(kc == 7)),
                             reads=[wb, bXN[kc][nti]], writes=[pb])
                    sv, bsv = s_v.next()
                    P.op("act", lambda e, sv=sv, pv=pv: e.activation(out=sv, in_=pv, func=AF.Copy), reads=[pb], writes=[bsv])
                    ub = u0 + blk * 128
                    for dr in range(2):
                        P.op("sp", lambda e, d=KVT[hf * 4:hf * 4 + 4, dr, ub:ub + 128, 1, :].rearrange("h t d -> t h d"),
                             s=sv.rearrange("p (h d) -> p h d", d=128): e.dma_start(out=d, in_=s),
                             reads=[bsv], writes=[dbuf(("KVT", hf * 4 + q, dr)) for q in range(4)], dma=True)
            hid_all = [bHID[j][nti] for j in range(HC) for nti in range(2)]
            P.op("dve", lambda e: e.engine_nop(), reads=[], writes=hid_all + hg_bufs)
            units = []
            for hd in range(8):
                for nti, (off, n) in enumerate(nts):
                    for dr in range(2):
                        ui = len(units)
                        units.append(dict(hd=hd, nti=nti, off=off, n=n, uo=u0 + off, dr=dr,
                                          ts=(fsets if dr == 0 else bsets)[(ui // 2) % 3]))
            wcur = [None]

            def ustep(k, U):
                hd, nti, off, n, uo, dr, ts = U["hd"], U["nti"], U["off"], U["n"], U["uo"], U["dr"], U["ts"]
                ncn = n // 64
                E, bE = ts["e"]
                LA, bLA = ts["la"]
                LB, bLB = ts["lb"]
                G, bG = ts["g"]
                sk_, bsk = ts["sk"]
                GG, bGG = (G, bG) if dr == 0 else ts["g2"]
                if k == 0:
                    if nti == 0 and dr == 0:
                        wcur[0] = ws.get()
                    wv, wb = wcur[0]

                    def mm8(c0):
                        pv, pb = psR.next()
                        for kc in range(8):
                            P.op("pe", lambda e, o=pv[:, 0:n], a=wv[:, kc, c0:c0 + 128], r=XN[:, kc, off:off + n], kc=kc:
                                 e.matmul(o, lhsT=a, rhs=r, start=(kc == 0), stop=(kc == 7)),
                                 reads=[wb, bXN[kc][nti]], writes=[pb])
                        return pv, pb
                    if dr == 0:
                        pq, bq_ = mm8(0)
                        tq, btq = tqR.next()
                        U["tq"] = (tq, btq)
                        P.op("act", lambda e: e.activation(out=tq[:, 0:n], in_=pq[:, 0:n], func=AF.Silu), reads=[bq_], writes=[btq])
                        if not w:
                            po, bo = mm8(384)
                            ov, bov = s_og.next()
                            P.op("act", lambda e: e.activation(out=ov[:, 0:n], in_=po[:, 0:n], func=AF.Silu), reads=[bo], writes=[bov])
                            P.op("sp", lambda e, d=SOG[hd, :, uo - LC:uo - LC + n]: e.dma_start(out=d, in_=ov[:, 0:n]),
                                 reads=[bov], writes=[dbuf(("SOG", hd))], dma=True)
                    else:
                        U["tq"] = units[U["ui"] - 1]["tq"]
                    px, bx = mm8(128 * (1 + dr))
                    P.op("act", lambda e: e.activation(out=E[:, 0:n], in_=px[:, 0:n], func=AF.Exp, scale=-1.0), reads=[bx], writes=[bE])
                elif k == 1:
                    P.op("act", lambda e: e.activation(out=LA[:, 0:n], in_=E[:, 0:n], func=AF.Ln, scale=lbT[:, dr, hd:hd + 1], bias=1.0),
                         reads=[bE, bC], writes=[bLA])
                    P.op("act", lambda e: e.activation(out=LB[:, 0:n], in_=E[:, 0:n], func=AF.Ln, bias=1.0), reads=[bE], writes=[bLB])
                elif k == 2:
                    P.op("pool", lambda e: e.tensor_tensor(out=LA[:, 0:n], in0=LA[:, 0:n], in1=LB[:, 0:n], op=ALU.subtract),
                         reads=[bLA, bLB], writes=[bLA])
                    P.op("dve", lambda e: e.tensor_tensor_scan(out=G[:, 0:n], data0=mask01[:, 0:n], data1=LA[:, 0:n],
                                                               initial=0.0, op0=ALU.mult, op1=ALU.add),
                         reads=[bLA, bC], writes=[bG])
                elif k == 3:
                    if dr == 1:
                        P.op("pool", lambda e: e.tensor_tensor(out=LA[:, 0:n], in0=LA[:, 0:n], in1=G[:, 0:n], op=ALU.subtract),
                             reads=[bLA, bG], writes=[bLA])
                        last = bass.AP(G.tensor, G.offset + 63, [list(G.ap[0]), [64, ncn], [0, 64]])
                        P.op("dve", lambda e: e.tensor_tensor(out=GG[:, 0:n].rearrange("p (c j) -> p c j", j=64),
                                                               in0=LA[:, 0:n].rearrange("p (c j) -> p c j", j=64), in1=last, op=ALU.add),
                             reads=[bLA, bG], writes=[bGG])
                    P.op("pool", lambda e: e.tensor_tensor(out=LB[:, 0:n], in0=LB[:, 0:n], in1=GG[:, 0:n], op=ALU.add),
                         reads=[bLB, bGG], writes=[bLB])
                elif k == 4:
                    P.op("act", lambda e: e.activation(out=LA[:, 0:n], in_=GG[:, 0:n], func=AF.Exp), reads=[bGG], writes=[bLA])
                    P.op("act", lambda e: e.activation(out=LB[:, 0:n], in_=LB[:, 0:n], func=AF.Exp, scale=-1.0), reads=[bLB], writes=[bLB])
                elif k == 5:
                    tq, btq = U["tq"]
                    sq_, bsq = sqR.next()
                    P.op("dve", lambda e: e.tensor_tensor(out=sq_[:, 0:n], in0=tq[:, 0:n], in1=LA[:, 0:n], op=ALU.mult),
                         reads=[btq, bLA], writes=[bsq])
                    P.op("sp", lambda e, d=QK[hd, dr, :, 0, uo:uo + n]: e.dma_start(out=d, in_=sq_[:, 0:n]),
                         reads=[bsq], writes=[dbuf(("QK", hd, dr))], dma=True)
                    P.op("dve", lambda e: e.scalar_tensor_tensor(out=sk_[:, 0:n], in0=E[:, 0:n], scalar=omlT[:, dr, hd:hd + 1],
                                                                 in1=LB[:, 0:n], op0=ALU.mult, op1=ALU.mult),
                         reads=[bE, bLB, bC], writes=[bsk])
                    P.op("sp", lambda e, d=QK[hd, dr, :, 1, uo:uo + n]: e.dma_start(out=d, in_=sk_[:, 0:n]),
                         reads=[bsk], writes=[dbuf(("QK", hd, dr))], dma=True)
                    dv, bdv = decR.next()
                    pos = 63 if dr == 0 else 0
                    dsrc = bass.AP(LA.tensor, LA.offset + pos, [list(LA.ap[0]), [64, ncn]])
                    P.op("pool", lambda e: e.tensor_copy(out=dv[:, 0:ncn], in_=dsrc), reads=[bLA], writes=[bdv])
                    P.op("sp", lambda e, d=DEC[hd, dr, :, uo // 64:uo // 64 + ncn]: e.dma_start(out=d, in_=dv[:, 0:ncn]),
                         reads=[bdv], writes=[dbuf(("DEC", hd, dr))], dma=True)
                elif k == 6:
                    pv, pb = psR.next()
                    pvb = pv[:, 0:256].bitcast(BF16)
                    for bq in range(n // 128):
                        P.op("pe", lambda e, o=pvb[:, bq * 128:(bq + 1) * 128], i=sk_[:, bq * 128:(bq + 1) * 128]:
                             e.transpose(o, i, ident_bf), reads=[bsk, bC], writes=[pb])
                    skt, bskt = sktR.next()
                    P.op("dve", lambda e: e.tensor_copy(out=skt[:, 0:n], in_=pvb[:, 0:n]), reads=[pb], writes=[bskt])
                    P.op("sp", lambda e, d=KVT[hd, dr, uo:uo + n, 0, :].rearrange("(b t) d -> t b d", t=128):
                         e.dma_start(out=d, in_=skt[:, 0:n].rearrange("p (b d) -> p b d", d=128)),
                         reads=[bskt], writes=[dbuf(("KVT", hd, dr))], dma=True)
            for ui, U in enumerate(units):
                U["ui"] = ui
            NU = len(units)
            for tick in range(NU + 6):
                for k in range(6, -1, -1):
                    ui = tick - k
                    if 0 <= ui < NU:
                        ustep(k, units[ui])
            P.op("dve", lambda e: e.engine_nop(), reads=[], writes=hid_all + hg_bufs)

    def stage3():
        NG = DBG.get('NG', UU // 256)
        per = 16 * KB
        for hg in range(DBG.get('HG', 2)):
            chains = []
            for q in range(4):
                for dr in range(2):
                    ci = q * 2 + dr
                    base = o_h + ci * per
                    off = [base]

                    def al(nelem, dt):
                        v = view(off[0], nelem, dt)
                        off[0] += (nelem * (4 if dt == F32 else 2) + 31) // 32 * 32
                        assert off[0] <= base + per
                        return v
                    ch = dict(hd=hg * 4 + q, dr=dr,
                              qk=[(al(512, BF16), Buf()) for _ in range(2)],
                              kv=[(al(512, BF16), Buf()) for _ in range(2)],
                              attm=[(al(128, BF16), Buf()) for _ in range(2)],
                              Pm=[(al(128, F32), Buf()) for _ in range(2)],
                              Sb=[(al(128, BF16), Buf()) for _ in range(2)],
                              ost=[(al(256, F32), [Buf() for _ in range(4)]) for _ in range(2)],
                              dec=(al(132, F32), Buf()), step=0)
                    chains.append(ch)
            for ch in chains:
                hd, dr = ch["hd"], ch["dr"]
                P.op("sp", lambda e, d=ch["dec"][0], s=DEC[hd, dr]: e.dma_start(out=d, in_=s),
                     reads=[dbuf(("DEC", hd, dr))], writes=[ch["dec"][1]], dma=True)
                P.op("pool", lambda e, o=ch["Pm"][1][0]: e.memset(o, 0.0), writes=[ch["Pm"][1][1]])
                P.op("pool", lambda e, o=ch["Sb"][1][0]: e.memset(o, 0.0), writes=[ch["Sb"][1][1]])

            def gidx(ch, gi):
                if ch["dr"] == 0 or gi == 0:
                    return gi
                return NG - gi

            def issue_loads(ch, gi):
                g = gidx(ch, gi)
                hd, dr = ch["hd"], ch["dr"]
                qv, qb = ch["qk"][gi % 2]
                kv, kb = ch["kv"][gi % 2]
                P.op("sp", lambda e, d=qv.rearrange("p (a t) -> p a t", a=2), s=QK[hd, dr, :, :, g * 256:(g + 1) * 256]:
                     e.dma_start(out=d, in_=s), reads=[dbuf(("QK", hd, dr))], writes=[qb], dma=True)
                P.op("sp", lambda e, d=kv.rearrange("p (b a d) -> p b a d", b=2, a=2),
                     s=KVT[hd, dr, g * 256:(g + 1) * 256].rearrange("(b t) a d -> t b a d", t=128):
                     e.dma_start(out=d, in_=s), reads=[dbuf(("KVT", hd, dr))], writes=[kb], dma=True)

            for ch in chains:
                issue_loads(ch, 0)
            pbk = [Buf() for _ in range(7)]
            psA = Ring([(psb[0][:, 0:128], pbk[0])])
            psO = Ring([(psb[1 + i][:, 0:64], pbk[1 + i]) for i in range(3)])
            psU = Ring([(psb[4 + i][:, 0:128], pbk[4 + i]) for i in range(3)])
            kk = 0
            for gi in range(NG):
                for ch in chains:
                    if gi + 1 < NG:
                        issue_loads(ch, gi + 1)
                    g = gidx(ch, gi)
                    hd, dr = ch["hd"], ch["dr"]
                    qv, qb = ch["qk"][gi % 2]
                    kv, kb = ch["kv"][gi % 2]
                    q2 = qv.rearrange("p (a t) -> p a t", a=2)
                    kv4 = kv.rearrange("p (b a d) -> p b a d", b=2, a=2)
                    latent = g >= 1
                    ch["cur_attm"] = None
                    if latent:
                        av, ab = psA.next()
                        for i in range(4):
                            b_, h_ = i // 2, i % 2
                            P.op("pe", lambda e, o=av[h_ * 64:(h_ + 1) * 64, b_ * 64:(b_ + 1) * 64],
                                 a=q2[:, 1, i * 64:(i + 1) * 64], r=q2[:, 0, i * 64:(i + 1) * 64]:
                                 e.matmul(o, lhsT=a, rhs=r, start=True, stop=True), reads=[qb], writes=[ab])
                        mv, mb = ch["attm"][gi % 2]
                        msk = maskF if dr == 0 else maskB
                        mb3 = bass.AP(msk.tensor, msk.offset, [list(msk.ap[0]), [0, 2], [1, 64]])
                        P.op("dve", lambda e, o=mv.rearrange("p (b j) -> p b j", j=64), i=av.rearrange("p (b j) -> p b j", j=64), m=mb3:
                             e.tensor_tensor(out=o, in0=i, in1=m, op=ALU.mult), reads=[ab, bC], writes=[mb])
                        ch["cur_attm"] = (mv, mb)
                        ch["cur_ost"] = ch["ost"][gi % 2]
                for ii in range(4):
                    for ch in chains:
                        g = gidx(ch, gi)
                        hd, dr = ch["hd"], ch["dr"]
                        i = ii if dr == 0 else 3 - ii
                        b_, h_ = i // 2, i % 2
                        c_glob = g * 4 + i
                        qv, qb = ch["qk"][gi % 2]
                        kv, kb = ch["kv"][gi % 2]
                        q2 = qv.rearrange("p (a t) -> p a t", a=2)
                        kv4 = kv.rearrange("p (b a d) -> p b a d", b=2, a=2)
                        st_ = ch["step"]
                        Pold, bPold = ch["Pm"][(st_ + 1) % 2]
                        Pnew, bPnew = ch["Pm"][st_ % 2]
                        Sold, bSold = ch["Sb"][(st_ + 1) % 2]
                        Snew, bSnew = ch["Sb"][st_ % 2]
                        decv, decb = ch["dec"]
                        ktm = kv4[h_ * 64:(h_ + 1) * 64, b_, 0, :]
                        vtm = kv4[h_ * 64:(h_ + 1) * 64, b_, 1, :]
                        if g >= 1:
                            mv, mb = ch["cur_attm"]
                            ov, ob = psO.next()
                            P.op("pe", lambda e, o=ov, a=vtm, r=mv[h_ * 64:(h_ + 1) * 64, b_ * 64:(b_ + 1) * 64]:
                                 e.matmul(o, lhsT=a, rhs=r, start=True, stop=False), reads=[kb, mb], writes=[ob])
                            P.op("pe", lambda e, o=ov, a=Sold, r=q2[:, 0, i * 64:(i + 1) * 64]:
                                 e.matmul(o, lhsT=a, rhs=r, start=False, stop=True), reads=[bSold, qb], writes=[ob])
                            osv, osb = ch["cur_ost"]
                            kk += 1
                            if kk % 2 == 0:
                                P.op("act", lambda e, o=osv[:, i * 64:(i + 1) * 64], s=ov: e.activation(out=o, in_=s, func=AF.Copy),
                                     reads=[ob], writes=[osb[i]])
                            else:
                                P.op("dve", lambda e, o=osv[:, i * 64:(i + 1) * 64], s=ov: e.tensor_copy(out=o, in_=s),
                                     reads=[ob], writes=[osb[i]])
                        uv, ub = psU.next()
                        P.op("pe", lambda e, o=uv, a=ktm, r=vtm: e.matmul(o, lhsT=a, rhs=r, start=True, stop=True),
                             reads=[kb], writes=[ub])
                        if st_ == 0:
                            P.op("dve", lambda e, o=Pnew, s=uv: e.tensor_copy(out=o, in_=s), reads=[ub], writes=[bPnew])
                        else:
                            pc = ch["prev_c"]
                            P.op("dve", lambda e, o=Pnew, a=Pold, s=uv, d=decv[:, pc:pc + 1]:
                                 e.scalar_tensor_tensor(out=o, in0=a, scalar=d, in1=s, op0=ALU.mult, op1=ALU.add),
                                 reads=[bPold, ub, decb], writes=[bPnew])
                        P.op("act", lambda e, o=Snew, a=Pnew, d=decv[:, c_glob:c_glob + 1]:
                             e.activation(out=o, in_=a, func=AF.Copy, scale=d), reads=[bPnew, decb], writes=[bSnew])
                        ch["prev_c"] = c_glob
                        ch["step"] = st_ + 1
                for ch in chains:
                    g = gidx(ch, gi)
                    if g >= 1:
                        hd, dr = ch["hd"], ch["dr"]
                        osv, osb = ch["cur_ost"]
                        dst = (OFW if dr == 0 else OBW)[hd, :, (g - 1) * 256:g * 256]
                        P.op("sp", lambda e, d=dst, s=osv: e.dma_start(out=d, in_=s), reads=osb,
                             writes=[dbuf(("O", dr, hd))], dma=True)

    def stage4():
        ws = WStream()
        sts = supertiles(False)
        for st in sts:
            for m in range(8):
                ws.add(HWOUT[m], ("HWOUT", m), 8, 128)
            ffn_plan(ws, 1, 1)
        inR = Ring([tuple((view(o_stg + (i * 3 + k) * 4 * KB, 1024, F32), Buf()) for k in range(3)) for i in range(2)])
        OST = view(o_hid, 4 * 1024, F32).rearrange("p (b d) -> p b d", d=1024)
        bOST = [Buf() for _ in range(8)]
        ONs = [view(o_cv + i * 16 * KB, 8 * 1024, BF16).rearrange("p (f t) -> p f t", f=8) for i in range(2)]
        bONs = [[[Buf() for _ in range(2)] for _ in range(8)] for _ in range(2)]

        rs4 = Ring([(view(o_rs + i * 2 * KB, 512, F32), Buf()) for i in range(2)] +
                   [(view(o_cv + 32 * KB + i * 2 * KB, 512, F32), Buf()) for i in range(2)])

        def pro(ki):
            (u0, T, nts, w) = sts[ki]
            t0 = u0 - LC
            ON, bON = ONs[ki % 2], bONs[ki % 2]
            for fp in range(4):
                items = []
                for fc in (2 * fp, 2 * fp + 1):
                    (fv, fb), (bv, bb), (gv, gb) = inR.next()
                    P.op("sp", lambda e, d=fv, s=OFW[fc, :, t0:t0 + T]: e.dma_start(out=d, in_=s),
                         reads=[dbuf(("O", 0, fc))], writes=[fb], dma=True)
                    P.op("sp", lambda e, d=bv, s=OBW[fc, :, t0:t0 + T]: e.dma_start(out=d, in_=s),
                         reads=[dbuf(("O", 1, fc))], writes=[bb], dma=True)
                    P.op("sp", lambda e, d=gv, s=SOG[fc, :, t0:t0 + T]: e.dma_start(out=d, in_=s),
                         reads=[dbuf(("SOG", fc))], writes=[gb], dma=True)
                    P.op("pool", lambda e, fv=fv, bv=bv: e.tensor_tensor(out=fv, in0=fv, in1=bv, op=ALU.add),
                         reads=[fb, bb], writes=[fb])
                    for nti, (off, n) in enumerate(nts):
                        items.append(dict(fc=fc, nti=nti, off=off, n=n, fv=fv, fb=fb, gv=gv, gb=gb))
                yield
                yield
                for it in items:
                    k = sq_i[0] % 8
                    sq_i[0] += 1
                    it["k"] = k
                    P.op("act", lambda e, o=SQ[:, k, 0:it["n"]], s=it["fv"][:, it["off"]:it["off"] + it["n"]]:
                         e.activation(out=o, in_=s, func=AF.Square), reads=[it["fb"]], writes=[bSQ[k]])
                yield
                yield
                for it in items:
                    it["ps"] = psR.next()
                    P.op("pe", lambda e, o=it["ps"][0][:, 0:it["n"]], r=SQ[:, it["k"], 0:it["n"]]:
                         e.matmul(o, lhsT=ones_bf, rhs=r, start=True, stop=True), reads=[bSQ[it["k"]], bC], writes=[it["ps"][1]])
                for it in items:
                    it["rs"] = rs4.next()
                    P.op("act", lambda e, o=it["rs"][0][:, 0:it["n"]], i=it["ps"][0][:, 0:it["n"]]:
                         e.activation(out=o, in_=i, func=AF.Ln, scale=1.0 / 128, bias=EPS), reads=[it["ps"][1]], writes=[it["rs"][1]])
                for it in items:
                    P.op("act", lambda e, o=it["rs"][0][:, 0:it["n"]]: e.activation(out=o, in_=o, func=AF.Exp, scale=-0.5),
                         reads=[it["rs"][1]], writes=[it["rs"][1]])
                yield
                for it in items:
                    fc, nti, off, n = it["fc"], it["nti"], it["off"], it["n"]
                    tv, tb = tmpR.next()
                    P.op("dve", lambda e, o=tv[:, 0:n], a=it["fv"][:, off:off + n], r=it["rs"][0][:, 0:n]:
                         e.tensor_tensor(out=o, in0=a, in1=r, op=ALU.mult), reads=[it["fb"], it["rs"][1]], writes=[tb])
                    P.op("dve", lambda e, o=ON[:, fc, off:off + n], a=tv[:, 0:n], g=it["gv"][:, off:off + n], fc=fc:
                         e.scalar_tensor_tensor(out=o, in0=a, scalar=sp_gn[:, fc:fc + 1], in1=g, op0=ALU.mult, op1=ALU.mult),
                         reads=[tb, it["gb"], bC], writes=[bON[fc][nti]])
                yield

        for _ in pro(0):
            pass
        for ki, (u0, T, nts, w) in enumerate(sts):
            t0 = u0 - LC
            load_h(hB, u0, T, nts)
            BG[0] = pro(ki + 1) if ki + 1 < len(sts) else None
            proj_out(ws, nts, 1, 0, ONs[ki % 2], bONs[ki % 2])
            ffn(ws, nts, 1, 1, 0)
            tick(100)
            for nti, (off, n) in enumerate(nts):
                RSTD, bRSTD = sumsq_rstd(nts, nti, [Hh[:, fc, off:off + n] for fc in range(8)], [bH[fc][nti] for fc in range(8)], D, pre=True)
                for fc in range(8):
                    P.op("dve", lambda e, o=Hh[:, fc, off:off + n], fc=fc:
                         e.scalar_tensor_tensor(out=o, in0=o, scalar=sp_fg[:, fc:fc + 1], in1=RSTD[:, 0:n], op0=ALU.mult, op1=ALU.mult),
                         reads=[bH[fc][nti], bRSTD, bC], writes=[bH[fc][nti]])
                hid_all = [bHID[j][q] for j in range(HC) for q in range(2)]
                k = 0
                for blk in range(4):
                    for half in range(2):
                        pv, pb = psR.next()
                        for q in range(4):
                            fc = half * 4 + q
                            P.op("pe", lambda e, o=pv[:, q * 128:(q + 1) * 128], i=Hh[:, fc, off + blk * 128:off + (blk + 1) * 128]:
                                 e.transpose(o, i, ident_f), reads=[bH[fc][nti], bC], writes=[pb])
                        k += 1
                        wr = [bOST[blk * 2 + half]] + (hid_all if (blk == 0 and half == 0) else [])
                        if k % 2 == 0:
                            P.op("act", lambda e, o=OST[:, blk, half * 512:(half + 1) * 512], i=pv: e.activation(out=o, in_=i, func=AF.Copy),
                                 reads=[pb], writes=wr)
                        else:
                            P.op("dve", lambda e, o=OST[:, blk, half * 512:(half + 1) * 512], i=pv: e.tensor_copy(out=o, in_=i),
                                 reads=[pb], writes=wr)
                a0 = t0 + off
                P.op("sp", lambda e, d=out_d[a0:a0 + 512, :].rearrange("(b p) d -> p b d", p=128): e.dma_start(out=d, in_=OST),
                     reads=bOST, writes=[dbuf("out")] + hid_all, dma_key="out")

    if not DBG.get("NOCONV"):
        stage0_pre()
        conv_all()
        stage0()
    P.barrier(skip_pool=True, extra=[last_ci])
    if 1 in stages:
        stage1()
        P.barrier()
    if 2 in stages:
        stage2()
        P.barrier()
    if 3 in stages:
        stage3()
        P.barrier()
    if 4 in stages:
        stage4()
    fk = [k for k in P.dma_keys if k.startswith("auto_sp") or k == "out"]
    P.emit(final_wait_keys=fk)
    return nc


def _fm(v):
    return np.ascontiguousarray(np.asarray(v, np.float32).reshape(8, 128).T)


def make_inputs(b, x, c, ctx, c_ctx, ada_w, ada_b, norm_g, ffn_w_gu, ffn_w_down, conv_w_in, conv_w,
                conv_w_out, hg_w_in, hg_lb_logits, hg_gnorm_g, hg_w_out, final_norm_g):
    sp = np.zeros((128, NSP), np.float32)
    cc = np.stack([_fm(c[b]), _fm(c_ctx)], axis=-1)
    sp[:, 0:16] = cc.reshape(128, 16)
    ng = np.stack([np.stack([_fm(norm_g[l, s]) for s in range(3)], 1) for l in range(2)], 1)
    sp[:, 16:64] = ng.reshape(128, 48)
    cw = np.stack([_fm(conv_w[0, j]) for j in range(3)], 1)
    sp[:, 64:88] = cw.reshape(128, 24)
    lb = np.stack([np.stack([_fm(hg_lb_logits[l, r]) for r in range(2)], 1) for l in range(2)], 1)
    sp[:, 88:120] = lb.reshape(128, 32)
    sp[:, 120:128] = _fm(hg_gnorm_g[0])
    sp[:, 128:136] = _fm(final_norm_g)
    ab = np.stack([np.ascontiguousarray(np.asarray(ada_b[l], np.float32).reshape(72, 128).T) for l in range(2)], 1)
    sp[:, 136:280] = ab.reshape(128, 144)
    return {
        "x": np.ascontiguousarray(x[b]), "ctx": np.ascontiguousarray(ctx[b]), "smallp": sp,
        "ada_w": ada_w, "ffn_w_gu": ffn_w_gu, "ffn_w_down": ffn_w_down,
        "conv_w_in": np.ascontiguousarray(conv_w_in[0]), "conv_w_out": np.ascontiguousarray(conv_w_out[0]),
        "hg_w_in": np.ascontiguousarray(hg_w_in[0]), "hg_w_out": np.ascontiguousarray(hg_w_out[0]),
    }


def kernel(**inputs):
    inputs = {k: np.asarray(v) for k, v in inputs.items()}
    nc = build_program()
    in_maps = [make_inputs(b, **inputs) for b in range(8)]
    res = run_bass_kernel_spmd(nc, in_maps, core_ids=list(range(8)))
    return np.stack([np.asarray(r["out"], np.float32) for r in res.results], axis=0)
```

```python
import types
import numpy as np
import concourse.bass as bass
import concourse.mybir as mybir
from concourse.bass_utils import run_bass_kernel_spmd

F32 = mybir.dt.float32
BF16 = mybir.dt.bfloat16
AF = mybir.ActivationFunctionType
ALU = mybir.AluOpType

D = 1024
FC = 8
FF = 2816
HC = 22
S = 8192
LC = 256
UU = S + LC
EPS = 1e-6
NSP = 16 + 48 + 24 + 32 + 8 + 8 + 144
ENGS = ("pe", "act", "dve", "pool", "sp")
DBG = {}
N_ACT_CONV = 38


def _freeze(fn):
    if fn.__closure__ is None:
        return fn
    cells = []
    for c in fn.__closure__:
        try:
            cells.append(types.CellType(c.cell_contents))
        except ValueError:
            cells.append(c)
    return types.FunctionType(fn.__code__, fn.__globals__, fn.__name__, fn.__defaults__, tuple(cells))


class Buf:
    __slots__ = ("name", "last_w", "readers")

    def __init__(self, name=""):
        self.name = name
        self.last_w = None
        self.readers = []


class Op:
    __slots__ = ("eng", "fn", "idx", "deps", "signal", "dma_key", "dma_cnt")


class Prog:
    def __init__(self, nc, n_auto=24):
        self.nc = nc
        self.ops = {e: [] for e in ENGS}
        self.order = []
        self.dma_keys = {}
        self.dma_last = {}
        self.pending = {e: [] for e in ENGS}
        self.n_auto = n_auto
        self.auto_i = {e: 0 for e in ENGS}

    def op(self, eng, fn, reads=(), writes=(), dma=False, dma_key=None):
        o = Op()
        o.eng = eng
        o.fn = _freeze(fn)
        o.idx = len(self.ops[eng])
        o.signal = False
        if dma and dma_key is None:
            dma_key = "auto_%s_%d" % (eng, self.auto_i[eng] % self.n_auto)
            self.auto_i[eng] += 1
        o.dma_key = dma_key
        o.dma_cnt = 0
        deps = list(self.pending[eng])
        self.pending[eng] = []
        for b in reads:
            if b.last_w is not None:
                deps.append(b.last_w)
        for b in writes:
            if b.last_w is not None:
                deps.append(b.last_w)
            deps.extend(b.readers)
        if dma_key is not None:
            prev = self.dma_last.get(dma_key)
            if prev is not None:
                deps.append(prev)
            self.dma_last[dma_key] = o
            c = self.dma_keys.get(dma_key, 0) + 1
            self.dma_keys[dma_key] = c
            o.dma_cnt = c
        red = {}
        for d in deps:
            if d is o:
                continue
            if eng == "pe" and d.eng == "pe" and d.dma_key is None:
                continue
            if d.dma_key is not None:
                k = ("dma", d.dma_key)
                v = d.dma_cnt
            else:
                k = ("eng", d.eng)
                v = d.idx
            if k not in red or red[k][0] < v:
                red[k] = (v, d)
        o.deps = red
        for b in writes:
            b.last_w = o
            b.readers = []
        for b in reads:
            if b not in writes:
                b.readers.append(o)
                if len(b.readers) > 48:
                    keep = {}
                    for r in b.readers:
                        k = ("dma", r.dma_key) if r.dma_key is not None else ("eng", r.eng)
                        keep[k] = r
                    b.readers = list(keep.values())
        self.ops[eng].append(o)
        self.order.append(o)
        return o

    def barrier(self, skip_pool=False, extra=()):
        lasts = list(extra)
        for e in ENGS:
            if skip_pool and e == "pool":
                continue
            for o in reversed(self.ops[e]):
                if o.dma_key is None:
                    lasts.append(o)
                    break
        for k, o in self.dma_last.items():
            if skip_pool and k.startswith("cv"):
                continue
            lasts.append(o)
        for e in ENGS:
            if skip_pool and e == "pool":
                continue
            self.pending[e].extend(lasts)

    def emit(self, final_wait_keys=()):
        nc = self.nc
        for o in self.order:
            for (k, (v, d)) in o.deps.items():
                if k[0] == "eng":
                    d.signal = True
        esem = {e: nc.alloc_semaphore(name="sem_%s" % e) for e in ENGS}
        dsem = {k: nc.alloc_semaphore(name="dsem_%d" % i) for i, k in enumerate(self.dma_keys)}
        cnts = {}
        for e in ENGS:
            c = 0
            arr = []
            for o in self.ops[e]:
                if o.signal and o.dma_key is None:
                    c += 1
                arr.append(c)
            cnts[e] = arr

        def run_engine(e, engobj):
            waited = {}
            for o in self.ops[e]:
                for (k, (v, d)) in o.deps.items():
                    if k[0] == "eng":
                        sem = esem[k[1]]
                        val = cnts[k[1]][v]
                    else:
                        sem = dsem[k[1]]
                        val = 16 * v
                    if waited.get(k, 0) >= val:
                        continue
                    engobj.wait_ge(sem, val)
                    waited[k] = val
                ins = o.fn(engobj)
                if o.dma_key is not None:
                    ins.then_inc(dsem[o.dma_key], 16)
                elif o.signal:
                    ins.then_inc(esem[e], 1)
            if e == "sp":
                for k in final_wait_keys:
                    engobj.wait_ge(dsem[k], 16 * self.dma_keys[k])

        with nc.Block() as block:
            @block.tensor
            def _(eng):
                run_engine("pe", eng)

            @block.scalar
            def _(eng):
                run_engine("act", eng)

            @block.vector
            def _(eng):
                run_engine("dve", eng)

            @block.gpsimd
            def _(eng):
                run_engine("pool", eng)

            @block.sync
            def _(eng):
                run_engine("sp", eng)


class Ring:
    def __init__(self, items):
        self.items = items
        self.i = 0

    def next(self):
        it = self.items[self.i % len(self.items)]
        self.i += 1
        return it


def build_program(debug=(), stages=(0, 1, 2, 3, 4)):
    nc = bass.Bass("TRN2", target_bir_lowering=False)
    P = Prog(nc)

    def din(name, shape):
        return nc.dram_tensor(name, shape, F32, kind="ExternalInput").ap()

    x_d = din("x", [S, D])
    ctx_d = din("ctx", [LC, D])
    sp_d = din("smallp", [128, NSP])
    adaw_d = din("ada_w", [2, D, 9 * D])
    wgu_d = din("ffn_w_gu", [2, 2, D, 2 * FF])
    wdn_d = din("ffn_w_down", [2, 2, FF, D])
    cwin_d = din("conv_w_in", [D, 3 * D])
    cwout_d = din("conv_w_out", [D, D])
    hwin_d = din("hg_w_in", [D, 5 * D])
    hwout_d = din("hg_w_out", [D, D])
    out_d = nc.dram_tensor("out", [S, D], F32, kind="ExternalOutput").ap()

    def scr(name, shape, dt=F32):
        kind = "ExternalOutput" if name in debug else "Internal"
        return nc.dram_tensor(name, shape, dt, kind=kind).ap()

    WGU = [[scr("WGU%d%d" % (l, h), [HC, 128, 8, 256], BF16) for h in range(2)] for l in range(2)]
    WDN = [[scr("WDN%d%d" % (l, h), [8, 128, HC, 128], BF16) for h in range(2)] for l in range(2)]
    CWIN = scr("CWIN", [8, 128, 8, 384], BF16)
    CWOUT = scr("CWOUT", [8, 128, 8, 128], BF16)
    HWIN = scr("HWIN", [10, 128, 8, 512], BF16)
    HWOUT = scr("HWOUT", [8, 128, 8, 128], BF16)
    hA = scr("hA", [8, 128, UU])
    zA = scr("zA", [8, 128, UU])
    bgA = scr("bgA", [8, 128, UU])
    hB = scr("hB", [8, 128, UU])
    QK = scr("QK", [8, 2, 128, 2, UU], BF16)
    KVT = scr("KVT", [8, 2, UU, 2, 128], BF16)
    DEC = scr("DEC", [8, 2, 128, 132])
    SOG = scr("SOG", [8, 128, S])
    OFW = scr("OFW", [8, 128, S])
    OBW = scr("OBW", [8, 128, S])
    dbufs = {}

    def dbuf(key):
        if key not in dbufs:
            dbufs[key] = Buf(str(key))
        return dbufs[key]

    AR_BYTES = 204 * 1024
    arena = nc.alloc_sbuf_tensor("arena", [128, AR_BYTES // 2], BF16).ap()

    def view(off, nelem, dt):
        assert off % 32 == 0
        if dt == F32:
            assert off + 4 * nelem <= AR_BYTES, (off, nelem)
            return arena[:, off // 2: off // 2 + 2 * nelem].bitcast(F32)
        assert off + 2 * nelem <= AR_BYTES, (off, nelem)
        return arena[:, off // 2: off // 2 + nelem]

    KB = 1024
    o_const = 0
    o_h = 8 * KB
    o_xn = o_h + 32 * KB
    o_hid = o_xn + 16 * KB
    o_wr = o_hid + 44 * KB
    NSLOT = 3
    o_sq = o_wr + NSLOT * 8 * KB
    o_tmp = o_sq + 8 * KB
    o_rs = o_tmp + 8 * KB
    o_stg = o_rs + 4 * KB
    o_cv = o_stg + 24 * KB

    c_off = [o_const]

    def calloc(nelem, dt):
        sz = nelem * (4 if dt == F32 else 2)
        sz = (sz + 31) // 32 * 32
        v = view(c_off[0], nelem, dt)
        c_off[0] += sz
        assert c_off[0] <= o_h
        return v

    ones_bf = calloc(128, BF16)
    ident_f = calloc(128, F32)
    ident_bf = calloc(128, BF16)
    tri_f = calloc(128, F32)
    tri_b = calloc(128, F32)
    mask01 = calloc(512, F32)
    maskF = calloc(64, F32)
    maskB = calloc(64, F32)
    smallp = calloc(NSP, F32)
    modT = calloc(2 * 72 * 2, F32).rearrange("p (l c w) -> p l c w", l=2, w=2)
    gsT = calloc(2 * 3 * 8 * 2, F32).rearrange("p (l s f w) -> p l s f w", l=2, s=3, w=2)
    gtT = calloc(2 * 3 * 8 * 2, F32).rearrange("p (l s f w) -> p l s f w", l=2, s=3, w=2)
    cs_t = calloc(16, F32).rearrange("p (k w) -> p k w", w=2)
    lbT = calloc(16, F32).rearrange("p (r f) -> p r f", f=8)
    omlT = calloc(16, F32).rearrange("p (r f) -> p r f", f=8)
    lbtmp = calloc(16, F32).rearrange("p (r f) -> p r f", f=8)
    bC = Buf("const")

    sp_cc = smallp[:, 0:16].rearrange("p (k w) -> p k w", w=2)
    sp_ng = smallp[:, 16:64].rearrange("p (l s f) -> p l s f", l=2, s=3)
    sp_cw = smallp[:, 64:88].rearrange("p (j f) -> p j f", j=3)
    sp_lb = smallp[:, 88:120].rearrange("p (l r f) -> p l r f", l=2, r=2)
    sp_gn = smallp[:, 120:128]
    sp_fg = smallp[:, 128:136]
    sp_ab = smallp[:, 136:280].rearrange("p (l c) -> p l c", l=2)

    Hh = view(o_h, 8 * 1024, F32).rearrange("p (f t) -> p f t", f=8)
    XN = view(o_xn, 8 * 1024, BF16).rearrange("p (f t) -> p f t", f=8)
    HID = view(o_hid, HC * 1024, BF16).rearrange("p (j t) -> p j t", j=HC)
    bH = [[Buf() for _ in range(2)] for _ in range(8)]
    bXN = [[Buf() for _ in range(2)] for _ in range(8)]
    bHID = [[Buf() for _ in range(2)] for _ in range(HC)]
    SQ = view(o_sq, 8 * 512, BF16).rearrange("p (f t) -> p f t", f=8)
    bSQ = [Buf() for _ in range(8)]
    tmpR = Ring([(view(o_tmp + i * 2 * KB, 512, F32), Buf()) for i in range(4)])
    RT = view(o_rs, 512, F32)
    RSTD = view(o_rs + 2 * KB, 512, F32)
    bRT = Buf()
    bRSTD = Buf()
    sq_i = [0]

    psb = [nc.alloc_psum_tensor("ps%d" % i, [128, 512], F32).ap() for i in range(7)]
    psR = Ring([(psb[i], Buf()) for i in range(6)])
    ps_ss = psb[6]
    b_ss = Buf()
    ps_bf = nc.alloc_psum_tensor("psbf", [128, 1024], BF16).ap()
    b_psbf = Buf()

    def sl(ap, a, n):
        return ap[:, a:a + n]

    wslots = [(view(o_wr + i * 8 * KB, 4096, BF16), Buf()) for i in range(NSLOT)]

    class WStream:
        def __init__(self):
            self.plan = []
            self.issued = 0
            self.cur = 0

        def add(self, dram_ap, dkey, kc, w):
            self.plan.append((dram_ap, dkey, kc, w))

        def _issue(self, i):
            dram_ap, dkey, kc, w = self.plan[i]
            slot, sb = wslots[i % NSLOT]
            dst = slot[:, 0:kc * w].rearrange("p (k c) -> p k c", c=w)
            P.op("sp", lambda e, d=dst, s=dram_ap: e.dma_start(out=d, in_=s),
                 reads=[dbuf(dkey)], writes=[sb], dma_key="wr%d" % (i % NSLOT))

        def get(self):
            i = self.cur
            while self.issued < min(len(self.plan), i + NSLOT):
                self._issue(self.issued)
                self.issued += 1
            dram_ap, dkey, kc, w = self.plan[i]
            slot, sb = wslots[i % NSLOT]
            self.cur += 1
            return slot[:, 0:kc * w].rearrange("p (k c) -> p k c", c=w), sb

    P.op("sp", lambda e: e.dma_start(out=smallp, in_=sp_d), writes=[bC], dma=True)
    P.op("pool", lambda e: e.memset(ones_bf, 1.0), writes=[bC])
    P.op("pool", lambda e: e.memset(ident_f, 1.0), writes=[bC])
    P.op("pool", lambda e: e.affine_select(out=ident_f, in_=ident_f, pattern=[[-1, 128]],
                                           compare_op=ALU.is_equal, fill=0.0, base=0, channel_multiplier=1),
         reads=[bC], writes=[bC])
    P.op("pool", lambda e: e.tensor_copy(out=ident_bf, in_=ident_f), reads=[bC], writes=[bC])
    P.op("pool", lambda e: e.memset(tri_f, 1.0), writes=[bC])
    P.op("pool", lambda e: e.affine_select(out=tri_f, in_=tri_f, pattern=[[1, 128]],
                                           compare_op=ALU.is_ge, fill=0.0, base=0, channel_multiplier=-1),
         reads=[bC], writes=[bC])
    P.op("pool", lambda e: e.memset(tri_b, 1.0), writes=[bC])
    P.op("pool", lambda e: e.affine_select(out=tri_b, in_=tri_b, pattern=[[-1, 128]],
                                           compare_op=ALU.is_ge, fill=0.0, base=0, channel_multiplier=1),
         reads=[bC], writes=[bC])
    for hh in range(2):
        P.op("pool", lambda e, hh=hh: e.tensor_copy(out=maskF[hh * 64:(hh + 1) * 64, :],
                                                    in_=tri_f[hh * 64:(hh + 1) * 64, hh * 64:(hh + 1) * 64]),
             reads=[bC], writes=[bC])
        P.op("pool", lambda e, hh=hh: e.tensor_copy(out=maskB[hh * 64:(hh + 1) * 64, :],
                                                    in_=tri_b[hh * 64:(hh + 1) * 64, hh * 64:(hh + 1) * 64]),
             reads=[bC], writes=[bC])
    P.op("pool", lambda e: e.memset(mask01, 1.0), writes=[bC])
    last_ci = P.op("pool", lambda e: e.memset(mask01.rearrange("p (c j) -> p c j", j=64)[:, :, 0:1], 0.0),
                   reads=[bC], writes=[bC])

    cvf = Ring([(view(o_cv + i * 12 * KB, 3072, F32), Buf()) for i in range(2)])
    cvb = Ring([(view(o_cv + 24 * KB + i * 6 * KB, 3072, BF16), Buf()) for i in range(2)])
    cv_n = [0]

    def conv_piece(src2d, kc, segs, dst_ap, dkey):
        i = cv_n[0]
        cv_n[0] += 1
        fv, fb = cvf.next()
        bv, bb = cvb.next()
        wtot = sum(w for _, w in segs)
        f3 = fv[:, 0:kc * wtot].rearrange("p (k c) -> p k c", c=wtot)
        b3 = bv[:, 0:kc * wtot].rearrange("p (k c) -> p k c", c=wtot)
        src3 = src2d.rearrange("(k p) n -> p k n", p=128)
        c = 0
        for si, (c0, w) in enumerate(segs):
            P.op("act" if i < N_ACT_CONV else "pool", lambda e, d=f3[:, :, c:c + w], s=src3[:, :, c0:c0 + w]: e.dma_start(out=d, in_=s),
                 writes=[fb], dma_key="cvl%d_%d" % (i % 2, si))
            c += w
        P.op("pool", lambda e: e.tensor_copy(out=b3, in_=f3), reads=[fb], writes=[bb])
        P.op("pool", lambda e: e.dma_start(out=dst_ap, in_=b3), reads=[bb], writes=[dbuf(dkey)],
             dma_key="cvs%d" % (i % 2))

    def conv_ffn(l, h):
        for j in range(HC):
            conv_piece(wgu_d[l, h], 8, [(j * 128, 128), (FF + j * 128, 128)], WGU[l][h][j], ("WGU", l, h, j))
        for m in range(8):
            conv_piece(wdn_d[l, h], HC, [(m * 128, 128)], WDN[l][h][m], ("WDN", l, h, m))

    def conv_all():
        conv_ffn(0, 0)
        for m in range(8):
            conv_piece(cwin_d, 8, [(m * 128, 128), (D + m * 128, 128), (2 * D + m * 128, 128)], CWIN[m], ("CWIN", m))
        for m in range(8):
            conv_piece(cwout_d, 8, [(m * 128, 128)], CWOUT[m], ("CWOUT", m))
        conv_ffn(0, 1)
        conv_ffn(1, 0)
        for hd in range(8):
            conv_piece(hwin_d, 8, [(hd * 128, 128), (2 * D + hd * 128, 128)], HWIN[hd][:, :, 0:256], ("HWIN", hd))
            conv_piece(hwin_d, 8, [(3 * D + hd * 128, 128), (4 * D + hd * 128, 128)], HWIN[hd][:, :, 256:512], ("HWIN", hd))
        for hf in range(2):
            conv_piece(hwin_d, 8, [(D + hf * 512, 256)], HWIN[8 + hf][:, :, 0:256], ("HWIN", 8 + hf))
            conv_piece(hwin_d, 8, [(D + hf * 512 + 256, 256)], HWIN[8 + hf][:, :, 256:512], ("HWIN", 8 + hf))
        for m in range(8):
            conv_piece(hwout_d, 8, [(m * 128, 128)], HWOUT[m], ("HWOUT", m))
        conv_ffn(1, 1)

    def stage0_pre():
        P.op("act", lambda e: e.activation(out=cs_t, in_=sp_cc, func=AF.Silu), reads=[bC], writes=[bC])
        P.op("dve", lambda e: e.tensor_tensor(out=lbtmp, in0=sp_lb[:, 1], in1=sp_lb[:, 0], op=ALU.subtract),
             reads=[bC], writes=[bC])
        P.op("act", lambda e: e.activation(out=lbT, in_=lbtmp, func=AF.Sigmoid), reads=[bC], writes=[bC])
        P.op("dve", lambda e: e.tensor_scalar(out=omlT, in0=lbT, scalar1=-1.0, scalar2=1.0, op0=ALU.mult, op1=ALU.add),
             reads=[bC], writes=[bC])

    def stage0():
        mT = view(o_h, 9 * D, F32)
        bmT = Buf()
        adaR = Ring([(view(o_hid + i * 16 * KB, 4096, F32).rearrange("p (k c) -> p k c", c=512), Buf())
                     for i in range(2)])
        for l in range(2):
            src3 = adaw_d[l].rearrange("(k p) n -> p k n", p=128)
            for cb in range(18):
                av, ab = adaR.next()
                P.op("sp", lambda e, d=av, s=src3[:, :, cb * 512:(cb + 1) * 512]: e.dma_start(out=d, in_=s),
                     writes=[ab], dma=True)
                pv, pb = psR.next()
                for kc in range(8):
                    P.op("pe", lambda e, o=pv[0:2, :], a=cs_t[:, kc, :], r=av[:, kc, :], kc=kc:
                         e.matmul(o, lhsT=a, rhs=r, start=(kc == 0), stop=(kc == 7)),
                         reads=[ab, bC], writes=[pb])
                P.op("dve", lambda e, o=mT[0:2, cb * 512:(cb + 1) * 512], i=pv[0:2, :]: e.tensor_copy(out=o, in_=i),
                     reads=[pb], writes=[bmT])
            pv, pb = psR.next()
            for ch in range(72):
                P.op("pe", lambda e, o=pv[:, ch * 2:ch * 2 + 2], i=mT[0:2, ch * 128:(ch + 1) * 128]:
                     e.transpose(o, i, ident_f[0:2, 0:2]), reads=[bmT, bC], writes=[pb])
            bias_b = bass.AP(sp_ab.tensor, sp_ab[:, l].offset, [list(sp_ab.ap[0]), [1, 72], [0, 2]])
            P.op("dve", lambda e, o=modT[:, l], i=pv[:, 0:144].rearrange("p (c w) -> p c w", w=2), b=bias_b:
                 e.tensor_tensor(out=o, in0=i, in1=b, op=ALU.add), reads=[pb, bC], writes=[bC])
            for s in range(3):
                g_b = bass.AP(sp_ng.tensor, sp_ng[:, l, s].offset, [list(sp_ng.ap[0]), [1, 8], [0, 2]])
                P.op("dve", lambda e, o=gsT[:, l, s], i=modT[:, l, (3 * s + 1) * 8:(3 * s + 2) * 8], g=g_b:
                     e.scalar_tensor_tensor(out=o, in0=i, scalar=1.0, in1=g, op0=ALU.add, op1=ALU.mult),
                     reads=[bC], writes=[bC])
                P.op("dve", lambda e, o=gtT[:, l, s], i=modT[:, l, (3 * s + 2) * 8:(3 * s + 3) * 8], s=s:
                     e.tensor_scalar(out=o, in0=i, scalar1=(1.0 if s == 1 else 0.5), scalar2=None, op0=ALU.mult),
                     reads=[bC], writes=[bC])

    def m_gs(l, s, fc, w):
        return gsT[:, l, s, fc, w:w + 1]

    def m_sh(l, s, fc, w):
        return modT[:, l, 3 * s * 8 + fc, w:w + 1]

    def m_gt(l, s, fc, w):
        return gtT[:, l, s, fc, w:w + 1]

    rsR = Ring([(view(o_rs + i * 2 * KB, 512, F32), Buf()) for i in range(2)])
    ssacc = [(ps_ss, b_ss), (ps_bf.bitcast(F32), b_psbf)]
    sqacc_i = [0]

    def preacc_square(m, nti, off, n):
        k = sqacc_i[0] % 8
        sqacc_i[0] += 1
        P.op("act", lambda e, o=SQ[:, k, 0:n], s=Hh[:, m, off:off + n]: e.activation(out=o, in_=s, func=AF.Square),
             reads=[bH[m][nti]], writes=[bSQ[k]])
        return (k, m, nti, n)

    def preacc_mm(pend):
        for (k, m, nti, n) in pend:
            av, ab = ssacc[nti]
            P.op("pe", lambda e, o=av[:, 0:n], r=SQ[:, k, 0:n]: e.matmul(o, lhsT=ones_bf, rhs=r, start=(m == 0), stop=(m == 7)),
                 reads=[bSQ[k], bC], writes=[ab])


    def sumsq_rstd(nts, nti, srcs, src_bufs, nfeat, pre=False):
        off, n = nts[nti]
        if pre:
            pv, pb = ssacc[nti]
            rv, rb = rsR.next()
            P.op("act", lambda e: e.activation(out=rv[:, 0:n], in_=pv[:, 0:n], func=AF.Ln, scale=1.0 / nfeat, bias=EPS),
                 reads=[pb], writes=[rb])
            P.op("act", lambda e: e.activation(out=rv[:, 0:n], in_=rv[:, 0:n], func=AF.Exp, scale=-0.5), reads=[rb], writes=[rb])
            return rv, rb
        nsrc = len(srcs)
        sqi = []
        for i in range(nsrc):
            k = sq_i[0] % 8 if nsrc == 1 else i
            sq_i[0] += 1
            sqi.append(k)
            P.op("act", lambda e, o=SQ[:, k, 0:n], s=srcs[i]: e.activation(out=o, in_=s, func=AF.Square),
                 reads=[src_bufs[i]], writes=[bSQ[k]])
        pv, pb = psR.next()
        for i in range(nsrc):
            k = sqi[i]
            P.op("pe", lambda e, o=pv[:, 0:n], r=SQ[:, k, 0:n], i=i:
                 e.matmul(o, lhsT=ones_bf, rhs=r, start=(i == 0), stop=(i == nsrc - 1)),
                 reads=[bSQ[k], bC], writes=[pb])
        rv, rb = rsR.next()
        P.op("act", lambda e: e.activation(out=rv[:, 0:n], in_=pv[:, 0:n], func=AF.Ln, scale=1.0 / nfeat, bias=EPS),
             reads=[pb], writes=[rb])
        P.op("act", lambda e: e.activation(out=rv[:, 0:n], in_=rv[:, 0:n], func=AF.Exp, scale=-0.5), reads=[rb], writes=[rb])
        return rv, rb

    def norm_mod(nts, l, s, w, pre=True):
        for nti, (off, n) in enumerate(nts):
            RSTD, bRSTD = sumsq_rstd(nts, nti, [Hh[:, fc, off:off + n] for fc in range(8)], [bH[fc][nti] for fc in range(8)], D, pre=pre)
            for fc in range(8):
                tv, tb = tmpR.next()
                P.op("dve", lambda e, o=tv[:, 0:n], a=Hh[:, fc, off:off + n]:
                     e.tensor_tensor(out=o, in0=a, in1=RSTD[:, 0:n], op=ALU.mult),
                     reads=[bH[fc][nti], bRSTD], writes=[tb])
                P.op("act", lambda e, o=XN[:, fc, off:off + n], i=tv[:, 0:n], fc=fc:
                     e.activation(out=o, in_=i, func=AF.Identity, scale=m_gs(l, s, fc, w), bias=m_sh(l, s, fc, w)),
                     reads=[tb, bC], writes=[bXN[fc][nti]])

    BG = [None]

    def tick(nmax=1):
        for _ in range(nmax):
            g = BG[0]
            if g is None:
                return
            try:
                next(g)
            except StopIteration:
                BG[0] = None

    def ffn(ws, nts, l, hf, w, pre=True):
        s = 0 if hf == 0 else 2
        norm_mod(nts, l, s, w, pre=pre)
        for j in range(HC):
            wv, wb = ws.get()
            for nti, (off, n) in enumerate(nts):
                pg, bg_ = psR.next()
                pu, bu_ = psR.next()
                for kc in range(8):
                    P.op("pe", lambda e, o=pg[:, 0:n], a=wv[:, kc, 0:128], r=XN[:, kc, off:off + n], kc=kc:
                         e.matmul(o, lhsT=a, rhs=r, start=(kc == 0), stop=(kc == 7)),
                         reads=[wb, bXN[kc][nti]], writes=[bg_])
                for kc in range(8):
                    P.op("pe", lambda e, o=pu[:, 0:n], a=wv[:, kc, 128:256], r=XN[:, kc, off:off + n], kc=kc:
                         e.matmul(o, lhsT=a, rhs=r, start=(kc == 0), stop=(kc == 7)),
                         reads=[wb, bXN[kc][nti]], writes=[bu_])
                tv, tb = tmpR.next()
                P.op("act", lambda e, o=tv[:, 0:n], i=pg[:, 0:n]: e.activation(out=o, in_=i, func=AF.Silu),
                     reads=[bg_], writes=[tb])
                P.op("dve", lambda e, o=HID[:, j, off:off + n], a=tv[:, 0:n], b=pu[:, 0:n]:
                     e.tensor_tensor(out=o, in0=a, in1=b, op=ALU.mult),
                     reads=[tb, bu_], writes=[bHID[j][nti]])
            tick()
        pend = []
        for m in range(8):
            wv, wb = ws.get()
            pend_new = []
            for nti, (off, n) in enumerate(nts):
                py, by_ = psR.next()
                for j in range(HC):
                    P.op("pe", lambda e, o=py[:, 0:n], a=wv[:, j, :], r=HID[:, j, off:off + n], j=j:
                         e.matmul(o, lhsT=a, rhs=r, start=(j == 0), stop=(j == HC - 1)),
                         reads=[wb, bHID[j][nti]], writes=[by_])
                P.op("dve", lambda e, o=Hh[:, m, off:off + n], i=py[:, 0:n], m=m:
                     e.scalar_tensor_tensor(out=o, in0=i, scalar=m_gt(l, s, m, w), in1=o, op0=ALU.mult, op1=ALU.add),
                     reads=[by_, bC, bH[m][nti]], writes=[bH[m][nti]])
                pend_new.append(preacc_square(m, nti, off, n))
            preacc_mm(pend)
            pend = pend_new
            tick()
        preacc_mm(pend)

    def ffn_plan(ws, l, hf):
        for j in range(HC):
            ws.add(WGU[l][hf][j], ("WGU", l, hf, j), 8, 256)
        for m in range(8):
            ws.add(WDN[l][hf][m], ("WDN", l, hf, m), HC, 128)

    def proj_out(ws, nts, l, w, SRC=None, bSRC=None):
        SRC = XN if SRC is None else SRC
        bSRC = bXN if bSRC is None else bSRC
        pend = []
        for m in range(8):
            wv, wb = ws.get()
            pend_new = []
            for nti, (off, n) in enumerate(nts):
                py, by_ = psR.next()
                for kc in range(8):
                    P.op("pe", lambda e, o=py[:, 0:n], a=wv[:, kc, :], r=SRC[:, kc, off:off + n], kc=kc:
                         e.matmul(o, lhsT=a, rhs=r, start=(kc == 0), stop=(kc == 7)),
                         reads=[wb, bSRC[kc][nti]], writes=[by_])
                P.op("dve", lambda e, o=Hh[:, m, off:off + n], i=py[:, 0:n], m=m:
                     e.scalar_tensor_tensor(out=o, in0=i, scalar=m_gt(l, 1, m, w), in1=o, op0=ALU.mult, op1=ALU.add),
                     reads=[by_, bC, bH[m][nti]], writes=[bH[m][nti]])
                pend_new.append(preacc_square(m, nti, off, n))
            preacc_mm(pend)
            pend = pend_new
        preacc_mm(pend)

    def supertiles(with_ctx):
        sts = []
        for k in range(8):
            sts.append((LC + k * 1024, 1024, [(0, 512), (512, 512)], 0))
        if with_ctx:
            sts.append((0, 256, [(0, 256)], 1))
        return sts

    def load_h(src, u0, T, nts):
        for nti, (off, n) in enumerate(nts):
            P.op("sp", lambda e: e.dma_start(out=Hh[:, :, off:off + n], in_=src.rearrange("f p u -> p f u")[:, :, u0 + off:u0 + off + n]),
                 reads=[dbuf((id(src), u0))], writes=[bH[fc][nti] for fc in range(8)], dma=True)

    def store_h(dst, u0, T, nts):
        P.op("sp", lambda e: e.dma_start(out=dst.rearrange("f p u -> p f u")[:, :, u0:u0 + T], in_=Hh[:, :, 0:T]),
             reads=[bH[fc][nti] for fc in range(8) for nti in range(len(nts))], writes=[dbuf((id(dst), u0))], dma=True)

    def stage1():
        ws = WStream()
        sts = supertiles(True)
        for st in sts:
            ffn_plan(ws, 0, 0)
            for m in range(8):
                ws.add(CWIN[m], ("CWIN", m), 8, 384)
        XT = view(o_hid, 8 * 1024, F32).rearrange("p (b d) -> p b d", d=1024)
        bXT = Buf()
        bXTs = [Buf(), Buf()]
        stgR = Ring([(view(o_stg + i * 2 * KB, 512, F32), Buf()) for i in range(6)])
        for (u0, T, nts, w) in sts:
            nb = T // 128
            src = ctx_d if w else x_d[u0 - LC:u0 - LC + T]
            for nti, (off, n) in enumerate(nts):
                P.op("sp", lambda e, s=src[off:off + n]: e.dma_start(out=XT[:, off // 128:(off + n) // 128, :], in_=s.rearrange("(b p) d -> p b d", p=128)),
                     writes=[bXTs[nti]] + ([bHID[j][q] for j in range(HC) for q in range(2)] if nti == 0 else []), dma=True)
            k = 0
            for nti, (off, n) in enumerate(nts):
                for fc in range(8):
                    pv, pb = psR.next()
                    for bq in range(n // 128):
                        blk = off // 128 + bq
                        P.op("pe", lambda e, o=pv[:, bq * 128:(bq + 1) * 128], i=XT[:, blk, fc * 128:(fc + 1) * 128]:
                             e.transpose(o, i, ident_f), reads=[bXTs[nti], bC], writes=[pb])
                    eng = "act" if k % 2 == 0 else "dve"
                    k += 1
                    if eng == "act":
                        P.op("act", lambda e, o=Hh[:, fc, off:off + n], i=pv[:, 0:n]: e.activation(out=o, in_=i, func=AF.Copy),
                             reads=[pb], writes=[bH[fc][nti]])
                    else:
                        P.op("dve", lambda e, o=Hh[:, fc, off:off + n], i=pv[:, 0:n]: e.tensor_copy(out=o, in_=i),
                             reads=[pb], writes=[bH[fc][nti]])
            for j in range(HC):
                for nti in range(2):
                    bHID[j][nti].readers.extend(bXTs[0].readers + bXTs[1].readers)
            ffn(ws, nts, 0, 0, w, pre=False)
            store_h(hA, u0, T, nts)
            norm_mod(nts, 0, 1, w)
            for m in range(8):
                wv, wb = ws.get()
                for nti, (off, n) in enumerate(nts):
                    pB, bB = psR.next()
                    pC, bCc = psR.next()
                    pV, bV = psR.next()
                    for (pp, bb, c0) in ((pB, bB, 0), (pC, bCc, 128), (pV, bV, 256)):
                        for kc in range(8):
                            P.op("pe", lambda e, o=pp[:, 0:n], a=wv[:, kc, c0:c0 + 128], r=XN[:, kc, off:off + n], kc=kc:
                                 e.matmul(o, lhsT=a, rhs=r, start=(kc == 0), stop=(kc == 7)),
                                 reads=[wb, bXN[kc][nti]], writes=[bb])
                    tv, tb = tmpR.next()
                    P.op("act", lambda e, o=tv[:, 0:n], i=pC[:, 0:n]: e.activation(out=o, in_=i, func=AF.Copy),
                         reads=[bCc], writes=[tb])
                    zv, zb = stgR.next()
                    P.op("dve", lambda e, o=zv[:, 0:n], a=tv[:, 0:n], b=pV[:, 0:n]: e.tensor_tensor(out=o, in0=a, in1=b, op=ALU.mult),
                         reads=[tb, bV], writes=[zb])
                    P.op("sp", lambda e, d=zA[m, :, u0 + off:u0 + off + n], s=zv[:, 0:n]: e.dma_start(out=d, in_=s),
                         reads=[zb], writes=[dbuf(("zA", m))], dma=True)
                    gv, gb = stgR.next()
                    P.op("act", lambda e, o=gv[:, 0:n], i=pB[:, 0:n]: e.activation(out=o, in_=i, func=AF.Copy),
                         reads=[bB], writes=[gb])
                    P.op("sp", lambda e, d=bgA[m, :, u0 + off:u0 + off + n], s=gv[:, 0:n]: e.dma_start(out=d, in_=s),
                         reads=[gb], writes=[dbuf(("bgA", m))], dma=True)

    def stage2():
        ws = WStream()
        sts = supertiles(True)
        for st in sts:
            for m in range(8):
                ws.add(CWOUT[m], ("CWOUT", m), 8, 128)
            ffn_plan(ws, 0, 1)
            ffn_plan(ws, 1, 0)
            for c in (8, 9, 0, 1, 2, 3, 4, 5, 6, 7):
                ws.add(HWIN[c], ("HWIN", c), 8, 512)
        zinR = Ring([(view(o_stg + i * 4608, 1152, F32), Buf()) for i in range(2)])
        bginR = Ring([(view(o_stg + 9216 + i * 4096, 1024, F32), Buf()) for i in range(2)])
        ZC = view(o_stg + 9216 + 8192, 1024, F32)
        bZC = Buf()
        bZCh = [Buf(), Buf()]
        hoff = [o_hid, o_wr]

        def halloc(nelem, dt):
            v = view(hoff[0], nelem, dt)
            hoff[0] += (nelem * (4 if dt == F32 else 2) + 31) // 32 * 32
            assert hoff[0] <= hoff[1], (hoff, nelem)
            return v

        def mkset(bw):
            d = dict(e=(halloc(512, F32), Buf()), la=(halloc(512, F32), Buf()), lb=(halloc(512, F32), Buf()),
                     g=(halloc(512, F32), Buf()), sk=(halloc(512, BF16), Buf()))
            if bw:
                d["g2"] = (halloc(512, F32), Buf())
            return d
        fsets = [mkset(False), mkset(False)]
        bsets = [mkset(True), mkset(True)]
        s_og = Ring([(halloc(512, F32), Buf()) for _ in range(2)])
        hg_bufs = [ts[k][1] for ts in fsets + bsets for k in ts] + [b for _, b in s_og.items]
        hoff[0], hoff[1] = o_cv, AR_BYTES
        fsets.append(mkset(False))
        bsets.append(mkset(True))
        tqR = Ring([(halloc(512, F32), Buf()) for _ in range(3)])
        sqR = Ring([(halloc(512, BF16), Buf()) for _ in range(3)])
        sktR = Ring([(halloc(512, BF16), Buf()) for _ in range(3)])
        decR = Ring([(halloc(8, F32), Buf()) for _ in range(4)])
        s_v = Ring([(halloc(512, BF16), Buf()) for _ in range(2)])

        for sti, (u0, T, nts, w) in enumerate(sts):
            if sti == 0:
                load_h(hA, u0, T, nts)
            R = 256 if w else 64
            for fc in range(8):
                zv, zb = zinR.next()
                gv, gb = bginR.next()
                lo = u0 - 64
                hi = u0 + T + 64
                if w:
                    lo, hi = u0, u0 + T
                else:
                    if lo < LC:
                        P.op("pool", lambda e, o=zv[:, 0:64]: e.memset(o, 0.0), writes=[zb])
                        lo = u0
                    if hi > UU:
                        P.op("pool", lambda e, o=zv[:, 64 + T:128 + T]: e.memset(o, 0.0), writes=[zb])
                        hi = u0 + T
                P.op("sp", lambda e, d=zv[:, 64 + lo - u0:64 + hi - u0], s=zA[fc, :, lo:hi]: e.dma_start(out=d, in_=s),
                     reads=[dbuf(("zA", fc))], writes=[zb], dma=True)
                P.op("sp", lambda e, d=gv[:, 0:T], s=bgA[fc, :, u0:u0 + T]: e.dma_start(out=d, in_=s),
                     reads=[dbuf(("bgA", fc))], writes=[gb], dma=True)
                cw0, cw1, cw2 = sp_cw[:, 0, fc:fc + 1], sp_cw[:, 1, fc:fc + 1], sp_cw[:, 2, fc:fc + 1]
                hs = [(nti, off, n, bZCh[nti]) for nti, (off, n) in enumerate(nts)]
                for (nti, off, n, bz) in hs:
                    P.op("act", lambda e, i=zv[:, 64 + off:64 + off + n], c=cw1, o=ZC[:, off:off + n]:
                         e.activation(out=o, in_=i, func=AF.Copy, scale=c), reads=[zb, bC], writes=[bz])
                for step in range(2):
                    for (nti, off, n, bz) in hs:
                        cw = cw0 if step == 0 else cw2
                        if w or fc < 4:
                            zc3 = ZC[:, off:off + n].rearrange("p (r c) -> p r c", c=R)
                            zi3 = zv[:, 64 + off:64 + off + n].rearrange("p (r c) -> p r c", c=R)
                            if step == 0:
                                o_, i_ = zc3[:, :, 1:R], zi3[:, :, 0:R - 1]
                            else:
                                o_, i_ = zc3[:, :, 0:R - 1], zi3[:, :, 1:R]
                        else:
                            o_ = ZC[:, off:off + n]
                            i_ = zv[:, off:off + n] if step == 0 else zv[:, 128 + off:128 + off + n]
                        P.op("dve", lambda e, o=o_, i=i_, c=cw:
                             e.scalar_tensor_tensor(out=o, in0=i, scalar=c, in1=o, op0=ALU.mult, op1=ALU.add),
                             reads=[zb, bC, bz], writes=[bz])
                for (nti, off, n, bz) in hs:
                    P.op("pool", lambda e, o=XN[:, fc, off:off + n], a=gv[:, off:off + n], z=ZC[:, off:off + n]:
                         e.tensor_tensor(out=o, in0=a, in1=z, op=ALU.mult), reads=[gb, bz], writes=[bXN[fc][nti]])
            proj_out(ws, nts, 0, w)
            ffn(ws, nts, 0, 1, w)
            ffn(ws, nts, 1, 0, w)
            if not w:
                store_h(hB, u0, T, nts)
            norm_mod(nts, 1, 1, w)
            if sti + 1 < len(sts):
                nu0, nT, nnts, _ = sts[sti + 1]
                load_h(hA, nu0, nT, nnts)
            for hf in range(2):
                wv, wb = ws.get()
                for blk in range(T // 128):
                    nti = (blk * 128) // 512
                    pv, pb = psR.next()
                    for kc in range(8):
                        P.op("pe", lambda e, o=pv, a=XN[:, kc, blk * 128:(blk + 1) * 128], r=wv[:, kc, :], kc=kc:
                             e.matmul(o, lhsT=a, rhs=r, start=(kc == 0), stop=(kc == 7)),
                             reads=[wb, bXN[kc][nti]], writes=[pb])
                    sv, bsv = s_v.next()
                    P.op("act", lambda e, sv=sv, pv=pv: e.activation(out=sv, in_=pv, func=AF.Copy), reads=[pb], writes=[bsv])
                    ub = u0 + blk * 128
                    for dr in range(2):
                        P.op("sp", lambda e, d=KVT[hf * 4:hf * 4 + 4, dr, ub:ub + 128, 1, :].rearrange("h t d -> t h d"),
                             s=sv.rearrange("p (h d) -> p h d", d=128): e.dma_start(out=d, in_=s),
                             reads=[bsv], writes=[dbuf(("KVT", hf * 4 + q, dr)) for q in range(4)], dma=True)
            hid_all = [bHID[j][nti] for j in range(HC) for nti in range(2)]
            P.op("dve", lambda e: e.engine_nop(), reads=[], writes=hid_all + hg_bufs)
            units = []
            for hd in range(8):
                for nti, (off, n) in enumerate(nts):
                    for dr in range(2):
                        ui = len(units)
                        units.append(dict(hd=hd, nti=nti, off=off, n=n, uo=u0 + off, dr=dr,
                                          ts=(fsets if dr == 0 else bsets)[(ui // 2) % 3]))
            wcur = [None]

            def ustep(k, U):
                hd, nti, off, n, uo, dr, ts = U["hd"], U["nti"], U["off"], U["n"], U["uo"], U["dr"], U["ts"]
                ncn = n // 64
                E, bE = ts["e"]
                LA, bLA = ts["la"]
                LB, bLB = ts["lb"]
                G, bG = ts["g"]
                sk_, bsk = ts["sk"]
                GG, bGG = (G, bG) if dr == 0 else ts["g2"]
                if k == 0:
                    if nti == 0 and dr == 0:
                        wcur[0] = ws.get()
                    wv, wb = wcur[0]

                    def mm8(c0):
                        pv, pb = psR.next()
                        for kc in range(8):
                            P.op("pe", lambda e, o=pv[:, 0:n], a=wv[:, kc, c0:c0 + 128], r=XN[:, kc, off:off + n], kc=kc:
                                 e.matmul(o, lhsT=a, rhs=r, start=(kc == 0), stop=(kc == 7)),
                                 reads=[wb, bXN[kc][nti]], writes=[pb])
                        return pv, pb
                    if dr == 0:
                        pq, bq_ = mm8(0)
                        tq, btq = tqR.next()
                        U["tq"] = (tq, btq)
                        P.op("act", lambda e: e.activation(out=tq[:, 0:n], in_=pq[:, 0:n], func=AF.Silu), reads=[bq_], writes=[btq])
                        if not w:
                            po, bo = mm8(384)
                            ov, bov = s_og.next()
                            P.op("act", lambda e: e.activation(out=ov[:, 0:n], in_=po[:, 0:n], func=AF.Silu), reads=[bo], writes=[bov])
                            P.op("sp", lambda e, d=SOG[hd, :, uo - LC:uo - LC + n]: e.dma_start(out=d, in_=ov[:, 0:n]),
                                 reads=[bov], writes=[dbuf(("SOG", hd))], dma=True)
                    else:
                        U["tq"] = units[U["ui"] - 1]["tq"]
                    px, bx = mm8(128 * (1 + dr))
                    P.op("act", lambda e: e.activation(out=E[:, 0:n], in_=px[:, 0:n], func=AF.Exp, scale=-1.0), reads=[bx], writes=[bE])
                elif k == 1:
                    P.op("act", lambda e: e.activation(out=LA[:, 0:n], in_=E[:, 0:n], func=AF.Ln, scale=lbT[:, dr, hd:hd + 1], bias=1.0),
                         reads=[bE, bC], writes=[bLA])
                    P.op("act", lambda e: e.activation(out=LB[:, 0:n], in_=E[:, 0:n], func=AF.Ln, bias=1.0), reads=[bE], writes=[bLB])
                elif k == 2:
                    P.op("pool", lambda e: e.tensor_tensor(out=LA[:, 0:n], in0=LA[:, 0:n], in1=LB[:, 0:n], op=ALU.subtract),
                         reads=[bLA, bLB], writes=[bLA])
                    P.op("dve", lambda e: e.tensor_tensor_scan(out=G[:, 0:n], data0=mask01[:, 0:n], data1=LA[:, 0:n],
                                                               initial=0.0, op0=ALU.mult, op1=ALU.add),
                         reads=[bLA, bC], writes=[bG])
                elif k == 3:
                    if dr == 1:
                        P.op("pool", lambda e: e.tensor_tensor(out=LA[:, 0:n], in0=LA[:, 0:n], in1=G[:, 0:n], op=ALU.subtract),
                             reads=[bLA, bG], writes=[bLA])
                        last = bass.AP(G.tensor, G.offset + 63, [list(G.ap[0]), [64, ncn], [0, 64]])
                        P.op("dve", lambda e: e.tensor_tensor(out=GG[:, 0:n].rearrange("p (c j) -> p c j", j=64),
                                                               in0=LA[:, 0:n].rearrange("p (c j) -> p c j", j=64), in1=last, op=ALU.add),
                             reads=[bLA, bG], writes=[bGG])
                    P.op("pool", lambda e: e.tensor_tensor(out=LB[:, 0:n], in0=LB[:, 0:n], in1=GG[:, 0:n], op=ALU.add),
                         reads=[bLB, bGG], writes=[bLB])
                elif k == 4:
                    P.op("act", lambda e: e.activation(out=LA[:, 0:n], in_=GG[:, 0:n], func=AF.Exp), reads=[bGG], writes=[bLA])
                    P.op("act", lambda e: e.activation(out=LB[:, 0:n], in_=LB[:, 0:n], func=AF.Exp, scale=-1.0), reads=[bLB], writes=[bLB])
                elif k == 5:
                    tq, btq = U["tq"]
                    sq_, bsq = sqR.next()
                    P.op("dve", lambda e: e.tensor_tensor(out=sq_[:, 0:n], in0=tq[:, 0:n], in1=LA[:, 0:n], op=ALU.mult),
                         reads=[btq, bLA], writes=[bsq])
                    P.op("sp", lambda e, d=QK[hd, dr, :, 0, uo:uo + n]: e.dma_start(out=d, in_=sq_[:, 0:n]),
                         reads=[bsq], writes=[dbuf(("QK", hd, dr))], dma=True)
                    P.op("dve", lambda e: e.scalar_tensor_tensor(out=sk_[:, 0:n], in0=E[:, 0:n], scalar=omlT[:, dr, hd:hd + 1],
                                                                 in1=LB[:, 0:n], op0=ALU.mult, op1=ALU.mult),
                         reads=[bE, bLB, bC], writes=[bsk])
                    P.op("sp", lambda e, d=QK[hd, dr, :, 1, uo:uo + n]: e.dma_start(out=d, in_=sk_[:, 0:n]),
                         reads=[bsk], writes=[dbuf(("QK", hd, dr))], dma=True)
                    dv, bdv = decR.next()
                    pos = 63 if dr == 0 else 0
                    dsrc = bass.AP(LA.tensor, LA.offset + pos, [list(LA.ap[0]), [64, ncn]])
                    P.op("pool", lambda e: e.tensor_copy(out=dv[:, 0:ncn], in_=dsrc), reads=[bLA], writes=[bdv])
                    P.op("sp", lambda e, d=DEC[hd, dr, :, uo // 64:uo // 64 + ncn]: e.dma_start(out=d, in_=dv[:, 0:ncn]),
                         reads=[bdv], writes=[dbuf(("DEC", hd, dr))], dma=True)
                elif k == 6:
                    pv, pb = psR.next()
                    pvb = pv[:, 0:256].bitcast(BF16)
                    for bq in range(n // 128):
                        P.op("pe", lambda e, o=pvb[:, bq * 128:(bq + 1) * 128], i=sk_[:, bq * 128:(bq + 1) * 128]:
                             e.transpose(o, i, ident_bf), reads=[bsk, bC], writes=[pb])
                    skt, bskt = sktR.next()
                    P.op("dve", lambda e: e.tensor_copy(out=skt[:, 0:n], in_=pvb[:, 0:n]), reads=[pb], writes=[bskt])
                    P.op("sp", lambda e, d=KVT[hd, dr, uo:uo + n, 0, :].rearrange("(b t) d -> t b d", t=128):
                         e.dma_start(out=d, in_=skt[:, 0:n].rearrange("p (b d) -> p b d", d=128)),
                         reads=[bskt], writes=[dbuf(("KVT", hd, dr))], dma=True)
            for ui, U in enumerate(units):
                U["ui"] = ui
            NU = len(units)
            for tick in range(NU + 6):
                for k in range(6, -1, -1):
                    ui = tick - k
                    if 0 <= ui < NU:
                        ustep(k, units[ui])
            P.op("dve", lambda e: e.engine_nop(), reads=[], writes=hid_all + hg_bufs)

    def stage3():
        NG = DBG.get('NG', UU // 256)
        per = 16 * KB
        for hg in range(DBG.get('HG', 2)):
            chains = []
            for q in range(4):
                for dr in range(2):
                    ci = q * 2 + dr
                    base = o_h + ci * per
                    off = [base]

                    def al(nelem, dt):
                        v = view(off[0], nelem, dt)
                        off[0] += (nelem * (4 if dt == F32 else 2) + 31) // 32 * 32
                        assert off[0] <= base + per
                        return v
                    ch = dict(hd=hg * 4 + q, dr=dr,
                              qk=[(al(512, BF16), Buf()) for _ in range(2)],
                              kv=[(al(512, BF16), Buf()) for _ in range(2)],
                              attm=[(al(128, BF16), Buf()) for _ in range(2)],
                              Pm=[(al(128, F32), Buf()) for _ in range(2)],
                              Sb=[(al(128, BF16), Buf()) for _ in range(2)],
                              ost=[(al(256, F32), [Buf() for _ in range(4)]) for _ in range(2)],
                              dec=(al(132, F32), Buf()), step=0)
                    chains.append(ch)
            for ch in chains:
                hd, dr = ch["hd"], ch["dr"]
                P.op("sp", lambda e, d=ch["dec"][0], s=DEC[hd, dr]: e.dma_start(out=d, in_=s),
                     reads=[dbuf(("DEC", hd, dr))], writes=[ch["dec"][1]], dma=True)
                P.op("pool", lambda e, o=ch["Pm"][1][0]: e.memset(o, 0.0), writes=[ch["Pm"][1][1]])
                P.op("pool", lambda e, o=ch["Sb"][1][0]: e.memset(o, 0.0), writes=[ch["Sb"][1][1]])

            def gidx(ch, gi):
                if ch["dr"] == 0 or gi == 0:
                    return gi
                return NG - gi

            def issue_loads(ch, gi):
                g = gidx(ch, gi)
                hd, dr = ch["hd"], ch["dr"]
                qv, qb = ch["qk"][gi % 2]
                kv, kb = ch["kv"][gi % 2]
                P.op("sp", lambda e, d=qv.rearrange("p (a t) -> p a t", a=2), s=QK[hd, dr, :, :, g * 256:(g + 1) * 256]:
                     e.dma_start(out=d, in_=s), reads=[dbuf(("QK", hd, dr))], writes=[qb], dma=True)
                P.op("sp", lambda e, d=kv.rearrange("p (b a d) -> p b a d", b=2, a=2),
                     s=KVT[hd, dr, g * 256:(g + 1) * 256].rearrange("(b t) a d -> t b a d", t=128):
                     e.dma_start(out=d, in_=s), reads=[dbuf(("KVT", hd, dr))], writes=[kb], dma=True)

            for ch in chains:
                issue_loads(ch, 0)
            pbk = [Buf() for _ in range(7)]
            psA = Ring([(psb[0][:, 0:128], pbk[0])])
            psO = Ring([(psb[1 + i][:, 0:64], pbk[1 + i]) for i in range(3)])
            psU = Ring([(psb[4 + i][:, 0:128], pbk[4 + i]) for i in range(3)])
            kk = 0
            for gi in range(NG):
                for ch in chains:
                    if gi + 1 < NG:
                        issue_loads(ch, gi + 1)
                    g = gidx(ch, gi)
                    hd, dr = ch["hd"], ch["dr"]
                    qv, qb = ch["qk"][gi % 2]
                    kv, kb = ch["kv"][gi % 2]
                    q2 = qv.rearrange("p (a t) -> p a t", a=2)
                    kv4 = kv.rearrange("p (b a d) -> p b a d", b=2, a=2)
                    latent = g >= 1
                    ch["cur_attm"] = None
                    if latent:
                        av, ab = psA.next()
                        for i in range(4):
                            b_, h_ = i // 2, i % 2
                            P.op("pe", lambda e, o=av[h_ * 64:(h_ + 1) * 64, b_ * 64:(b_ + 1) * 64],
                                 a=q2[:, 1, i * 64:(i + 1) * 64], r=q2[:, 0, i * 64:(i + 1) * 64]:
                                 e.matmul(o, lhsT=a, rhs=r, start=True, stop=True), reads=[qb], writes=[ab])
                        mv, mb = ch["attm"][gi % 2]
                        msk = maskF if dr == 0 else maskB
                        mb3 = bass.AP(msk.tensor, msk.offset, [list(msk.ap[0]), [0, 2], [1, 64]])
                        P.op("dve", lambda e, o=mv.rearrange("p (b j) -> p b j", j=64), i=av.rearrange("p (b j) -> p b j", j=64), m=mb3:
                             e.tensor_tensor(out=o, in0=i, in1=m, op=ALU.mult), reads=[ab, bC], writes=[mb])
                        ch["cur_attm"] = (mv, mb)
                        ch["cur_ost"] = ch["ost"][gi % 2]
                for ii in range(4):
                    for ch in chains:
                        g = gidx(ch, gi)
                        hd, dr = ch["hd"], ch["dr"]
                        i = ii if dr == 0 else 3 - ii
                        b_, h_ = i // 2, i % 2
                        c_glob = g * 4 + i
                        qv, qb = ch["qk"][gi % 2]
                        kv, kb = ch["kv"][gi % 2]
                        q2 = qv.rearrange("p (a t) -> p a t", a=2)
                        kv4 = kv.rearrange("p (b a d) -> p b a d", b=2, a=2)
                        st_ = ch["step"]
                        Pold, bPold = ch["Pm"][(st_ + 1) % 2]
                        Pnew, bPnew = ch["Pm"][st_ % 2]
                        Sold, bSold = ch["Sb"][(st_ + 1) % 2]
                        Snew, bSnew = ch["Sb"][st_ % 2]
                        decv, decb = ch["dec"]
                        ktm = kv4[h_ * 64:(h_ + 1) * 64, b_, 0, :]
                        vtm = kv4[h_ * 64:(h_ + 1) * 64, b_, 1, :]
                        if g >= 1:
                            mv, mb = ch["cur_attm"]
                            ov, ob = psO.next()
                            P.op("pe", lambda e, o=ov, a=vtm, r=mv[h_ * 64:(h_ + 1) * 64, b_ * 64:(b_ + 1) * 64]:
                                 e.matmul(o, lhsT=a, rhs=r, start=True, stop=False), reads=[kb, mb], writes=[ob])
                            P.op("pe", lambda e, o=ov, a=Sold, r=q2[:, 0, i * 64:(i + 1) * 64]:
                                 e.matmul(o, lhsT=a, rhs=r, start=False, stop=True), reads=[bSold, qb], writes=[ob])
                            osv, osb = ch["cur_ost"]
                            kk += 1
                            if kk % 2 == 0:
                                P.op("act", lambda e, o=osv[:, i * 64:(i + 1) * 64], s=ov: e.activation(out=o, in_=s, func=AF.Copy),
                                     reads=[ob], writes=[osb[i]])
                            else:
                                P.op("dve", lambda e, o=osv[:, i * 64:(i + 1) * 64], s=ov: e.tensor_copy(out=o, in_=s),
                                     reads=[ob], writes=[osb[i]])
                        uv, ub = psU.next()
                        P.op("pe", lambda e, o=uv, a=ktm, r=vtm: e.matmul(o, lhsT=a, rhs=r, start=True, stop=True),
                             reads=[kb], writes=[ub])
                        if st_ == 0:
                            P.op("dve", lambda e, o=Pnew, s=uv: e.tensor_copy(out=o, in_=s), reads=[ub], writes=[bPnew])
                        else:
                            pc = ch["prev_c"]
                            P.op("dve", lambda e, o=Pnew, a=Pold, s=uv, d=decv[:, pc:pc + 1]:
                                 e.scalar_tensor_tensor(out=o, in0=a, scalar=d, in1=s, op0=ALU.mult, op1=ALU.add),
                                 reads=[bPold, ub, decb], writes=[bPnew])
                        P.op("act", lambda e, o=Snew, a=Pnew, d=decv[:, c_glob:c_glob + 1]:
                             e.activation(out=o, in_=a, func=AF.Copy, scale=d), reads=[bPnew, decb], writes=[bSnew])
                        ch["prev_c"] = c_glob
                        ch["step"] = st_ + 1
                for ch in chains:
                    g = gidx(ch, gi)
                    if g >= 1:
                        hd, dr = ch["hd"], ch["dr"]
                        osv, osb = ch["cur_ost"]
                        dst = (OFW if dr == 0 else OBW)[hd, :, (g - 1) * 256:g * 256]
                        P.op("sp", lambda e, d=dst, s=osv: e.dma_start(out=d, in_=s), reads=osb,
                             writes=[dbuf(("O", dr, hd))], dma=True)

    def stage4():
        ws = WStream()
        sts = supertiles(False)
        for st in sts:
            for m in range(8):
                ws.add(HWOUT[m], ("HWOUT", m), 8, 128)
            ffn_plan(ws, 1, 1)
        inR = Ring([tuple((view(o_stg + (i * 3 + k) * 4 * KB, 1024, F32), Buf()) for k in range(3)) for i in range(2)])
        OST = view(o_hid, 4 * 1024, F32).rearrange("p (b d) -> p b d", d=1024)
        bOST = [Buf() for _ in range(8)]
        ONs = [view(o_cv + i * 16 * KB, 8 * 1024, BF16).rearrange("p (f t) -> p f t", f=8) for i in range(2)]
        bONs = [[[Buf() for _ in range(2)] for _ in range(8)] for _ in range(2)]

        rs4 = Ring([(view(o_rs + i * 2 * KB, 512, F32), Buf()) for i in range(2)] +
                   [(view(o_cv + 32 * KB + i * 2 * KB, 512, F32), Buf()) for i in range(2)])

        def pro(ki):
            (u0, T, nts, w) = sts[ki]
            t0 = u0 - LC
            ON, bON = ONs[ki % 2], bONs[ki % 2]
            for fp in range(4):
                items = []
                for fc in (2 * fp, 2 * fp + 1):
                    (fv, fb), (bv, bb), (gv, gb) = inR.next()
                    P.op("sp", lambda e, d=fv, s=OFW[fc, :, t0:t0 + T]: e.dma_start(out=d, in_=s),
                         reads=[dbuf(("O", 0, fc))], writes=[fb], dma=True)
                    P.op("sp", lambda e, d=bv, s=OBW[fc, :, t0:t0 + T]: e.dma_start(out=d, in_=s),
                         reads=[dbuf(("O", 1, fc))], writes=[bb], dma=True)
                    P.op("sp", lambda e, d=gv, s=SOG[fc, :, t0:t0 + T]: e.dma_start(out=d, in_=s),
                         reads=[dbuf(("SOG", fc))], writes=[gb], dma=True)
                    P.op("pool", lambda e, fv=fv, bv=bv: e.tensor_tensor(out=fv, in0=fv, in1=bv, op=ALU.add),
                         reads=[fb, bb], writes=[fb])
                    for nti, (off, n) in enumerate(nts):
                        items.append(dict(fc=fc, nti=nti, off=off, n=n, fv=fv, fb=fb, gv=gv, gb=gb))
                yield
                yield
                for it in items:
                    k = sq_i[0] % 8
                    sq_i[0] += 1
                    it["k"] = k
                    P.op("act", lambda e, o=SQ[:, k, 0:it["n"]], s=it["fv"][:, it["off"]:it["off"] + it["n"]]:
                         e.activation(out=o, in_=s, func=AF.Square), reads=[it["fb"]], writes=[bSQ[k]])
                yield
                yield
                for it in items:
                    it["ps"] = psR.next()
                    P.op("pe", lambda e, o=it["ps"][0][:, 0:it["n"]], r=SQ[:, it["k"], 0:it["n"]]:
                         e.matmul(o, lhsT=ones_bf, rhs=r, start=True, stop=True), reads=[bSQ[it["k"]], bC], writes=[it["ps"][1]])
                for it in items:
                    it["rs"] = rs4.next()
                    P.op("act", lambda e, o=it["rs"][0][:, 0:it["n"]], i=it["ps"][0][:, 0:it["n"]]:
                         e.activation(out=o, in_=i, func=AF.Ln, scale=1.0 / 128, bias=EPS), reads=[it["ps"][1]], writes=[it["rs"][1]])
                for it in items:
                    P.op("act", lambda e, o=it["rs"][0][:, 0:it["n"]]: e.activation(out=o, in_=o, func=AF.Exp, scale=-0.5),
                         reads=[it["rs"][1]], writes=[it["rs"][1]])
                yield
                for it in items:
                    fc, nti, off, n = it["fc"], it["nti"], it["off"], it["n"]
                    tv, tb = tmpR.next()
                    P.op("dve", lambda e, o=tv[:, 0:n], a=it["fv"][:, off:off + n], r=it["rs"][0][:, 0:n]:
                         e.tensor_tensor(out=o, in0=a, in1=r, op=ALU.mult), reads=[it["fb"], it["rs"][1]], writes=[tb])
                    P.op("dve", lambda e, o=ON[:, fc, off:off + n], a=tv[:, 0:n], g=it["gv"][:, off:off + n], fc=fc:
                         e.scalar_tensor_tensor(out=o, in0=a, scalar=sp_gn[:, fc:fc + 1], in1=g, op0=ALU.mult, op1=ALU.mult),
                         reads=[tb, it["gb"], bC], writes=[bON[fc][nti]])
                yield

        for _ in pro(0):
            pass
        for ki, (u0, T, nts, w) in enumerate(sts):
            t0 = u0 - LC
            load_h(hB, u0, T, nts)
            BG[0] = pro(ki + 1) if ki + 1 < len(sts) else None
            proj_out(ws, nts, 1, 0, ONs[ki % 2], bONs[ki % 2])
            ffn(ws, nts, 1, 1, 0)
            tick(100)
            for nti, (off, n) in enumerate(nts):
                RSTD, bRSTD = sumsq_rstd(nts, nti, [Hh[:, fc, off:off + n] for fc in range(8)], [bH[fc][nti] for fc in range(8)], D, pre=True)
                for fc in range(8):
                    P.op("dve", lambda e, o=Hh[:, fc, off:off + n], fc=fc:
                         e.scalar_tensor_tensor(out=o, in0=o, scalar=sp_fg[:, fc:fc + 1], in1=RSTD[:, 0:n], op0=ALU.mult, op1=ALU.mult),
                         reads=[bH[fc][nti], bRSTD, bC], writes=[bH[fc][nti]])
                hid_all = [bHID[j][q] for j in range(HC) for q in range(2)]
                k = 0
                for blk in range(4):
                    for half in range(2):
                        pv, pb = psR.next()
                        for q in range(4):
                            fc = half * 4 + q
                            P.op("pe", lambda e, o=pv[:, q * 128:(q + 1) * 128], i=Hh[:, fc, off + blk * 128:off + (blk + 1) * 128]:
                                 e.transpose(o, i, ident_f), reads=[bH[fc][nti], bC], writes=[pb])
                        k += 1
                        wr = [bOST[blk * 2 + half]] + (hid_all if (blk == 0 and half == 0) else [])
                        if k % 2 == 0:
                            P.op("act", lambda e, o=OST[:, blk, half * 512:(half + 1) * 512], i=pv: e.activation(out=o, in_=i, func=AF.Copy),
                                 reads=[pb], writes=wr)
                        else:
                            P.op("dve", lambda e, o=OST[:, blk, half * 512:(half + 1) * 512], i=pv: e.tensor_copy(out=o, in_=i),
                                 reads=[pb], writes=wr)
                a0 = t0 + off
                P.op("sp", lambda e, d=out_d[a0:a0 + 512, :].rearrange("(b p) d -> p b d", p=128): e.dma_start(out=d, in_=OST),
                     reads=bOST, writes=[dbuf("out")] + hid_all, dma_key="out")

    if not DBG.get("NOCONV"):
        stage0_pre()
        conv_all()
        stage0()
    P.barrier(skip_pool=True, extra=[last_ci])
    if 1 in stages:
        stage1()
        P.barrier()
    if 2 in stages:
        stage2()
        P.barrier()
    if 3 in stages:
        stage3()
        P.barrier()
    if 4 in stages:
        stage4()
    fk = [k for k in P.dma_keys if k.startswith("auto_sp") or k == "out"]
    P.emit(final_wait_keys=fk)
    return nc


def _fm(v):
    return np.ascontiguousarray(np.asarray(v, np.float32).reshape(8, 128).T)


def make_inputs(b, x, c, ctx, c_ctx, ada_w, ada_b, norm_g, ffn_w_gu, ffn_w_down, conv_w_in, conv_w,
                conv_w_out, hg_w_in, hg_lb_logits, hg_gnorm_g, hg_w_out, final_norm_g):
    sp = np.zeros((128, NSP), np.float32)
    cc = np.stack([_fm(c[b]), _fm(c_ctx)], axis=-1)
    sp[:, 0:16] = cc.reshape(128, 16)
    ng = np.stack([np.stack([_fm(norm_g[l, s]) for s in range(3)], 1) for l in range(2)], 1)
    sp[:, 16:64] = ng.reshape(128, 48)
    cw = np.stack([_fm(conv_w[0, j]) for j in range(3)], 1)
    sp[:, 64:88] = cw.reshape(128, 24)
    lb = np.stack([np.stack([_fm(hg_lb_logits[l, r]) for r in range(2)], 1) for l in range(2)], 1)
    sp[:, 88:120] = lb.reshape(128, 32)
    sp[:, 120:128] = _fm(hg_gnorm_g[0])
    sp[:, 128:136] = _fm(final_norm_g)
    ab = np.stack([np.ascontiguousarray(np.asarray(ada_b[l], np.float32).reshape(72, 128).T) for l in range(2)], 1)
    sp[:, 136:280] = ab.reshape(128, 144)
    return {
        "x": np.ascontiguousarray(x[b]), "ctx": np.ascontiguousarray(ctx[b]), "smallp": sp,
        "ada_w": ada_w, "ffn_w_gu": ffn_w_gu, "ffn_w_down": ffn_w_down,
        "conv_w_in": np.ascontiguousarray(conv_w_in[0]), "conv_w_out": np.ascontiguousarray(conv_w_out[0]),
        "hg_w_in": np.ascontiguousarray(hg_w_in[0]), "hg_w_out": np.ascontiguousarray(hg_w_out[0]),
    }


def kernel(**inputs):
    inputs = {k: np.asarray(v) for k, v in inputs.items()}
    nc = build_program()
    in_maps = [make_inputs(b, **inputs) for b in range(8)]
    res = run_bass_kernel_spmd(nc, in_maps, core_ids=list(range(8)))
    return np.stack([np.asarray(r["out"], np.float32) for r in res.results], axis=0)
```
